# Optimizing a Trainium2 kernel written in Bass

```python
import math
import jax, jax.numpy as jnp
from jax import lax
import numpy as np

D_MODEL = 1024
BATCH = 8
SEQ = 2048
DEPTH = 2

CTX_LEN = 256
GRID_W = 64
N_BRANCH = 3
BRANCH_W = D_MODEL // 2
RET_HEADS = 4
RET_DK = BRANCH_W // RET_HEADS
DIFF_HEADS = 4
DIFF_DV = BRANCH_W // DIFF_HEADS
DIFF_DQK = DIFF_DV // 2
MLP_GROUPS = 4
MLP_GW = BRANCH_W // MLP_GROUPS
CHUNK = 128
RET_CHUNK = 128
Q_BLOCK = 128
ROPE_BASE = 10000.0
EPS = 1e-6
KV_SIZES = (BRANCH_W, BRANCH_W, BRANCH_W, BRANCH_W)
REST_SIZES = (BRANCH_W,) * 7 + (N_BRANCH * D_MODEL,)
IN_SIZES = KV_SIZES + REST_SIZES
KV_COLS = 4 * BRANCH_W
IN_COLS = KV_COLS + 7 * BRANCH_W + N_BRANCH * D_MODEL

kernel_name = "hybrid_retention_gmlp_diffattn_prefix_dit"


def rms_norm(x, g=None):
    xf = x.astype(jnp.float32)
    y = xf * lax.rsqrt(jnp.mean(xf * xf, axis=-1, keepdims=True) + EPS)
    if g is not None:
        y = y * g.astype(jnp.float32)
    return y.astype(x.dtype)


def layer_norm(x):
    xf = x.astype(jnp.float32)
    mu = jnp.mean(xf, axis=-1, keepdims=True)
    var = jnp.mean(jnp.square(xf - mu), axis=-1, keepdims=True)
    return ((xf - mu) * lax.rsqrt(var + EPS)).astype(x.dtype)


def modulate(x, g, shift, scale):
    return rms_norm(x, g) * (1.0 + scale) + shift


def split_cols(z, sizes):
    idx = np.cumsum(sizes)[:-1].tolist()
    return jnp.split(z, idx, axis=-1)


def to_heads(t, n_heads):
    b, n, w = t.shape
    return t.reshape(b, n, n_heads, w // n_heads).transpose(0, 2, 1, 3)


def to_diff_heads(t):
    b, n, _ = t.shape
    return t.reshape(b, n, DIFF_HEADS, 2, DIFF_DQK).transpose(0, 2, 3, 1, 4)


def from_heads(t):
    b, h, n, d = t.shape
    return t.transpose(0, 2, 1, 3).reshape(b, n, h * d)


def rope_angles(row_pos, col_pos, head_dim):
    n_freq = head_dim // 4
    inv = ROPE_BASE ** (-jnp.arange(n_freq, dtype=jnp.float32) / n_freq)
    ang = jnp.concatenate([row_pos[:, None] * inv, col_pos[:, None] * inv], axis=-1)
    return jnp.cos(ang), jnp.sin(ang)


def apply_rope(x, cos, sin):
    x1, x2 = jnp.split(x, 2, axis=-1)
    cos = cos.astype(x.dtype)
    sin = sin.astype(x.dtype)
    return jnp.concatenate([x1 * cos - x2 * sin, x1 * sin + x2 * cos], axis=-1)


def retention_chunkwise(q, k, v, log_gamma, state0):
    in_dtype = q.dtype
    q, k, v = q.astype(jnp.float32), k.astype(jnp.float32), v.astype(jnp.float32)
    b, h, n, dk = q.shape
    dv = v.shape[-1]
    nc = n // RET_CHUNK
    qc = q.reshape(b, h, nc, RET_CHUNK, dk)
    kc = k.reshape(b, h, nc, RET_CHUNK, dk)
    vc = v.reshape(b, h, nc, RET_CHUNK, dv)
    pos = jnp.arange(RET_CHUNK, dtype=jnp.float32)
    lg = log_gamma.astype(jnp.float32)[:, None]
    dist = pos[:, None] - pos[None, :]
    d_in = jnp.where(dist >= 0, jnp.exp(lg[:, :, None] * jnp.maximum(dist, 0.0)), 0.0)
    q_dec = jnp.exp(lg * (pos + 1.0))
    k_dec = jnp.exp(lg * (RET_CHUNK - 1.0 - pos))
    chunk_dec = jnp.exp(lg[:, 0] * RET_CHUNK)
    s = jnp.einsum('bhnid,bhnjd->bhnij', qc, kc) * d_in[None, :, None]
    inner = jnp.einsum('bhnij,bhnje->bhnie', s, vc)
    kv = jnp.einsum('bhnjd,hj,bhnje->bhnde', kc, k_dec, vc)

    def step(state, kv_n):
        return chunk_dec[None, :, None, None] * state + kv_n, state

    final, prev = lax.scan(step, state0.astype(jnp.float32), jnp.moveaxis(kv, 2, 0))
    prev = jnp.moveaxis(prev, 0, 2)
    cross = jnp.einsum('bhnid,hi,bhnde->bhnie', qc, q_dec, prev)
    out = (inner + cross).reshape(b, h, n, dv)
    return out.astype(in_dtype), final


def retention_final_state(k, v, log_gamma):
    n = k.shape[2]
    w = jnp.exp(log_gamma[:, None] * (n - 1.0 - jnp.arange(n, dtype=jnp.float32)))
    return jnp.einsum('bhjd,hj,bhje->bhde', k.astype(jnp.float32), w, v.astype(jnp.float32))


def bidir_retention(q, k, v, log_gamma, state_f, state_b):
    out_f, fin_f = retention_chunkwise(q, k, v, log_gamma[0], state_f)
    out_b, fin_b = retention_chunkwise(jnp.flip(q, 2), jnp.flip(k, 2), jnp.flip(v, 2), log_gamma[1], state_b)
    return out_f + jnp.flip(out_b, 2), fin_f, fin_b


def diff_attend(q, k, v, lam):
    s = jnp.einsum('bhmqd,bhmkd->bhmqk', q.astype(jnp.float32), k.astype(jnp.float32)) * (DIFF_DQK ** -0.5)
    p = jax.nn.softmax(s, axis=-1)
    a = p[:, :, 0] - lam * p[:, :, 1]
    return jnp.einsum('bhqk,bhkv->bhqv', a, v.astype(jnp.float32)).astype(v.dtype)


def diff_attend_blocked(q, k, v, lam):
    b, h, _, n, d = q.shape
    nb = n // Q_BLOCK
    qb = jnp.moveaxis(q.reshape(b, h, 2, nb, Q_BLOCK, d), 3, 0)
    o = lax.map(lambda blk: diff_attend(blk, k, v, lam), qb)
    return jnp.moveaxis(o, 0, 2).reshape(b, h, n, -1)


def chunk_mlp(u, v, gate, w_s, b_s):
    b, n, _ = u.shape
    nch = n // CHUNK
    vn = layer_norm(v).reshape(b, nch, CHUNK, MLP_GROUPS, MLP_GW)
    sp = jnp.einsum('gij,bnjgc->bnigc', w_s, vn) + b_s.T[:, :, None]
    return u * sp.reshape(b, n, BRANCH_W) * jax.nn.silu(gate)


def merge_branches(o_ret, o_mlp, o_diff, merge_logits, w_bo, w_o):
    g_ret, g_mlp, g_diff = jnp.split(jax.nn.sigmoid(merge_logits), N_BRANCH, axis=-1)
    y = g_ret * (o_ret @ w_bo[0]) + g_mlp * (o_mlp @ w_bo[1]) + g_diff * (o_diff @ w_bo[2])
    return y @ w_o


def setup_inputs(seed: int = 0) -> dict:
    key = jax.random.key(seed)
    ks = jax.random.split(key, 16)
    f32 = jnp.float32
    nrm = jax.random.normal
    x = nrm(ks[0], (BATCH, SEQ, D_MODEL), f32)
    c = nrm(ks[1], (BATCH, D_MODEL), f32)
    ctx = nrm(ks[2], (BATCH, CTX_LEN, D_MODEL), f32)
    c_ctx = nrm(ks[3], (D_MODEL,), f32)
    w_mod = nrm(ks[4], (DEPTH, D_MODEL, 3 * D_MODEL), f32) * (0.5 * D_MODEL ** -0.5)
    b_mod = 0.02 * nrm(ks[5], (DEPTH, 3 * D_MODEL), f32)
    g_pre = 1.0 + 0.05 * nrm(ks[6], (DEPTH, D_MODEL), f32)
    g_post = 1.0 + 0.05 * nrm(ks[7], (DEPTH, D_MODEL), f32)
    w_in = nrm(ks[8], (DEPTH, D_MODEL, IN_COLS), f32) * (D_MODEL ** -0.5)
    base = jnp.log(2.0 ** (5.0 + jnp.arange(RET_HEADS, dtype=f32)) - 1.0)
    ret_decay_logit = base[None, None, :] + 0.1 * nrm(ks[9], (DEPTH, 2, RET_HEADS), f32)
    mlp_w_s = nrm(ks[10], (DEPTH, MLP_GROUPS, CHUNK, CHUNK), f32) * (CHUNK ** -0.5)
    mlp_b_s = 1.0 + 0.1 * nrm(ks[11], (DEPTH, MLP_GROUPS, CHUNK), f32)
    diff_lambda_q = 0.1 * nrm(ks[12], (DEPTH, 2, DIFF_DQK), f32)
    diff_lambda_k = 0.1 * nrm(ks[13], (DEPTH, 2, DIFF_DQK), f32)
    w_branch_out = nrm(ks[14], (DEPTH, N_BRANCH, BRANCH_W, D_MODEL), f32) * (BRANCH_W ** -0.5)
    w_out = nrm(ks[15], (DEPTH, D_MODEL, D_MODEL), f32) * (D_MODEL ** -0.5)
    return {"x": x, "c": c, "ctx": ctx, "c_ctx": c_ctx, "w_mod": w_mod, "b_mod": b_mod,
            "g_pre": g_pre, "g_post": g_post, "w_in": w_in, "ret_decay_logit": ret_decay_logit,
            "mlp_w_s": mlp_w_s, "mlp_b_s": mlp_b_s, "diff_lambda_q": diff_lambda_q,
            "diff_lambda_k": diff_lambda_k, "w_branch_out": w_branch_out, "w_out": w_out}


def reference(x, c, ctx, c_ctx, w_mod, b_mod, g_pre, g_post, w_in, ret_decay_logit,
              mlp_w_s, mlp_b_s, diff_lambda_q, diff_lambda_k, w_branch_out, w_out):
    b = x.shape[0]
    n_lat = x.shape[1]
    rows = n_lat // GRID_W
    row_pos = jnp.repeat(jnp.arange(rows, dtype=jnp.float32), GRID_W)
    col_pos = jnp.tile(jnp.arange(GRID_W, dtype=jnp.float32), rows)
    cos_r, sin_r = rope_angles(row_pos, col_pos, RET_DK)
    cos_d, sin_d = rope_angles(row_pos, col_pos, DIFF_DQK)
    silu_c = jax.nn.silu(c)
    silu_cc = jax.nn.silu(c_ctx)
    ret_scale = RET_DK ** -0.5

    for l in range(DEPTH):
        last = l == DEPTH - 1
        shift, scale, gate = jnp.split((silu_c @ w_mod[l] + b_mod[l])[:, None, :], 3, axis=-1)
        shift_c, scale_c, gate_c = jnp.split(silu_cc @ w_mod[l] + b_mod[l], 3, axis=-1)
        h = modulate(x, g_pre[l], shift, scale)
        hc = modulate(ctx, g_pre[l], shift_c, scale_c)
        log_gamma = -jax.nn.softplus(-ret_decay_logit[l].astype(jnp.float32))
        lam_init = 0.8 - 0.6 * math.exp(-0.3 * l)
        lam = (jnp.exp(jnp.sum(diff_lambda_q[l, 0] * diff_lambda_k[l, 0]))
               - jnp.exp(jnp.sum(diff_lambda_q[l, 1] * diff_lambda_k[l, 1])) + lam_init).astype(jnp.float32)

        if last:
            c_rk, c_rv, c_dk, c_dv = split_cols(hc @ w_in[l][:, :KV_COLS], KV_SIZES)
        else:
            zc = split_cols(hc @ w_in[l], IN_SIZES)
            c_rk, c_rv, c_dk, c_dv = zc[:4]
        crk = to_heads(c_rk, RET_HEADS) * ret_scale
        crv = to_heads(c_rv, RET_HEADS)
        cdk = to_diff_heads(c_dk)
        cdv = to_heads(c_dv, DIFF_HEADS)
        if last:
            s_f = retention_final_state(crk, crv, log_gamma[0])
            s_b = retention_final_state(jnp.flip(crk, 2), jnp.flip(crv, 2), log_gamma[1])
        else:
            c_rq, c_rg, c_dq, c_dg, c_mu, c_mv, c_mg, c_mlog = zc[4:]
            zero = jnp.zeros((b, RET_HEADS, RET_DK, RET_DK), jnp.float32)
            c_ret, s_f, s_b = bidir_retention(to_heads(c_rq, RET_HEADS), crk, crv, log_gamma, zero, zero)
            co_ret = from_heads(rms_norm(c_ret)) * jax.nn.silu(c_rg)
            c_diff = diff_attend(to_diff_heads(c_dq), cdk, cdv, lam)
            co_diff = from_heads(rms_norm(c_diff) * (1.0 - lam_init)) * jax.nn.silu(c_dg)
            co_mlp = chunk_mlp(c_mu, c_mv, c_mg, mlp_w_s[l], mlp_b_s[l])
            c_out = merge_branches(co_ret, co_mlp, co_diff, c_mlog, w_branch_out[l], w_out[l])
            ctx_next = ctx + gate_c * rms_norm(c_out, g_post[l])

        rk_, rv_, dk_, dv_, rq_, rg_, dq_, dg_, mu_, mv_, mg_, mlog_ = split_cols(h @ w_in[l], IN_SIZES)
        rq = apply_rope(to_heads(rq_, RET_HEADS), cos_r, sin_r)
        rk = apply_rope(to_heads(rk_, RET_HEADS), cos_r, sin_r) * ret_scale
        rv = to_heads(rv_, RET_HEADS)
        ret_o, _, _ = bidir_retention(rq, rk, rv, log_gamma, s_f, s_b)
        o_ret = from_heads(rms_norm(ret_o)) * jax.nn.silu(rg_)

        dq = apply_rope(to_diff_heads(dq_), cos_d, sin_d)
        dk = apply_rope(to_diff_heads(dk_), cos_d, sin_d)
        dv = to_heads(dv_, DIFF_HEADS)
        k_all = jnp.concatenate([cdk, dk], axis=3)
        v_all = jnp.concatenate([cdv, dv], axis=2)
        diff_o = diff_attend_blocked(dq, k_all, v_all, lam)
        o_diff = from_heads(rms_norm(diff_o) * (1.0 - lam_init)) * jax.nn.silu(dg_)

        o_mlp = chunk_mlp(mu_, mv_, mg_, mlp_w_s[l], mlp_b_s[l])
        out = merge_branches(o_ret, o_mlp, o_diff, mlog_, w_branch_out[l], w_out[l])
        x = x + gate * rms_norm(out, g_post[l])
        if not last:
            ctx = ctx_next
    return x
```

```python
import math
from contextlib import ExitStack

import numpy as np
import concourse.bass as bass
import concourse.mybir as mybir
from concourse.bass_utils import run_bass_kernel_spmd

F32 = mybir.dt.float32
BF16 = mybir.dt.bfloat16
AF = mybir.ActivationFunctionType
ALU = mybir.AluOpType
AX = mybir.AxisListType

DEPTH = 2
DM = 1024
SEQ = 2048
CTX = 256
NTC = CTX // 128
NTL = SEQ // 128
NTT = NTC + NTL
ST = 4
IN_COLS = 8704
EPS = 1e-6
RET_SCALE = 128.0 ** -0.5
USE_WCACHE = True
C_RK, C_RV, C_DK, C_DV, C_RQ, C_RG, C_DQ, C_DG, C_MU, C_MV, C_MG, C_ML = [512 * i for i in range(12)]


class T:
    def __init__(self, h, name):
        self.h = h
        self.name = name
        self.w = {}
        self.r = {}
        self.dsem = None
        self.excl = False

    def __getitem__(self, k):
        return self.h[k]


class KB:
    def __init__(self, nc, stack):
        self.nc = nc
        self.stack = stack
        self.eng = {"pe": nc.tensor, "dve": nc.vector, "act": nc.scalar, "pool": nc.gpsimd, "sp": nc.sync}
        self.sem = {}
        self.cnt = {}
        for e in self.eng:
            self.sem[e] = stack.enter_context(nc.semaphore("s_" + e))
            self.cnt[e] = 0
        self.seen = {e: {} for e in self.eng}
        self.out_events = []
        self.nins = {e: 0 for e in self.eng}

    def sb(self, name, shape, dt):
        self.sb_bytes = getattr(self, "sb_bytes", 0) + int(np.prod(shape[1:])) * (2 if dt == BF16 else 4)
        return T(self.stack.enter_context(self.nc.sbuf_tensor(name, list(shape), dt)), name)

    def ps(self, name, shape, dt=F32):
        t = T(self.stack.enter_context(self.nc.psum_tensor(name, list(shape), dt)), name)
        t.excl = True
        return t

    def _wait(self, e, reads, writes):
        need = {}

        def add(k, v, raw):
            if k == e and e == "pe":
                return
            need[k] = max(need.get(k, 0), v)

        for t in reads:
            for k, v in t.w.items():
                add(k, v, True)
            if t.excl:
                for k, v in t.r.items():
                    if k != e:
                        add(k, v, False)
        for t in writes:
            for k, v in t.w.items():
                add(k, v, False)
            for k, v in t.r.items():
                add(k, v, False)
        for k, v in need.items():
            if k not in self.eng:
                v = self.cnt[k]
            if self.seen[e].get(k, 0) < v:
                assert v <= self.cnt[k], f"wait on unsignaled event {k} {v} {self.cnt[k]}"
                self.eng[e].wait_ge(self.sem[k], v)
                self.seen[e][k] = v

    def _post(self, ev, reads, writes):
        for t in reads:
            t.r[ev[0]] = max(t.r.get(ev[0], 0), ev[1])
        for t in writes:
            t.w[ev[0]] = max(t.w.get(ev[0], 0), ev[1])
            t.r = {}

    def op(self, e, fn, reads=(), writes=(), signal=True):
        self._wait(e, reads, writes)
        ins = fn()
        self.nins[e] += 1
        if signal:
            self.cnt[e] += 1
            ins.then_inc(self.sem[e], 1)
            ev = (e, self.cnt[e])
        else:
            ev = (e, self.cnt[e] + 1)
        self._post(ev, reads, writes)
        return ins

    def dma(self, q, out_ap, in_ap, sb_t, reads=(), writes=(), is_out=False):
        self._wait(q, reads, writes)
        if sb_t.dsem is None:
            sb_t.dsem = {}
        kind = "sw" if q == "pool" else "hw"
        if kind not in sb_t.dsem:
            nm = "d_" + sb_t.name + "_" + kind
            sb_t.dsem[kind] = nm
            self.sem[nm] = self.stack.enter_context(self.nc.semaphore(nm))
            self.cnt[nm] = 0
        k = sb_t.dsem[kind]
        ins = self.eng[q].dma_start(out=out_ap, in_=in_ap)
        self.cnt[k] += 16
        ins.then_inc(self.sem[k], 16)
        ev = (k, self.cnt[k])
        self._post(ev, reads, writes)
        if is_out:
            self.out_events.append(ev)
        return ins

    def finish(self):
        fin = {}
        for k, v in self.out_events:
            fin[k] = self.cnt[k]
        for k, v in fin.items():
            self.eng["sp"].wait_ge(self.sem[k], v)


def build_program(depth=DEPTH, debug=False):
    nc = bass.Bass("TRN2", target_bir_lowering=False)

    def din(name, shape):
        return nc.dram_tensor(name, list(shape), F32, kind="ExternalInput").ap()

    x_in = din("x", [SEQ, DM])
    ctx_in = din("ctx", [CTX, DM])
    cct = din("cct", [128, 8, 2])
    w_mod = din("w_mod", [DEPTH, DM, 3 * DM])
    b_mod = din("b_mod", [DEPTH, 3 * DM])
    g_pre = din("g_pre", [DEPTH, DM])
    g_post = din("g_post", [DEPTH, DM])
    w_in = din("w_in", [DEPTH, DM, IN_COLS])
    rdl = din("rdl", [DEPTH, 8])
    mws = din("mws", [DEPTH, 4, 128, 128])
    mbs = din("mbs", [DEPTH, 128, 4])
    dlq = din("dlq", [DEPTH, 128])
    dlk = din("dlk", [DEPTH, 128])
    w_bo = din("w_bo", [DEPTH, 3, 512, DM])
    w_out = din("w_out", [DEPTH, DM, DM])
    cf = din("cf", [128, 7 * 128 + 2])
    ropet = din("ropet", [128, 4, SEQ])
    pm_in = din("pm", [128, 256])
    out_d = nc.dram_tensor("out", [SEQ, DM], F32, kind="ExternalOutput").ap()
    xs_d = nc.dram_tensor("xs", [SEQ, DM], F32, kind="Internal").ap()
    mr_d = nc.dram_tensor("mrows", [DEPTH, 2, 3 * DM], F32, kind="Internal").ap()
    wbf_in = nc.dram_tensor("wbf_in", [DEPTH, DM, IN_COLS], BF16, kind="Internal").ap()
    wbf_bo = nc.dram_tensor("wbf_bo", [DEPTH, 3, 512, DM], BF16, kind="Internal").ap()
    wbf_out = nc.dram_tensor("wbf_out", [DEPTH, DM, DM], BF16, kind="Internal").ap()

    with ExitStack() as st:
        k = KB(nc, st)
        XSD = [[T(None, f"xsd{l}_{j}") for j in range(NTL)] for l in range(2)]
        MRD = [[T(None, f"mrd{l}_{j}") for j in range(12)] for l in range(DEPTH)]

        Cx = [k.sb(f"cx{g}", [128, DM], F32) for g in range(NTC)]
        Xs = [k.sb(f"xsb{i}", [128, DM], F32) for i in range(ST)]
        Y = [k.sb(f"y{i}", [128, DM], F32) for i in range(ST)]
        F1 = k.sb("f1", [128, DM], F32)
        F2 = k.sb("f2", [128, DM], F32)
        G1 = k.sb("g1", [128, DM], BF16)
        SH = k.sb("sh", [128, DM], BF16)
        GT = k.sb("gt", [128, DM], BF16)
        DKT = [k.sb(f"dkt{g}", [128, 4, 128], BF16) for g in range(NTT)]
        VA = [k.sb(f"va{g}", [128, 4, 130], BF16) for g in range(NTT)]
        TBS = [k.sb(f"tbs{g}", [128, 512], BF16) for g in range(NTT)]
        HT = [k.sb("hTa", [128, 8, ST * 128], BF16), k.sb("hTb", [128, 8, ST * 128], BF16)]
        hcur = {"i": 0}
        pend = {"gen": None, "pos": -1, "dst": None}
        HB = k.sb("hb", [128, DM], BF16)
        W = [k.sb(f"w{i}", [128, 8, 256], BF16) for i in range(4)]
        ZB = [k.sb(f"zb{i}", [128, ST * 128], BF16) for i in range(2)]
        PMR = k.sb("pmr", [128, 128], BF16)
        PMD = k.sb("pmd", [128, 128], BF16)
        RT = k.sb("ropes", [128, 4, ST * 128], BF16)
        QK = k.sb("qk", [128, 8, ST * 128], BF16)
        KTOK = [k.sb(f"ktok{i}", [128, 512], BF16) for i in range(ST)]
        RV = [k.sb(f"rv{i}", [128, 512], BF16) for i in range(ST)]
        SG_ = [k.sb(f"sg{i}", [128, 512], BF16) for i in range(ST)]
        OT = [k.sb(f"oT{b}", [128, 4, ST * 128], BF16) for b in range(3)]
        SMs = [k.sb(f"sm{i}", [128, 512], BF16) for i in range(2)]
        QFs = [k.sb(f"qf{i}", [128, 512], BF16) for i in range(2)]
        QBs = [k.sb(f"qb{i}", [128, 512], BF16) for i in range(2)]
        OBQ = [[k.sb(f"ob{b_}{h_}", [128, 128], BF16) for h_ in range(4)] for b_ in range(2)]
        E_ = [k.sb(f"e{i}", [128, 512], BF16) for i in range(3)]
        SGM = [k.sb(f"sgm{i}", [128, 256], BF16) for i in range(2)]
        SF = k.sb("sf", [128, 512], F32)
        SFB = k.sb("sfb", [128, 512], BF16)
        TC = k.sb("tcur", [128, 512], F32)
        MCT = k.sb("mct", [128, 512], BF16)
        QDF = k.sb("qdf", [128, 512], BF16)
        QDB = k.sb("qdb", [128, 512], BF16)
        IDB = k.sb("idb", [128, 128], BF16)
        WST = k.sb("wst", [128, 4, 128], BF16)
        SV = k.sb("sv", [128, 128], F32)
        R0 = k.sb("r0", [128, 4], F32)
        R1 = k.sb("r1", [128, 4], F32)
        SVMa = k.sb("svma", [128, 8], F32)
        SVMb = k.sb("svmb", [128, 8], F32)
        SVM = k.sb("svm", [128, 32], F32)
        SVH = k.sb("svh", [128, 32], F32)
        SVN = k.sb("svn", [128, 48], F32)
        HB2 = k.sb("hb2", [128, DM], BF16)
        SCC = k.sb("scc", [128, 8, 2], BF16)
        CCF = k.sb("ccf", [128, 8, 2], F32)
        c_eps, c_nlam, c_ss, c_rs, c_lg, c_kdec, c_cdec, c_bs, c_tmp = 0, 1, 4, 12, 20, 28, 36, 44, 48
        c_ss4, c_e2, c_mean, c_sum2 = 56, 76, 80, 84

        PG = [k.ps(f"pg{i}", [128, 512]) for i in range(3)]
        OA = [k.ps(f"oa{i}", [128, 512]) for i in range(2)]
        KVP = k.ps("kvp", [128, 512])
        TP = [k.ps(f"tp{i}", [128, 1024], BF16) for i in range(2)]
        rr = {"pg": 0, "tp": 0, "w": 0, "e": 0, "sgm": 0, "mrb": 0, "zb": 0, "pgx": 0}
        PGX = PG + OA

        def nxt(key, lst):
            t = lst[rr[key] % len(lst)]
            rr[key] += 1
            return t

        dbg_names = []

        def dbg(name, ap, t, shape):
            if not debug:
                return
            d = nc.dram_tensor("dbg_" + name, list(shape), F32, kind="ExternalOutput").ap()
            k.dma("pool", d, ap, t, reads=[t], is_out=True)
            dbg_names.append(name)

        V = nc.vector
        A = nc.scalar
        P = nc.gpsimd
        PE = nc.tensor

        def mm(out, lhsT, rhs, start, stop, reads, writes, signal, sgc=False):
            k.op("pe", lambda: PE.matmul(out, lhsT, rhs, start=start, stop=stop, skip_group_check=sgc), reads, writes, signal)

        def tr(out, in_, reads, writes, signal):
            k.op("pe", lambda: PE.transpose(out, in_, IDB[:]), list(reads) + [IDB], writes, signal)

        def col(c, n=1):
            return SV[:, c:c + n]

        k.dma("sp", F2[:, 0:898], cf, F2, writes=[F2])
        k.dma("sp", CCF[:], cct, CCF, writes=[CCF])
        for g in range(NTC):
            k.dma("sp", Cx[g][:], ctx_in[g * 128:(g + 1) * 128, :], Cx[g], writes=[Cx[g]])
        k.dma("pool", PMR[:], pm_in[:, 0:128], PMR, writes=[PMR])
        k.dma("pool", PMD[:], pm_in[:, 128:256], PMD, writes=[PMD])
        k.op("dve", lambda: V.tensor_copy(IDB[:], F2[:, 0:128]), [F2], [IDB])
        k.op("dve", lambda: V.memset(SV[:], 0.0), [], [SV])
        k.op("dve", lambda: V.memset(col(c_eps), EPS), [], [SV])
        for g in range(NTT):
            k.op("pool", lambda: P.memset(VA[g][:], 1.0), [], [VA[g]])
        k.op("act", lambda: A.activation(CCF[:], CCF[:], AF.Silu), [CCF], [CCF])
        k.op("act", lambda: A.copy(SCC[:], CCF[:]), [CCF], [SCC])
        CF = F2
        D1, U_, D2, L_, P1, P2 = [F2[:, (i + 1) * 128:(i + 2) * 128] for i in range(6)]
        PC = F2[:, 7 * 128:7 * 128 + 2]

        GW = {"keys": [], "tiles": {}, "ni": 0, "ng": 0}
        WDEPTH = 2
        wcache = {}

        def resolve(key):
            kind = key[0]
            if kind == "mod":
                _, l_, b_ = key
                return w_mod[l_][:, b_ * 256:(b_ + 1) * 256], 8, None
            if kind == "in":
                _, l_, c0 = key
                return w_in[l_][:, c0:c0 + 256], 8, wbf_in[l_][:, c0:c0 + 256]
            if kind == "bo":
                _, l_, br_, q_ = key
                return w_bo[l_, br_][:, q_ * 256:(q_ + 1) * 256], 4, wbf_bo[l_, br_][:, q_ * 256:(q_ + 1) * 256]
            _, l_, q_ = key
            return w_out[l_][:, q_ * 256:(q_ + 1) * 256], 8, wbf_out[l_][:, q_ * 256:(q_ + 1) * 256]

        def gw_issue(key):
            src2d, nk, dst2d = resolve(key)
            wt = nxt("w", W)
            if dst2d is not None and key in wcache:
                k.dma("sp", wt[:, 0:nk, :], dst2d.rearrange("(kc p) c -> p kc c", p=128), wt, reads=[wcache[key]], writes=[wt])
            else:
                k.dma("pool", wt[:, 0:nk, :], src2d.rearrange("(kc p) c -> p kc c", p=128), wt, writes=[wt])
                if dst2d is not None and USE_WCACHE:
                    tobj = T(None, "wc")
                    k.dma("sp", dst2d.rearrange("(kc p) c -> p kc c", p=128), wt[:, 0:nk, :], wt, reads=[wt], writes=[tobj])
                    wcache[key] = tobj
            return wt

        def gw_issue_upto(n_):
            while GW["ni"] < min(n_, len(GW["keys"])):
                GW["tiles"][GW["ni"]] = gw_issue(GW["keys"][GW["ni"]])
                GW["ni"] += 1

        class WStream:
            def __init__(self, specs):
                self.specs = specs
                self.i = 0
                assert GW["keys"][GW["ng"]:GW["ng"] + len(specs)] == list(specs), (specs[:2], GW["keys"][GW["ng"]:GW["ng"] + 2])
                gw_issue_upto(GW["ng"] + WDEPTH)

            def get(self):
                key = self.specs[self.i]
                self.i += 1
                i_ = GW["ng"]
                assert GW["keys"][i_] == key
                gw_issue_upto(i_ + 1 + WDEPTH)
                GW["ng"] += 1
                return GW["tiles"].pop(i_)

        def win_spec(l, c0):
            return ("in", l, c0)

        def setup_specs(l):
            return [("mod", l, b) for b in range(12)]

        def p1_specs(l):
            return [win_spec(l, C_RK), win_spec(l, C_RK + 256), win_spec(l, C_RV), win_spec(l, C_RV + 256),
                    win_spec(l, C_DK), win_spec(l, C_DK + 256), win_spec(l, C_DV), win_spec(l, C_DV + 256)]

        def R_specs(l, states_only):
            specs = [win_spec(l, C_RK), win_spec(l, C_RK + 256), win_spec(l, C_RV), win_spec(l, C_RV + 256)]
            if not states_only:
                specs += [win_spec(l, C_RQ), win_spec(l, C_RQ + 256), win_spec(l, C_RG), win_spec(l, C_RG + 256)]
            return specs

        def D_specs(l):
            return [win_spec(l, C_DQ), win_spec(l, C_DQ + 256), win_spec(l, C_DG), win_spec(l, C_DG + 256)]

        def M_specs(l):
            return [win_spec(l, C_MV), win_spec(l, C_MV + 256), win_spec(l, C_MG), win_spec(l, C_MG + 256),
                    win_spec(l, C_MU), win_spec(l, C_MU + 256)]

        def G_specs(l):
            specs = []
            for br in range(3):
                for q in range(4):
                    specs.append(win_spec(l, C_ML + br * DM + q * 256))
                    specs.append(("bo", l, br, q))
            return specs

        def O_specs(l):
            return [("out", l, q) for q in range(4)]

        def p2_specs(l):
            return R_specs(l, False) + D_specs(l) + M_specs(l) + G_specs(l) + O_specs(l)

        def layer_setup(l):
            ws = WStream(setup_specs(l))
            for b in range(12):
                wt = ws.get()
                pg = nxt("pg", PG)
                for kc in range(8):
                    mm(pg[0:2, 0:256], SCC[:, kc, :], wt[:, kc, :], kc == 0, kc == 7, [SCC, wt], [pg], kc == 7)
                k.dma("sp", F1[0:2, 512:768], b_mod[l, b * 256:(b + 1) * 256].partition_broadcast(2), F1, writes=[F1])
                k.op("dve", lambda: V.tensor_tensor(F1[0:2, 768:1024], pg[0:2, 0:256], F1[0:2, 512:768], ALU.add), [pg, F1], [F1])
                k.dma("sp", mr_d[l][:, b * 256:(b + 1) * 256], F1[0:2, 768:1024], F1, reads=[F1], writes=[MRD[l][b]])
            k.dma("sp", F2[:, 0:898], cf, F2, writes=[F2])
            k.dma("sp", col(c_lg, 8), rdl[l].partition_broadcast(128), SV, writes=[SV])
            k.op("act", lambda: A.activation(col(c_lg, 8), col(c_lg, 8), AF.Sigmoid), [SV], [SV])
            k.op("act", lambda: A.activation(col(c_lg, 8), col(c_lg, 8), AF.Ln), [SV], [SV])
            k.op("act", lambda: A.activation(col(c_cdec, 8), col(c_lg, 8), AF.Exp, scale=128.0), [SV], [SV])
            for h in range(4):
                hs = slice(h * 128, (h + 1) * 128)
                k.op("act", lambda: A.activation(F1[:, 0:128], D1, AF.Exp, scale=col(c_lg + h)), [CF, SV], [F1])
                k.op("act", lambda: A.activation(F1[:, 128:256], D2, AF.Exp, scale=col(c_lg + 4 + h)), [CF, SV], [F1])
                k.op("dve", lambda: V.tensor_tensor(F1[:, 0:128], F1[:, 0:128], U_, ALU.mult), [F1, CF], [F1])
                k.op("dve", lambda: V.tensor_tensor(F1[:, 128:256], F1[:, 128:256], L_, ALU.mult), [F1, CF], [F1])
                k.op("dve", lambda: V.tensor_tensor(F1[:, 0:128], F1[:, 0:128], F1[:, 128:256], ALU.add), [F1], [F1])
                k.op("dve", lambda: V.tensor_scalar_mul(MCT[:, hs], F1[:, 0:128], RET_SCALE), [F1], [MCT])
                k.op("act", lambda: A.activation(QDF[:, hs], P1, AF.Exp, scale=col(c_lg + h)), [CF, SV], [QDF])
                k.op("act", lambda: A.activation(QDB[:, hs], P2, AF.Exp, scale=col(c_lg + 4 + h)), [CF, SV], [QDB])
                k.op("act", lambda: A.activation(col(c_kdec + h), PC[:, 0:1], AF.Exp, scale=col(c_lg + h)), [CF, SV], [SV])
                k.op("act", lambda: A.activation(col(c_kdec + 4 + h), PC[:, 1:2], AF.Exp, scale=col(c_lg + 4 + h)), [CF, SV], [SV])
            k.op("dve", lambda: V.tensor_scalar_mul(col(c_kdec, 8), col(c_kdec, 8), RET_SCALE), [SV], [SV])
            lam_init = 0.8 - 0.6 * math.exp(-0.3 * l)
            LQ = F1[:, 256:384]
            LK = F1[:, 384:512]
            k.dma("sp", LQ, dlq[l].partition_broadcast(128), F1, writes=[F1])
            k.dma("sp", LK, dlk[l].partition_broadcast(128), F1, writes=[F1])
            k.op("dve", lambda: V.tensor_tensor(LQ, LQ, LK, ALU.mult), [F1], [F1])
            k.op("dve", lambda: V.reduce_sum(col(c_e2, 2), LQ.rearrange("p (a b) -> p a b", a=2), AX.X), [F1], [SV])
            k.op("act", lambda: A.activation(col(c_e2, 2), col(c_e2, 2), AF.Exp), [SV], [SV])
            k.op("dve", lambda: V.tensor_tensor(col(c_nlam), col(c_e2 + 1), col(c_e2), ALU.subtract), [SV], [SV])
            k.op("dve", lambda: V.tensor_scalar_add(col(c_nlam), col(c_nlam), -lam_init), [SV], [SV])
            WSN_t = nxt("w", W)
            WSN = WSN_t[:, 0:2, :].rearrange("p k (g j) -> p (k g) j", j=128)
            k.dma("pool", WSN, mws[l].rearrange("g i j -> i g j"), WSN_t, writes=[WSN_t])
            k.dma("sp", col(c_bs, 4), mbs[l], SV, writes=[SV])
            tp = nxt("tp", TP)
            for g in range(4):
                tr(tp[:, g * 128:(g + 1) * 128], WSN[:, g, :], [WSN_t], [tp], g == 3)
            k.op("dve", lambda: V.tensor_copy(WST[:].rearrange("p g j -> p (g j)"), tp[:, 0:512]), [tp], [WST])
            k.op("dve", lambda: V.memset(SF[:], 0.0), [], [SF])
            k.op("dve", lambda: V.memset(TC[:], 0.0), [], [TC])
            k.op("pool", lambda: P.memset(SFB[:], 0.0), [], [SFB])

        def load_mod(l, s):
            st_ = [Y[0], Y[1], Y[2], Y[3], F1]
            k.dma("sp", st_[0][:], mr_d[l][s, 0:DM].partition_broadcast(128), st_[0], reads=MRD[l][0:4], writes=[st_[0]])
            k.dma("sp", st_[1][:], mr_d[l][s, DM:2 * DM].partition_broadcast(128), st_[1], reads=MRD[l][4:8], writes=[st_[1]])
            k.dma("sp", st_[2][:], g_pre[l].partition_broadcast(128), st_[2], writes=[st_[2]])
            k.dma("sp", st_[3][:], mr_d[l][s, 2 * DM:3 * DM].partition_broadcast(128), st_[3], reads=MRD[l][8:12], writes=[st_[3]])
            k.dma("sp", st_[4][:], g_post[l].partition_broadcast(128), st_[4], writes=[st_[4]])
            k.op("act", lambda: A.copy(SH[:], st_[0][:]), [st_[0]], [SH])
            k.op("dve", lambda: V.scalar_tensor_tensor(G1[:], st_[1][:], 1.0, st_[2][:], ALU.add, ALU.mult), [st_[1], st_[2]], [G1])
            k.op("dve", lambda: V.tensor_tensor(GT[:], st_[3][:], st_[4][:], ALU.mult), [st_[3], st_[4]], [GT])

        def rstd_from(ss_ap, n, inv_n, out_ap, t=None):
            t = SV if t is None else t
            k.op("act", lambda: A.activation(out_ap, ss_ap, AF.Sqrt, bias=col(c_eps), scale=inv_n), [t, SV], [t])
            k.op("dve", lambda: V.reciprocal(out_ap, out_ap), [t], [t])

        xseq = []
        xstate = {"pos": 0, "loaded": None}

        def x_load(l, sup):
            xsrc_ = x_in if l == 0 else xs_d
            for i, g in enumerate(sup):
                j = g - NTC
                rd = [XSD[l][j]] if l > 0 else []
                k.dma("sp", Xs[i][:], xsrc_[j * 128:(j + 1) * 128, :], Xs[i], reads=rd, writes=[Xs[i]])

        def make_hT_gen(l, sup, is_ctx, dst, tmps, hbs):
            n = len(sup)
            src = []
            if not is_ctx:
                assert xseq[xstate["pos"]] == (l, sup[0])
                if xstate["loaded"] != xstate["pos"]:
                    x_load(l, sup)
                xstate["pos"] += 1
            for i, g in enumerate(sup):
                src.append(Cx[g] if is_ctx else Xs[i])
            k.op("dve", lambda: V.memset(SVN[:, 0:n], 0.0), [], [SVN])
            for i in range(n):
                Hj = hbs[i % len(hbs)]
                k.op("act", lambda: A.activation(Hj[:], src[i][:], AF.Square, accum_out=SVN[:, i:i + 1]), [src[i]], [Hj, SVN])
            rstd_from(SVN[:, 0:n], n, 1.0 / DM, SVN[:, 8:8 + n], SVN)
            yield
            for i in range(n):
                Fx = tmps[i % len(tmps)]
                Hx = hbs[i % len(hbs)]
                k.op("dve", lambda: V.scalar_tensor_tensor(Fx[:], src[i][:], SVN[:, 8 + i:9 + i], G1[:], ALU.mult, ALU.mult),
                     [src[i], SVN, G1], [Fx])
                k.op("pool", lambda: P.tensor_tensor(Hx[:], Fx[:], SH[:], ALU.add), [Fx, SH], [Hx])
                yield
                tp = nxt("tp", TP)
                for kc in range(8):
                    tr(tp[:, kc * 128:(kc + 1) * 128], Hx[:, kc * 128:(kc + 1) * 128], [Hx], [tp], kc == 7)
                k.op("act", lambda: A.copy(dst[:, :, i * 128:(i + 1) * 128], tp[:].rearrange("p (k t) -> p k t", k=8)),
                     [tp], [dst])
                yield
            if not is_ctx and xstate["pos"] < len(xseq):
                nl, ns0 = xseq[xstate["pos"]]
                nsup = list(range(ns0, ns0 + ST))
                if nl == 0 or all(len(XSD[nl][g - NTC].w) > 0 for g in nsup):
                    x_load(nl, nsup)
                    xstate["loaded"] = xstate["pos"]

        def tick():
            if pend["gen"] is not None:
                try:
                    next(pend["gen"])
                except StopIteration:
                    pend["gen"] = None

        def drain():
            while pend["gen"] is not None:
                tick()

        def make_hT(l, sup, is_ctx, xsrc):
            if (not is_ctx) and pend["pos"] == xstate["pos"] - 1 and pend["dst"] is not None and pend["key"] == (l, sup[0]):
                drain()
            else:
                drain()
                dst = HT[1 - hcur["i"]]
                pend.update(gen=make_hT_gen(l, sup, is_ctx, dst, [F1, F2], [HB, HB2]), dst=dst, key=(l, sup[0]))
                pend["pos"] = xstate["pos"]
                drain()
            hcur["i"] = HT.index(pend["dst"])
            pend["dst"] = None

        def schedule_next_hT(l, tmps):
            if xstate["pos"] >= len(xseq):
                return
            nl, ns0 = xseq[xstate["pos"]]
            if nl != l:
                return
            nsup = list(range(ns0, ns0 + ST))
            dst = HT[1 - hcur["i"]]
            pend.update(gen=make_hT_gen(nl, nsup, False, dst, tmps, [HB, HB2]), dst=dst, key=(nl, ns0))
            pend["pos"] = xstate["pos"]
            tick()

        def load_rope(sup):
            j0 = sup[0] - NTC
            n = len(sup)
            k.dma("pool", RT[:, :, 0:n * 128], ropet[:, :, j0 * 128:(j0 + n) * 128], RT, writes=[RT])

        def proj_fm_group(ws, dst, NT, is_ctx, kind, split=False, dkt_sup=None, hook=None):
            ci, si = (0, 1) if kind == "ret" else (2, 3)
            pm = PMR if kind == "ret" else PMD
            if split:
                k.op("pool", lambda: P.memset(QK[64:128, 0:4, 0:NT], 0.0), [], [QK])
                k.op("pool", lambda: P.memset(QK[0:64, 4:8, 0:NT], 0.0), [], [QK])
            wts = {}

            def proj(hc):
                wi, c = divmod(hc, 2)
                if c == 0:
                    wts[wi] = ws.get()
                wt = wts[wi]
                tick()
                hT = HT[hcur["i"]]
                pz = PG[hc % 2]
                for kc in range(8):
                    mm(pz[:, 0:NT], wt[:, kc, c * 128:(c + 1) * 128], hT[:, kc, 0:NT], kc == 0, kc == 7, [wt, hT], [pz], kc == 7)
                return pz

            def finish(hc, pz):
                outs = [(slice(0, 128), dst + hc)] if not split else [(slice(0, 64), hc), (slice(64, 128), 4 + hc)]
                if is_ctx:
                    if dkt_sup is None:
                        for rs_, oc in outs:
                            k.op("act", lambda: A.copy(QK[rs_, oc, 0:NT], pz[rs_, 0:NT]), [pz], [QK])
                    else:
                        for i, g in enumerate(dkt_sup):
                            k.op("act", lambda: A.copy(DKT[g][:, hc, :], pz[:, i * 128:(i + 1) * 128]), [pz], [DKT[g]])
                    return
                zb = nxt("zb", ZB)
                k.op("act", lambda: A.copy(zb[:, 0:NT], pz[:, 0:NT]), [pz], [zb])
                pr = PG[2]
                mm(pr[:, 0:NT], pm[:], zb[:, 0:NT], True, True, [pm, zb], [pr], True)
                Fa, Fb = (F1, F2) if hc % 2 == 0 else (Y[2], Y[3])
                k.op("dve", lambda: V.tensor_tensor(Fa[:, 0:NT], pz[:, 0:NT], RT[:, ci, 0:NT], ALU.mult), [pz, RT], [Fa])
                k.op("dve", lambda: V.tensor_tensor(Fb[:, 0:NT], pr[:, 0:NT], RT[:, si, 0:NT], ALU.mult), [pr, RT], [Fb])
                if dkt_sup is None:
                    for rs_, oc in outs:
                        k.op("pool", lambda: P.tensor_tensor(QK[rs_, oc, 0:NT], Fa[rs_, 0:NT], Fb[rs_, 0:NT], ALU.add), [Fa, Fb], [QK])
                else:
                    for i, g in enumerate(dkt_sup):
                        ts = slice(i * 128, (i + 1) * 128)
                        k.op("pool", lambda: P.tensor_tensor(DKT[g][:, hc, :], Fa[:, ts], Fb[:, ts], ALU.add), [Fa, Fb], [DKT[g]])

            pz_prev = proj(0)
            if hook is not None:
                hook(0)
            for hc in range(1, 4):
                pz_cur = proj(hc)
                if hook is not None:
                    hook(hc)
                finish(hc - 1, pz_prev)
                pz_prev = pz_cur
            finish(3, pz_prev)

        dst_t = [None]
        VN = KTOK
        DFO = Y
        yT = QK

        def proj_tm(wt, i, nk=8, src=None):
            pg = nxt("pgx", PGX)
            s = HT[hcur["i"]] if src is None else src
            for kc in range(nk):
                mm(pg[:, 0:256], s[:, kc, i * 128:(i + 1) * 128], wt[:, kc, :], kc == 0, kc == nk - 1, [wt, s], [pg], kc == nk - 1)
            return pg

        def k_side(l, ws, sup, is_ctx, dirn):
            n = len(sup)
            NT = n * 128
            proj_fm_group(ws, 4, NT, is_ctx, "ret")
            for i in range(n):
                tp = nxt("tp", TP)
                for h in range(4):
                    tr(tp[:, h * 128:(h + 1) * 128], QK[:, 4 + h, i * 128:(i + 1) * 128], [QK], [tp], h == 3)
                k.op("dve", lambda: V.tensor_tensor(KTOK[i][:].rearrange("p (h d) -> p h d", h=4), tp[:, 0:512].rearrange("p (h d) -> p h d", h=4),
                                                    col(c_kdec + 4 * dirn, 4).unsqueeze(2).to_broadcast([128, 4, 128]), ALU.mult), [tp, SV], [KTOK[i]])
            for wi in range(2):
                wt = ws.get()
                for i in range(n):
                    pg = proj_tm(wt, i)
                    k.op("act", lambda: A.copy(RV[i][:, wi * 256:(wi + 1) * 256], pg[:, 0:256]), [pg], [RV[i]])

        def kv_update(i, state, dirn):
            for h in range(4):
                hs = slice(h * 128, (h + 1) * 128)
                mm(KVP[:, hs], KTOK[i][:, hs], RV[i][:, hs], True, True, [KTOK[i], RV[i]], [KVP], h == 3)
            for h in range(4):
                hs = slice(h * 128, (h + 1) * 128)
                k.op("dve", lambda: V.scalar_tensor_tensor(state[:, hs], state[:, hs], col(c_cdec + 4 * dirn + h), KVP[:, hs], ALU.mult, ALU.add),
                     [state, SV, KVP], [state])

        def headnorm_batch(items, const, dstT, do_tr=True):
            nI = len(items)
            for j, (i, src_ap, src_t, gate_t) in enumerate(items):
                half = F2[:, (j % 2) * 512:(j % 2 + 1) * 512]
                k.op("act", lambda: A.activation(half, src_ap, AF.Square), [src_t], [F2])
                k.op("dve", lambda: V.reduce_sum(SVH[:, 4 * j:4 * j + 4], half.rearrange("p (h e) -> p h e", h=4), AX.X), [F2], [SVH])
            k.op("act", lambda: A.activation(SVH[:, 0:4 * nI], SVH[:, 0:4 * nI], AF.Sqrt, bias=col(c_eps), scale=1.0 / 128), [SVH, SV], [SVH])
            k.op("dve", lambda: V.reciprocal(SVH[:, 0:4 * nI], SVH[:, 0:4 * nI]), [SVH], [SVH])
            for j, (i, src_ap, src_t, gate_t) in enumerate(items):
                ob = OBQ[i % 2]
                for h in range(4):
                    hs = slice(h * 128, (h + 1) * 128)
                    if const == 1.0:
                        k.op("dve", lambda: V.scalar_tensor_tensor(ob[h][:], src_ap[:, hs], SVH[:, 4 * j + h:4 * j + h + 1], gate_t[:, hs],
                                                                   ALU.mult, ALU.mult), [src_t, SVH, gate_t], [ob[h]])
                    else:
                        k.op("dve", lambda: V.tensor_scalar(F1[:, 512 + h * 128:512 + (h + 1) * 128], src_ap[:, hs], SVH[:, 4 * j + h:4 * j + h + 1], const,
                                                            ALU.mult, ALU.mult), [src_t, SVH], [F1])
                if const != 1.0:
                    for h in range(4):
                        k.op("pool", lambda: P.tensor_tensor(ob[h][:], F1[:, 512 + h * 128:512 + (h + 1) * 128], gate_t[:, h * 128:(h + 1) * 128], ALU.mult),
                             [F1, gate_t], [ob[h]])
                if do_tr:
                    transpose_into(ob, 4, dstT, i)

        def transpose_into(src_ts, nchunks, dstT, i):
            tp = nxt("tp", TP)
            for c in range(nchunks):
                tr(tp[:, c * 128:(c + 1) * 128], src_ts[c][:], [src_ts[c]], [tp], c == nchunks - 1)
            k.op("act", lambda: A.copy(dstT[:, :, i * 128:(i + 1) * 128], tp[:, 0:nchunks * 128].rearrange("p (c t) -> p c t", c=nchunks)),
                 [tp], [dstT])

        def pass1(l, sup, is_ctx, xsrc):
            n = len(sup)
            NT = n * 128
            make_hT(l, sup, is_ctx, xsrc)
            if not is_ctx:
                load_rope(sup)
                schedule_next_hT(l, [Y[0], Y[1]])
            ws = WStream(p1_specs(l))
            k_side(l, ws, sup, is_ctx, 1)
            order = list(reversed(range(n)))

            def chain_step(j):
                if j < n:
                    i = order[j]
                    g = sup[i]
                    k.op("act", lambda: A.copy(TBS[g][:], TC[:]), [TC], [TBS[g]])
                    kv_update(i, TC, 1)

            proj_fm_group(ws, 0, NT, is_ctx, "diff", dkt_sup=sup, hook=chain_step)
            for wi in range(2):
                wt = ws.get()
                for i, g in enumerate(sup):
                    pg = proj_tm(wt, i)
                    k.op("act", lambda: A.copy(VA[g][:, 2 * wi:2 * wi + 2, 0:128], pg[:, 0:256].rearrange("p (h e) -> p h e", h=2)),
                         [pg], [VA[g]])

        def stage_R(l, sup, is_ctx, states_only):
            n = len(sup)
            NT = n * 128
            ws = WStream(R_specs(l, states_only))
            k_side(l, ws, sup, is_ctx, 0)
            rgw = []
            if not states_only:
                proj_fm_group(ws, 0, NT, is_ctx, "ret")
                rgw = [ws.get(), ws.get()]

            def gate_proj(i):
                for wi in range(2):
                    pg = proj_tm(rgw[wi], i)
                    k.op("act", lambda: A.activation(SG_[i][:, wi * 256:(wi + 1) * 256], pg[:, 0:256], AF.Silu), [pg], [SG_[i]])
            def prep(i):
                ts = slice(i * 128, (i + 1) * 128)
                ps = nxt("pg", PG)
                for h in range(4):
                    hs = slice(h * 128, (h + 1) * 128)
                    mm(ps[:, hs], QK[:, 4 + h, ts], QK[:, h, ts], True, True, [QK], [ps], h == 3)
                SM, QF, QB = SMs[i % 2], QFs[i % 2], QBs[i % 2]
                k.op("dve", lambda: V.tensor_tensor(SM[:], ps[:], MCT[:], ALU.mult), [ps, MCT], [SM])
                k.op("pool", lambda: P.tensor_tensor(QF[:].rearrange("p (h t) -> p h t", h=4), QK[:, 0:4, ts],
                                                     QDF[:].rearrange("p (h t) -> p h t", h=4), ALU.mult), [QK, QDF], [QF])
                k.op("pool", lambda: P.tensor_tensor(QB[:].rearrange("p (h t) -> p h t", h=4), QK[:, 0:4, ts],
                                                     QDB[:].rearrange("p (h t) -> p h t", h=4), ALU.mult), [QK, QDB], [QB])

            if not states_only:
                prep(0)
            for i, g in enumerate(sup):
                if not states_only and i + 1 < n:
                    prep(i + 1)
                kv_update(i, SF, 0)
                if not states_only:
                    gate_proj(i)
                    SM, QF, QB = SMs[i % 2], QFs[i % 2], QBs[i % 2]
                    po = nxt("pg", PG)
                    for h in range(4):
                        hs = slice(h * 128, (h + 1) * 128)
                        mm(po[:, hs], SM[:, hs], RV[i][:, hs], True, False, [SM, RV[i]], [po], False)
                        mm(po[:, hs], QF[:, hs], SFB[:, hs], False, False, [QF, SFB], [po], False)
                        mm(po[:, hs], QB[:, hs], TBS[g][:, hs], False, True, [QB, TBS[g]], [po], h == 3)
                k.op("act", lambda: A.copy(SFB[:], SF[:]), [SF], [SFB])
                if not states_only:
                    headnorm_batch([(i, po[:], po, SG_[i])], 1.0, OT[0], do_tr=False)
                    if i > 0:
                        transpose_into(OBQ[(i - 1) % 2], 4, OT[0], i - 1)
            if not states_only:
                transpose_into(OBQ[(n - 1) % 2], 4, OT[0], n - 1)

        def stage_D(l, sup, is_ctx, before_tail=None):
            n = len(sup)
            NT = n * 128
            lam_init = 0.8 - 0.6 * math.exp(-0.3 * l)
            krange = list(range(NTC)) if is_ctx else list(range(NTT))
            ws = WStream(D_specs(l))
            proj_fm_group(ws, 0, NT, is_ctx, "diff", split=True)
            for wi in range(2):
                wt = ws.get()
                for i in range(n):
                    pg = proj_tm(wt, i)
                    k.op("act", lambda: A.activation(SG_[i][:, wi * 256:(wi + 1) * 256], pg[:, 0:256], AF.Silu), [pg], [SG_[i]])
            if not is_ctx:
                schedule_next_hT(l, [F2])
            steps = [(h, m, ki, kt) for h in range(4) for m in range(2) for ki, kt in enumerate(krange)]
            ps_of = {}
            LOOK = 2

            def emit_st(s_):
                h_, m_, ki_, kt_ = steps[s_]
                ps_ = nxt("pg", PG)
                mm(ps_[:, 0:NT], DKT[kt_][:, h_, :], QK[:, 4 * m_ + h_, 0:NT], True, True, [DKT[kt_], QK], [ps_], True)
                ps_of[s_] = ps_

            for s_ in range(min(LOOK, len(steps))):
                emit_st(s_)
            for s_ in range(len(steps)):
                h, m, ki, kt = steps[s_]
                ps = ps_of.pop(s_)
                e = nxt("e", E_)
                k.op("act", lambda: A.activation(e[:, 0:NT], ps[:, 0:NT], AF.Exp, scale=0.125), [ps], [e])
                if s_ + LOOK < len(steps):
                    emit_st(s_ + LOOK)
                if s_ % 8 == 7:
                    tick()
                for i in range(n):
                    oa = OA[i // 2]
                    o0 = (i % 2) * 129
                    mm(oa[:, o0:o0 + 129], e[:, i * 128:(i + 1) * 128], VA[kt][:, h, 0:129], ki == 0 and i % 2 == 0,
                       ki == len(krange) - 1, [e, VA[kt]], [oa], i == n - 1, sgc=True)
                if ki != len(krange) - 1:
                    continue
                nb = (n + 1) // 2
                for b_ in range(nb):
                    stg = Y[2 * m + b_]
                    k.op("dve", lambda: V.tensor_copy(stg[:, 512:512 + 258], OA[b_][:, 0:258]), [OA[b_]], [stg])
                if m == 0:
                    continue
                for b_ in range(nb):
                    nb2 = min(2, n - 2 * b_)
                    sv0 = Y[b_][:, 512:512 + 258].rearrange("p (t c) -> p t c", c=129)[:, 0:nb2, 128]
                    sv1 = Y[2 + b_][:, 512:512 + 258].rearrange("p (t c) -> p t c", c=129)[:, 0:nb2, 128]
                    k.op("dve", lambda: V.reciprocal(R0[:, 2 * b_:2 * b_ + nb2], sv0), [Y[b_]], [R0])
                    k.op("dve", lambda: V.reciprocal(R1[:, 2 * b_:2 * b_ + nb2], sv1), [Y[2 + b_]], [R1])
                k.op("dve", lambda: V.tensor_scalar_mul(R1[:, 0:n], R1[:, 0:n], col(c_nlam)), [R1, SV], [R1])
                for i in range(n):
                    s0 = Y[i // 2]
                    s1 = Y[2 + i // 2]
                    o0 = 512 + (i % 2) * 129
                    tt = SGM[i % 2]
                    ttv = tt[:].bitcast(F32)
                    k.op("dve", lambda: V.tensor_scalar_mul(ttv, s0[:, o0:o0 + 128], R0[:, i:i + 1]), [s0, R0], [tt])
                    k.op("dve", lambda: V.scalar_tensor_tensor(DFO[i][:, h * 128:(h + 1) * 128], s1[:, o0:o0 + 128], R1[:, i:i + 1], ttv,
                                                               ALU.mult, ALU.add), [s1, R1, tt], [DFO[i]])
            drain()
            if before_tail is not None:
                before_tail()
            headnorm_batch([(i, DFO[i][:, 0:512], DFO[i], SG_[i]) for i in range(n)], 1.0 - lam_init, OT[2])

        def stage_M_begin(l, sup):
            n = len(sup)
            ws = WStream(M_specs(l))
            k.op("dve", lambda: V.memset(SVMa[:], 0.0), [], [SVMa])
            k.op("dve", lambda: V.memset(SVMb[:], 0.0), [], [SVMb])
            for wi in range(2):
                wt = ws.get()
                for i in range(n):
                    pg = proj_tm(wt, i)
                    k.op("act", lambda: A.activation(Y[i][:, 512 + wi * 256:512 + (wi + 1) * 256], pg[:, 0:256], AF.Copy,
                                                     accum_out=SVMa[:, i * 2 + wi:i * 2 + wi + 1]), [pg], [Y[i], SVMa])
                    k.op("act", lambda: A.activation(VN[i][:, wi * 256:(wi + 1) * 256], pg[:, 0:256], AF.Square,
                                                     accum_out=SVMb[:, i * 2 + wi:i * 2 + wi + 1]), [pg], [VN[i], SVMb])
            return lambda: stage_M_rest(l, sup, ws)

        def stage_M_rest(l, sup, ws):
            n = len(sup)
            k.op("dve", lambda: V.reduce_sum(SVM[:, 16:16 + n], SVMa[:, 0:2 * n].rearrange("p (i w) -> p i w", w=2), AX.X), [SVMa], [SVM])
            k.op("dve", lambda: V.reduce_sum(SVM[:, 20:20 + n], SVMb[:, 0:2 * n].rearrange("p (i w) -> p i w", w=2), AX.X), [SVMb], [SVM])
            k.op("dve", lambda: V.tensor_scalar_mul(SVM[:, 16:16 + n], SVM[:, 16:16 + n], 1.0 / 512), [SVM], [SVM])
            k.op("dve", lambda: V.tensor_tensor(SVM[:, 24:24 + n], SVM[:, 16:16 + n], SVM[:, 16:16 + n], ALU.mult), [SVM], [SVM])
            k.op("dve", lambda: V.scalar_tensor_tensor(SVM[:, 20:20 + n], SVM[:, 20:20 + n], 1.0 / 512, SVM[:, 24:24 + n], ALU.mult, ALU.subtract), [SVM], [SVM])
            k.op("act", lambda: A.activation(SVM[:, 20:20 + n], SVM[:, 20:20 + n], AF.Sqrt, bias=col(c_eps), scale=1.0), [SVM, SV], [SVM])
            k.op("dve", lambda: V.reciprocal(SVM[:, 20:20 + n], SVM[:, 20:20 + n]), [SVM], [SVM])
            k.op("dve", lambda: V.scalar_tensor_tensor(SVM[:, 24:24 + n], SVM[:, 16:16 + n], -1.0, SVM[:, 20:20 + n], ALU.mult, ALU.mult), [SVM], [SVM])
            for i in range(n):
                k.op("pool", lambda: P.tensor_scalar(VN[i][:], Y[i][:, 512:1024], SVM[:, 20 + i:21 + i], SVM[:, 24 + i:25 + i], ALU.mult, ALU.add),
                     [Y[i], SVM], [VN[i]])
            for wi in range(2):
                wt = ws.get()
                for i in range(n):
                    pg = proj_tm(wt, i)
                    k.op("act", lambda: A.activation(SG_[i][:, wi * 256:(wi + 1) * 256], pg[:, 0:256], AF.Silu), [pg], [SG_[i]])
            for wi in range(2):
                wt = ws.get()
                for i in range(n):
                    pg = proj_tm(wt, i)
                    cs = slice(wi * 256, (wi + 1) * 256)
                    k.op("dve", lambda: V.tensor_tensor(SG_[i][:, cs], pg[:, 0:256], SG_[i][:, cs], ALU.mult), [pg, SG_[i]], [SG_[i]])
            for i in range(n):
                pg = nxt("pg", PG)
                for g in range(4):
                    gs = slice(g * 128, (g + 1) * 128)
                    mm(pg[:, gs], WST[:, g, :], VN[i][:, gs], True, True, [WST, VN[i]], [pg], g == 3)
                for g in range(4):
                    gs = slice(g * 128, (g + 1) * 128)
                    k.op("dve", lambda: V.scalar_tensor_tensor(OBQ[i % 2][g][:], pg[:, gs], col(c_bs + g), SG_[i][:, gs], ALU.add, ALU.mult),
                         [pg, SV, SG_[i]], [OBQ[i % 2][g]])
                transpose_into(OBQ[i % 2], 4, OT[1], i)

        def stage_G(l, sup):
            n = len(sup)
            ws = WStream(G_specs(l))
            for br in range(3):
                for q in range(4):
                    wl = ws.get()
                    wb = ws.get()
                    qs = slice(q * 256, (q + 1) * 256)
                    for i in range(n):
                        pa = proj_tm(wl, i)
                        sg = nxt("sgm", SGM)
                        k.op("act", lambda: A.activation(sg[:], pa[:, 0:256], AF.Sigmoid), [pa], [sg])
                        pb = proj_tm(wb, i, nk=4, src=OT[br])
                        if br == 0:
                            k.op("dve", lambda: V.tensor_tensor(Y[i][:, qs], pb[:, 0:256], sg[:], ALU.mult), [pb, sg], [Y[i]])
                        else:
                            ft = F1 if i % 2 == 0 else F2
                            k.op("dve", lambda: V.tensor_tensor(ft[:, 0:256], pb[:, 0:256], sg[:], ALU.mult), [pb, sg], [ft])
                            k.op("pool", lambda: P.tensor_tensor(Y[i][:, qs], Y[i][:, qs], ft[:, 0:256], ALU.add), [ft, Y[i]], [Y[i]])
            for i in range(n):
                k.op("act", lambda: A.copy(HB[:], Y[i][:]), [Y[i]], [HB])
                tp = nxt("tp", TP)
                for kc in range(8):
                    tr(tp[:, kc * 128:(kc + 1) * 128], HB[:, kc * 128:(kc + 1) * 128], [HB], [tp], kc == 7)
                k.op("act", lambda: A.copy(yT[:, :, i * 128:(i + 1) * 128], tp[:].rearrange("p (k t) -> p k t", k=8)), [tp], [yT])

        def stage_O(l, sup, is_ctx, xdst, last):
            n = len(sup)
            xsrc_ = x_in if l == 0 else xs_d
            FX = [F1, F2]

            def xres_load(i):
                j = sup[i] - NTC
                rd = [XSD[l][j]] if l > 0 else []
                k.dma("sp", FX[i % 2][:], xsrc_[j * 128:(j + 1) * 128, :], FX[i % 2], reads=rd, writes=[FX[i % 2]])

            if not is_ctx:
                for i in range(min(2, n)):
                    xres_load(i)
            ws = WStream(O_specs(l))
            k.op("dve", lambda: V.memset(SVN[:, 16:32], 0.0), [], [SVN])
            for q in range(4):
                wo = ws.get()
                qs = slice(q * 256, (q + 1) * 256)
                for i in range(n):
                    pg = proj_tm(wo, i, src=yT)
                    k.op("act", lambda: A.activation(Y[i][:, qs], pg[:, 0:256], AF.Copy), [pg], [Y[i]])
                    k.op("act", lambda: A.activation(SG_[i][:, 0:256], pg[:, 0:256], AF.Square, accum_out=SVN[:, 16 + i * 4 + q:17 + i * 4 + q]),
                         [pg], [SG_[i], SVN])
            k.op("dve", lambda: V.reduce_sum(SVN[:, 0:n], SVN[:, 16:16 + 4 * n].rearrange("p (i q) -> p i q", q=4), AX.X), [SVN], [SVN])
            rstd_from(SVN[:, 0:n], n, 1.0 / DM, SVN[:, 8:8 + n], SVN)
            for i, g in enumerate(sup):
                k.op("dve", lambda: V.scalar_tensor_tensor(Y[i][:], Y[i][:], SVN[:, 8 + i:9 + i], GT[:], ALU.mult, ALU.mult), [Y[i], SVN, GT], [Y[i]])
                if is_ctx:
                    k.op("pool", lambda: P.tensor_tensor(Cx[g][:], Cx[g][:], Y[i][:], ALU.add), [Cx[g], Y[i]], [Cx[g]])
                else:
                    fx = FX[i % 2]
                    k.op("pool", lambda: P.tensor_tensor(Y[i][:], Y[i][:], fx[:], ALU.add), [Y[i], fx], [Y[i]])
                    if i + 2 < n:
                        xres_load(i + 2)
                    j = g - NTC
                    if last:
                        k.dma("sp", xdst[j * 128:(j + 1) * 128, :], Y[i][:], Y[i], reads=[Y[i]], is_out=True)
                    else:
                        k.dma("sp", xdst[j * 128:(j + 1) * 128, :], Y[i][:], Y[i], reads=[Y[i]], writes=[XSD[l + 1][j]])

        def pass2(l, sup, is_ctx, xsrc, xdst, last):
            n = len(sup)
            dg = debug and l == 0 and (is_ctx or sup[0] == NTC)
            tag = "c_" if is_ctx else "l_"
            make_hT(l, sup, is_ctx, xsrc)
            if dg:
                dbg(tag + "hT", HT[hcur["i"]][:, :, 0:n * 128], HT[hcur["i"]], [128, 8, n * 128])
            if not is_ctx:
                load_rope(sup)
            stage_R(l, sup, is_ctx, False)
            if dg:
                dbg(tag + "qk_r", QK[:, :, 0:n * 128], QK, [128, 8, n * 128])
                dbg(tag + "oT_ret", OT[0][:, :, 0:n * 128], OT[0], [128, 4, n * 128])
                dbg(tag + "sf", SF[:], SF, [128, 512])
            mrest = {}
            stage_D(l, sup, is_ctx, before_tail=lambda: mrest.update(f=stage_M_begin(l, sup)))
            if dg:
                dbg(tag + "qk_d", QK[:, :, 0:n * 128], QK, [128, 8, n * 128])
                dbg(tag + "oT_diff", OT[2][:, :, 0:n * 128], OT[2], [128, 4, n * 128])
                dbg(tag + "dfo", Y[0][:, 0:512], Y[0], [128, 512])
            mrest["f"]()
            if dg:
                dbg(tag + "oT_mlp", OT[1][:, :, 0:n * 128], OT[1], [128, 4, n * 128])
            stage_G(l, sup)
            if dg:
                dbg(tag + "y", Y[0][:], Y[0], [128, DM])
            stage_O(l, sup, is_ctx, xdst, last)

        csup = list(range(NTC))
        lsups = [list(range(NTC + s * ST, NTC + (s + 1) * ST)) for s in range(NTL // ST)]
        for l in range(depth):
            xseq.extend([(l, sp_[0]) for sp_ in reversed(lsups)])
            xseq.extend([(l, sp_[0]) for sp_ in lsups])
        for l in range(depth):
            last_ = l == depth - 1
            GW["keys"] += setup_specs(l) + p1_specs(l)
            GW["keys"] += R_specs(l, True) if (last_ and depth == DEPTH) else p2_specs(l)
            for _ in lsups:
                GW["keys"] += p1_specs(l)
            for _ in lsups:
                GW["keys"] += p2_specs(l)
        for l in range(depth):
            last = l == depth - 1
            xsrc = x_in if l == 0 else xs_d
            xdst = out_d if last else xs_d
            layer_setup(l)
            load_mod(l, 1)
            pass1(l, csup, True, None)
            if last and depth == DEPTH:
                stage_R(l, csup, True, True)
            else:
                pass2(l, csup, True, None, None, False)
            if debug and l == 0:
                dbg("tc_ctx", TC[:], TC, [128, 512])
                dbg("dkt0", DKT[0][:], DKT[0], [128, 4, 128])
                dbg("va0", VA[0][:], VA[0], [128, 4, 130])
                dbg("mct", MCT[:], MCT, [128, 512])
                dbg("sv", SV[:], SV, [128, 128])
            load_mod(l, 0)
            for sup in reversed(lsups):
                pass1(l, sup, False, xsrc)
            for sup in lsups:
                pass2(l, sup, False, xsrc, xdst, last)
        assert GW["ng"] == len(GW["keys"]), (GW["ng"], len(GW["keys"]))
        k.finish()
        print("instr counts", k.nins, "sems", len(k.sem), "sbuf KiB/partition", k.sb_bytes / 1024)
    nc._dbg_names = dbg_names
    return nc


def _host_consts():
    j = np.arange(128, dtype=np.float64)[:, None]
    i = np.arange(128, dtype=np.float64)[None, :]
    ident = np.eye(128)
    d1 = np.maximum(i - j, 0.0)
    u = (i >= j).astype(np.float64)
    d2 = np.maximum(j - i, 0.0)
    lo = (j >= i).astype(np.float64)
    p1 = np.broadcast_to(i + 1.0, (128, 128))
    p2 = np.broadcast_to(128.0 - i, (128, 128))
    pc = np.concatenate([127.0 - j, j], axis=1)
    cf = np.concatenate([ident, d1, u, d2, lo, p1, p2, pc], axis=1).astype(np.float32)
    sel = np.zeros((2, 2, 128), np.float32)
    sel[0, 0] = 1.0
    sel[1, 1] = 1.0
    n = np.arange(SEQ)
    row = (n // 64).astype(np.float64)
    colp = (n % 64).astype(np.float64)

    def ang(head_dim):
        nf = head_dim // 4
        inv = 10000.0 ** (-np.arange(nf, dtype=np.float64) / nf)
        return np.concatenate([row[:, None] * inv, colp[:, None] * inv], axis=-1)

    a_r = ang(128)
    a_d = ang(64)
    rt = np.zeros((128, 4, SEQ), np.float32)
    d = np.arange(128)
    rt[:, 0, :] = np.cos(a_r)[:, d % 64].T
    rt[:, 1, :] = (np.sin(a_r)[:, d % 64] * np.where(d < 64, -1.0, 1.0)[None, :]).T
    dd = d % 64
    rt[:, 2, :] = np.cos(a_d)[:, dd % 32].T
    rt[:, 3, :] = (np.sin(a_d)[:, dd % 32] * np.where(dd < 32, -1.0, 1.0)[None, :]).T
    pm = np.zeros((128, 256), np.float32)
    d = np.arange(128)
    pm[d, (d + 64) % 128] = 1.0
    pm[d, 128 + (d // 64) * 64 + ((d % 64) + 32) % 64] = 1.0
    return cf, sel, rt, pm


_NC_CACHE = {}


_DBG = {}


def kernel(x, c, ctx, c_ctx, w_mod, b_mod, g_pre, g_post, w_in, ret_decay_logit, mlp_w_s, mlp_b_s,
           diff_lambda_q, diff_lambda_k, w_branch_out, w_out, _cores=8, _depth=DEPTH, _debug=False):
    f = lambda a: np.ascontiguousarray(np.asarray(a, dtype=np.float32))
    x, c, ctx, c_ctx = f(x), f(c), f(ctx), f(c_ctx)
    cf, sel, rt, pm = _host_consts()
    shared = {
        "w_mod": f(w_mod), "b_mod": f(b_mod), "g_pre": f(g_pre), "g_post": f(g_post), "w_in": f(w_in),
        "rdl": f(ret_decay_logit).reshape(DEPTH, 8), "mws": f(mlp_w_s),
        "mbs": np.ascontiguousarray(f(mlp_b_s).transpose(0, 2, 1)),
        "dlq": f(diff_lambda_q).reshape(DEPTH, 128), "dlk": f(diff_lambda_k).reshape(DEPTH, 128),
        "w_bo": f(w_branch_out), "w_out": f(w_out), "cf": cf, "ropet": rt, "pm": pm,
    }
    key = (_depth, _debug)
    if key not in _NC_CACHE:
        _NC_CACHE[key] = build_program(_depth, _debug)
    nc = _NC_CACHE[key]
    in_maps = []
    for b in range(_cores):
        cc = np.stack([c[b], c_ctx], axis=0)
        cct = np.ascontiguousarray(cc.reshape(2, 8, 128).transpose(2, 1, 0))
        m = dict(shared)
        m.update({"x": x[b], "ctx": ctx[b], "cct": cct})
        in_maps.append(m)
    res = run_bass_kernel_spmd(nc, in_maps, core_ids=list(range(_cores)))
    if _debug:
        for nm in nc._dbg_names:
            _DBG[nm] = np.asarray(res.results[0]["dbg_" + nm])
    return np.stack([np.asarray(r["out"], dtype=np.float32) for r in res.results], axis=0)
```

```python
import math
from contextlib import ExitStack

import numpy as np
import concourse.bass as bass
import concourse.mybir as mybir
from concourse.bass_utils import run_bass_kernel_spmd

F32 = mybir.dt.float32
BF16 = mybir.dt.bfloat16
AF = mybir.ActivationFunctionType
ALU = mybir.AluOpType
AX = mybir.AxisListType

DEPTH = 2
DM = 1024
SEQ = 2048
CTX = 256
NTC = CTX // 128
NTL = SEQ // 128
NTT = NTC + NTL
ST = 4
IN_COLS = 8704
EPS = 1e-6
RET_SCALE = 128.0 ** -0.5
USE_WCACHE = True
C_RK, C_RV, C_DK, C_DV, C_RQ, C_RG, C_DQ, C_DG, C_MU, C_MV, C_MG, C_ML = [512 * i for i in range(12)]


class T:
    def __init__(self, h, name):
        self.h = h
        self.name = name
        self.w = {}
        self.r = {}
        self.dsem = None
        self.excl = False

    def __getitem__(self, k):
        return self.h[k]


class KB:
    def __init__(self, nc, stack):
        self.nc = nc
        self.stack = stack
        self.eng = {"pe": nc.tensor, "dve": nc.vector, "act": nc.scalar, "pool": nc.gpsimd, "sp": nc.sync}
        self.sem = {}
        self.cnt = {}
        for e in self.eng:
            self.sem[e] = stack.enter_context(nc.semaphore("s_" + e))
            self.cnt[e] = 0
        self.seen = {e: {} for e in self.eng}
        self.out_events = []
        self.nins = {e: 0 for e in self.eng}

    def sb(self, name, shape, dt):
        self.sb_bytes = getattr(self, "sb_bytes", 0) + int(np.prod(shape[1:])) * (2 if dt == BF16 else 4)
        return T(self.stack.enter_context(self.nc.sbuf_tensor(name, list(shape), dt)), name)

    def ps(self, name, shape, dt=F32):
        t = T(self.stack.enter_context(self.nc.psum_tensor(name, list(shape), dt)), name)
        t.excl = True
        return t

    def _wait(self, e, reads, writes):
        need = {}

        def add(k, v, raw):
            if k == e and e == "pe":
                return
            need[k] = max(need.get(k, 0), v)

        for t in reads:
            for k, v in t.w.items():
                add(k, v, True)
            if t.excl:
                for k, v in t.r.items():
                    if k != e:
                        add(k, v, False)
        for t in writes:
            for k, v in t.w.items():
                add(k, v, False)
            for k, v in t.r.items():
                add(k, v, False)
        for k, v in need.items():
            if k not in self.eng:
                v = self.cnt[k]
            if self.seen[e].get(k, 0) < v:
                assert v <= self.cnt[k], f"wait on unsignaled event {k} {v} {self.cnt[k]}"
                self.eng[e].wait_ge(self.sem[k], v)
                self.seen[e][k] = v

    def _post(self, ev, reads, writes):
        for t in reads:
            t.r[ev[0]] = max(t.r.get(ev[0], 0), ev[1])
        for t in writes:
            t.w[ev[0]] = max(t.w.get(ev[0], 0), ev[1])
            t.r = {}

    def op(self, e, fn, reads=(), writes=(), signal=True):
        self._wait(e, reads, writes)
        ins = fn()
        self.nins[e] += 1
        if signal:
            self.cnt[e] += 1
            ins.then_inc(self.sem[e], 1)
            ev = (e, self.cnt[e])
        else:
            ev = (e, self.cnt[e] + 1)
        self._post(ev, reads, writes)
        return ins

    def dma(self, q, out_ap, in_ap, sb_t, reads=(), writes=(), is_out=False):
        self._wait(q, reads, writes)
        if sb_t.dsem is None:
            sb_t.dsem = {}
        kind = "sw" if q == "pool" else "hw"
        if kind not in sb_t.dsem:
            nm = "d_" + sb_t.name + "_" + kind
            sb_t.dsem[kind] = nm
            self.sem[nm] = self.stack.enter_context(self.nc.semaphore(nm))
            self.cnt[nm] = 0
        k = sb_t.dsem[kind]
        ins = self.eng[q].dma_start(out=out_ap, in_=in_ap)
        self.cnt[k] += 16
        ins.then_inc(self.sem[k], 16)
        ev = (k, self.cnt[k])
        self._post(ev, reads, writes)
        if is_out:
            self.out_events.append(ev)
        return ins

    def finish(self):
        fin = {}
        for k, v in self.out_events:
            fin[k] = self.cnt[k]
        for k, v in fin.items():
            self.eng["sp"].wait_ge(self.sem[k], v)


def build_program(depth=DEPTH, debug=False):
    nc = bass.Bass("TRN2", target_bir_lowering=False)

    def din(name, shape):
        return nc.dram_tensor(name, list(shape), F32, kind="ExternalInput").ap()

    x_in = din("x", [SEQ, DM])
    ctx_in = din("ctx", [CTX, DM])
    cct = din("cct", [128, 8, 2])
    w_mod = din("w_mod", [DEPTH, DM, 3 * DM])
    b_mod = din("b_mod", [DEPTH, 3 * DM])
    g_pre = din("g_pre", [DEPTH, DM])
    g_post = din("g_post", [DEPTH, DM])
    w_in = din("w_in", [DEPTH, DM, IN_COLS])
    rdl = din("rdl", [DEPTH, 8])
    mws = din("mws", [DEPTH, 4, 128, 128])
    mbs = din("mbs", [DEPTH, 128, 4])
    dlq = din("dlq", [DEPTH, 128])
    dlk = din("dlk", [DEPTH, 128])
    w_bo = din("w_bo", [DEPTH, 3, 512, DM])
    w_out = din("w_out", [DEPTH, DM, DM])
    cf = din("cf", [128, 7 * 128 + 2])
    ropet = din("ropet", [128, 4, SEQ])
    pm_in = din("pm", [128, 256])
    out_d = nc.dram_tensor("out", [SEQ, DM], F32, kind="ExternalOutput").ap()
    xs_d = nc.dram_tensor("xs", [SEQ, DM], F32, kind="Internal").ap()
    mr_d = nc.dram_tensor("mrows", [DEPTH, 2, 3 * DM], F32, kind="Internal").ap()
    wbf_in = nc.dram_tensor("wbf_in", [DEPTH, DM, IN_COLS], BF16, kind="Internal").ap()
    wbf_bo = nc.dram_tensor("wbf_bo", [DEPTH, 3, 512, DM], BF16, kind="Internal").ap()
    wbf_out = nc.dram_tensor("wbf_out", [DEPTH, DM, DM], BF16, kind="Internal").ap()

    with ExitStack() as st:
        k = KB(nc, st)
        XSD = [[T(None, f"xsd{l}_{j}") for j in range(NTL)] for l in range(2)]
        MRD = [[T(None, f"mrd{l}_{j}") for j in range(12)] for l in range(DEPTH)]

        Cx = [k.sb(f"cx{g}", [128, DM], F32) for g in range(NTC)]
        Xs = [k.sb(f"xsb{i}", [128, DM], F32) for i in range(ST)]
        Y = [k.sb(f"y{i}", [128, DM], F32) for i in range(ST)]
        F1 = k.sb("f1", [128, DM], F32)
        F2 = k.sb("f2", [128, DM], F32)
        G1 = k.sb("g1", [128, DM], BF16)
        SH = k.sb("sh", [128, DM], BF16)
        GT = k.sb("gt", [128, DM], BF16)
        DKT = [k.sb(f"dkt{g}", [128, 4, 128], BF16) for g in range(NTT)]
        VA = [k.sb(f"va{g}", [128, 4, 130], BF16) for g in range(NTT)]
        TBS = [k.sb(f"tbs{g}", [128, 512], BF16) for g in range(NTT)]
        HT = [k.sb("hTa", [128, 8, ST * 128], BF16), k.sb("hTb", [128, 8, ST * 128], BF16)]
        hcur = {"i": 0}
        pend = {"gen": None, "pos": -1, "dst": None}
        HB = k.sb("hb", [128, DM], BF16)
        W = [k.sb(f"w{i}", [128, 8, 256], BF16) for i in range(4)]
        ZB = [k.sb(f"zb{i}", [128, ST * 128], BF16) for i in range(2)]
        PMR = k.sb("pmr", [128, 128], BF16)
        PMD = k.sb("pmd", [128, 128], BF16)
        RT = k.sb("ropes", [128, 4, ST * 128], BF16)
        QK = k.sb("qk", [128, 8, ST * 128], BF16)
        KTOK = [k.sb(f"ktok{i}", [128, 512], BF16) for i in range(ST)]
        RV = [k.sb(f"rv{i}", [128, 512], BF16) for i in range(ST)]
        SG_ = [k.sb(f"sg{i}", [128, 512], BF16) for i in range(ST)]
        OT = [k.sb(f"oT{b}", [128, 4, ST * 128], BF16) for b in range(3)]
        SMs = [k.sb(f"sm{i}", [128, 512], BF16) for i in range(2)]
        QFs = [k.sb(f"qf{i}", [128, 512], BF16) for i in range(2)]
        QBs = [k.sb(f"qb{i}", [128, 512], BF16) for i in range(2)]
        OBQ = [[k.sb(f"ob{b_}{h_}", [128, 128], BF16) for h_ in range(4)] for b_ in range(2)]
        E_ = [k.sb(f"e{i}", [128, 512], BF16) for i in range(3)]
        SGM = [k.sb(f"sgm{i}", [128, 256], BF16) for i in range(2)]
        SF = k.sb("sf", [128, 512], F32)
        SFB = k.sb("sfb", [128, 512], BF16)
        TC = k.sb("tcur", [128, 512], F32)
        MCT = k.sb("mct", [128, 512], BF16)
        QDF = k.sb("qdf", [128, 512], BF16)
        QDB = k.sb("qdb", [128, 512], BF16)
        IDB = k.sb("idb", [128, 128], BF16)
        WST = k.sb("wst", [128, 4, 128], BF16)
        SV = k.sb("sv", [128, 128], F32)
        R0 = k.sb("r0", [128, 4], F32)
        R1 = k.sb("r1", [128, 4], F32)
        SVMa = k.sb("svma", [128, 8], F32)
        SVMb = k.sb("svmb", [128, 8], F32)
        SVM = k.sb("svm", [128, 32], F32)
        SVH = k.sb("svh", [128, 32], F32)
        SVN = k.sb("svn", [128, 48], F32)
        HB2 = k.sb("hb2", [128, DM], BF16)
        SCC = k.sb("scc", [128, 8, 2], BF16)
        CCF = k.sb("ccf", [128, 8, 2], F32)
        c_eps, c_nlam, c_ss, c_rs, c_lg, c_kdec, c_cdec, c_bs, c_tmp = 0, 1, 4, 12, 20, 28, 36, 44, 48
        c_ss4, c_e2, c_mean, c_sum2 = 56, 76, 80, 84

        PG = [k.ps(f"pg{i}", [128, 512]) for i in range(3)]
        OA = [k.ps(f"oa{i}", [128, 512]) for i in range(2)]
        KVP = k.ps("kvp", [128, 512])
        TP = [k.ps(f"tp{i}", [128, 1024], BF16) for i in range(2)]
        rr = {"pg": 0, "tp": 0, "w": 0, "e": 0, "sgm": 0, "mrb": 0, "zb": 0, "pgx": 0}
        PGX = PG + OA

        def nxt(key, lst):
            t = lst[rr[key] % len(lst)]
            rr[key] += 1
            return t

        dbg_names = []

        def dbg(name, ap, t, shape):
            if not debug:
                return
            d = nc.dram_tensor("dbg_" + name, list(shape), F32, kind="ExternalOutput").ap()
            k.dma("pool", d, ap, t, reads=[t], is_out=True)
            dbg_names.append(name)

        V = nc.vector
        A = nc.scalar
        P = nc.gpsimd
        PE = nc.tensor

        def mm(out, lhsT, rhs, start, stop, reads, writes, signal, sgc=False):
            k.op("pe", lambda: PE.matmul(out, lhsT, rhs, start=start, stop=stop, skip_group_check=sgc), reads, writes, signal)

        def tr(out, in_, reads, writes, signal):
            k.op("pe", lambda: PE.transpose(out, in_, IDB[:]), list(reads) + [IDB], writes, signal)

        def col(c, n=1):
            return SV[:, c:c + n]

        k.dma("sp", F2[:, 0:898], cf, F2, writes=[F2])
        k.dma("sp", CCF[:], cct, CCF, writes=[CCF])
        for g in range(NTC):
            k.dma("sp", Cx[g][:], ctx_in[g * 128:(g + 1) * 128, :], Cx[g], writes=[Cx[g]])
        k.dma("pool", PMR[:], pm_in[:, 0:128], PMR, writes=[PMR])
        k.dma("pool", PMD[:], pm_in[:, 128:256], PMD, writes=[PMD])
        k.op("dve", lambda: V.tensor_copy(IDB[:], F2[:, 0:128]), [F2], [IDB])
        k.op("dve", lambda: V.memset(SV[:], 0.0), [], [SV])
        k.op("dve", lambda: V.memset(col(c_eps), EPS), [], [SV])
        for g in range(NTT):
            k.op("pool", lambda: P.memset(VA[g][:], 1.0), [], [VA[g]])
        k.op("act", lambda: A.activation(CCF[:], CCF[:], AF.Silu), [CCF], [CCF])
        k.op("act", lambda: A.copy(SCC[:], CCF[:]), [CCF], [SCC])
        CF = F2
        D1, U_, D2, L_, P1, P2 = [F2[:, (i + 1) * 128:(i + 2) * 128] for i in range(6)]
        PC = F2[:, 7 * 128:7 * 128 + 2]

        GW = {"keys": [], "tiles": {}, "ni": 0, "ng": 0}
        WDEPTH = 2
        wcache = {}

        def resolve(key):
            kind = key[0]
            if kind == "mod":
                _, l_, b_ = key
                return w_mod[l_][:, b_ * 256:(b_ + 1) * 256], 8, None
            if kind == "in":
                _, l_, c0 = key
                return w_in[l_][:, c0:c0 + 256], 8, wbf_in[l_][:, c0:c0 + 256]
            if kind == "bo":
                _, l_, br_, q_ = key
                return w_bo[l_, br_][:, q_ * 256:(q_ + 1) * 256], 4, wbf_bo[l_, br_][:, q_ * 256:(q_ + 1) * 256]
            _, l_, q_ = key
            return w_out[l_][:, q_ * 256:(q_ + 1) * 256], 8, wbf_out[l_][:, q_ * 256:(q_ + 1) * 256]

        def gw_issue(key):
            src2d, nk, dst2d = resolve(key)
            wt = nxt("w", W)
            if dst2d is not None and key in wcache:
                k.dma("sp", wt[:, 0:nk, :], dst2d.rearrange("(kc p) c -> p kc c", p=128), wt, reads=[wcache[key]], writes=[wt])
            else:
                k.dma("pool", wt[:, 0:nk, :], src2d.rearrange("(kc p) c -> p kc c", p=128), wt, writes=[wt])
                if dst2d is not None and USE_WCACHE:
                    tobj = T(None, "wc")
                    k.dma("sp", dst2d.rearrange("(kc p) c -> p kc c", p=128), wt[:, 0:nk, :], wt, reads=[wt], writes=[tobj])
                    wcache[key] = tobj
            return wt

        def gw_issue_upto(n_):
            while GW["ni"] < min(n_, len(GW["keys"])):
                GW["tiles"][GW["ni"]] = gw_issue(GW["keys"][GW["ni"]])
                GW["ni"] += 1

        class WStream:
            def __init__(self, specs):
                self.specs = specs
                self.i = 0
                assert GW["keys"][GW["ng"]:GW["ng"] + len(specs)] == list(specs), (specs[:2], GW["keys"][GW["ng"]:GW["ng"] + 2])
                gw_issue_upto(GW["ng"] + WDEPTH)

            def get(self):
                key = self.specs[self.i]
                self.i += 1
                i_ = GW["ng"]
                assert GW["keys"][i_] == key
                gw_issue_upto(i_ + 1 + WDEPTH)
                GW["ng"] += 1
                return GW["tiles"].pop(i_)

        def win_spec(l, c0):
            return ("in", l, c0)

        def setup_specs(l):
            return [("mod", l, b) for b in range(12)]

        def p1_specs(l):
            return [win_spec(l, C_RK), win_spec(l, C_RK + 256), win_spec(l, C_RV), win_spec(l, C_RV + 256),
                    win_spec(l, C_DK), win_spec(l, C_DK + 256), win_spec(l, C_DV), win_spec(l, C_DV + 256)]

        def R_specs(l, states_only):
            specs = [win_spec(l, C_RK), win_spec(l, C_RK + 256), win_spec(l, C_RV), win_spec(l, C_RV + 256)]
            if not states_only:
                specs += [win_spec(l, C_RQ), win_spec(l, C_RQ + 256), win_spec(l, C_RG), win_spec(l, C_RG + 256)]
            return specs

        def D_specs(l):
            return [win_spec(l, C_DQ), win_spec(l, C_DQ + 256), win_spec(l, C_DG), win_spec(l, C_DG + 256)]

        def M_specs(l):
            return [win_spec(l, C_MV), win_spec(l, C_MV + 256), win_spec(l, C_MG), win_spec(l, C_MG + 256),
                    win_spec(l, C_MU), win_spec(l, C_MU + 256)]

        def G_specs(l):
            specs = []
            for br in range(3):
                for q in range(4):
                    specs.append(win_spec(l, C_ML + br * DM + q * 256))
                    specs.append(("bo", l, br, q))
            return specs

        def O_specs(l):
            return [("out", l, q) for q in range(4)]

        def p2_specs(l):
            return R_specs(l, False) + D_specs(l) + M_specs(l) + G_specs(l) + O_specs(l)

        def layer_setup(l):
            ws = WStream(setup_specs(l))
            for b in range(12):
                wt = ws.get()
                pg = nxt("pg", PG)
                for kc in range(8):
                    mm(pg[0:2, 0:256], SCC[:, kc, :], wt[:, kc, :], kc == 0, kc == 7, [SCC, wt], [pg], kc == 7)
                k.dma("sp", F1[0:2, 512:768], b_mod[l, b * 256:(b + 1) * 256].partition_broadcast(2), F1, writes=[F1])
                k.op("dve", lambda: V.tensor_tensor(F1[0:2, 768:1024], pg[0:2, 0:256], F1[0:2, 512:768], ALU.add), [pg, F1], [F1])
                k.dma("sp", mr_d[l][:, b * 256:(b + 1) * 256], F1[0:2, 768:1024], F1, reads=[F1], writes=[MRD[l][b]])
            k.dma("sp", F2[:, 0:898], cf, F2, writes=[F2])
            k.dma("sp", col(c_lg, 8), rdl[l].partition_broadcast(128), SV, writes=[SV])
            k.op("act", lambda: A.activation(col(c_lg, 8), col(c_lg, 8), AF.Sigmoid), [SV], [SV])
            k.op("act", lambda: A.activation(col(c_lg, 8), col(c_lg, 8), AF.Ln), [SV], [SV])
            k.op("act", lambda: A.activation(col(c_cdec, 8), col(c_lg, 8), AF.Exp, scale=128.0), [SV], [SV])
            for h in range(4):
                hs = slice(h * 128, (h + 1) * 128)
                k.op("act", lambda: A.activation(F1[:, 0:128], D1, AF.Exp, scale=col(c_lg + h)), [CF, SV], [F1])
                k.op("act", lambda: A.activation(F1[:, 128:256], D2, AF.Exp, scale=col(c_lg + 4 + h)), [CF, SV], [F1])
                k.op("dve", lambda: V.tensor_tensor(F1[:, 0:128], F1[:, 0:128], U_, ALU.mult), [F1, CF], [F1])
                k.op("dve", lambda: V.tensor_tensor(F1[:, 128:256], F1[:, 128:256], L_, ALU.mult), [F1, CF], [F1])
                k.op("dve", lambda: V.tensor_tensor(F1[:, 0:128], F1[:, 0:128], F1[:, 128:256], ALU.add), [F1], [F1])
                k.op("dve", lambda: V.tensor_scalar_mul(MCT[:, hs], F1[:, 0:128], RET_SCALE), [F1], [MCT])
                k.op("act", lambda: A.activation(QDF[:, hs], P1, AF.Exp, scale=col(c_lg + h)), [CF, SV], [QDF])
                k.op("act", lambda: A.activation(QDB[:, hs], P2, AF.Exp, scale=col(c_lg + 4 + h)), [CF, SV], [QDB])
                k.op("act", lambda: A.activation(col(c_kdec + h), PC[:, 0:1], AF.Exp, scale=col(c_lg + h)), [CF, SV], [SV])
                k.op("act", lambda: A.activation(col(c_kdec + 4 + h), PC[:, 1:2], AF.Exp, scale=col(c_lg + 4 + h)), [CF, SV], [SV])
            k.op("dve", lambda: V.tensor_scalar_mul(col(c_kdec, 8), col(c_kdec, 8), RET_SCALE), [SV], [SV])
            lam_init = 0.8 - 0.6 * math.exp(-0.3 * l)
            LQ = F1[:, 256:384]
            LK = F1[:, 384:512]
            k.dma("sp", LQ, dlq[l].partition_broadcast(128), F1, writes=[F1])
            k.dma("sp", LK, dlk[l].partition_broadcast(128), F1, writes=[F1])
            k.op("dve", lambda: V.tensor_tensor(LQ, LQ, LK, ALU.mult), [F1], [F1])
            k.op("dve", lambda: V.reduce_sum(col(c_e2, 2), LQ.rearrange("p (a b) -> p a b", a=2), AX.X), [F1], [SV])
            k.op("act", lambda: A.activation(col(c_e2, 2), col(c_e2, 2), AF.Exp), [SV], [SV])
            k.op("dve", lambda: V.tensor_tensor(col(c_nlam), col(c_e2 + 1), col(c_e2), ALU.subtract), [SV], [SV])
            k.op("dve", lambda: V.tensor_scalar_add(col(c_nlam), col(c_nlam), -lam_init), [SV], [SV])
            WSN_t = nxt("w", W)
            WSN = WSN_t[:, 0:2, :].rearrange("p k (g j) -> p (k g) j", j=128)
            k.dma("pool", WSN, mws[l].rearrange("g i j -> i g j"), WSN_t, writes=[WSN_t])
            k.dma("sp", col(c_bs, 4), mbs[l], SV, writes=[SV])
            tp = nxt("tp", TP)
            for g in range(4):
                tr(tp[:, g * 128:(g + 1) * 128], WSN[:, g, :], [WSN_t], [tp], g == 3)
            k.op("dve", lambda: V.tensor_copy(WST[:].rearrange("p g j -> p (g j)"), tp[:, 0:512]), [tp], [WST])
            k.op("dve", lambda: V.memset(SF[:], 0.0), [], [SF])
            k.op("dve", lambda: V.memset(TC[:], 0.0), [], [TC])
            k.op("pool", lambda: P.memset(SFB[:], 0.0), [], [SFB])

        def load_mod(l, s):
            st_ = [Y[0], Y[1], Y[2], Y[3], F1]
            k.dma("sp", st_[0][:], mr_d[l][s, 0:DM].partition_broadcast(128), st_[0], reads=MRD[l][0:4], writes=[st_[0]])
            k.dma("sp", st_[1][:], mr_d[l][s, DM:2 * DM].partition_broadcast(128), st_[1], reads=MRD[l][4:8], writes=[st_[1]])
            k.dma("sp", st_[2][:], g_pre[l].partition_broadcast(128), st_[2], writes=[st_[2]])
            k.dma("sp", st_[3][:], mr_d[l][s, 2 * DM:3 * DM].partition_broadcast(128), st_[3], reads=MRD[l][8:12], writes=[st_[3]])
            k.dma("sp", st_[4][:], g_post[l].partition_broadcast(128), st_[4], writes=[st_[4]])
            k.op("act", lambda: A.copy(SH[:], st_[0][:]), [st_[0]], [SH])
            k.op("dve", lambda: V.scalar_tensor_tensor(G1[:], st_[1][:], 1.0, st_[2][:], ALU.add, ALU.mult), [st_[1], st_[2]], [G1])
            k.op("dve", lambda: V.tensor_tensor(GT[:], st_[3][:], st_[4][:], ALU.mult), [st_[3], st_[4]], [GT])

        def rstd_from(ss_ap, n, inv_n, out_ap, t=None):
            t = SV if t is None else t
            k.op("act", lambda: A.activation(out_ap, ss_ap, AF.Sqrt, bias=col(c_eps), scale=inv_n), [t, SV], [t])
            k.op("dve", lambda: V.reciprocal(out_ap, out_ap), [t], [t])

        xseq = []
        xstate = {"pos": 0, "loaded": None}

        def x_load(l, sup):
            xsrc_ = x_in if l == 0 else xs_d
            for i, g in enumerate(sup):
                j = g - NTC
                rd = [XSD[l][j]] if l > 0 else []
                k.dma("sp", Xs[i][:], xsrc_[j * 128:(j + 1) * 128, :], Xs[i], reads=rd, writes=[Xs[i]])

        def make_hT_gen(l, sup, is_ctx, dst, tmps, hbs):
            n = len(sup)
            src = []
            if not is_ctx:
                assert xseq[xstate["pos"]] == (l, sup[0])
                if xstate["loaded"] != xstate["pos"]:
                    x_load(l, sup)
                xstate["pos"] += 1
            for i, g in enumerate(sup):
                src.append(Cx[g] if is_ctx else Xs[i])
            k.op("dve", lambda: V.memset(SVN[:, 0:n], 0.0), [], [SVN])
            for i in range(n):
                Hj = hbs[i % len(hbs)]
                k.op("act", lambda: A.activation(Hj[:], src[i][:], AF.Square, accum_out=SVN[:, i:i + 1]), [src[i]], [Hj, SVN])
            rstd_from(SVN[:, 0:n], n, 1.0 / DM, SVN[:, 8:8 + n], SVN)
            yield
            for i in range(n):
                Fx = tmps[i % len(tmps)]
                Hx = hbs[i % len(hbs)]
                k.op("dve", lambda: V.scalar_tensor_tensor(Fx[:], src[i][:], SVN[:, 8 + i:9 + i], G1[:], ALU.mult, ALU.mult),
                     [src[i], SVN, G1], [Fx])
                k.op("pool", lambda: P.tensor_tensor(Hx[:], Fx[:], SH[:], ALU.add), [Fx, SH], [Hx])
                yield
                tp = nxt("tp", TP)
                for kc in range(8):
                    tr(tp[:, kc * 128:(kc + 1) * 128], Hx[:, kc * 128:(kc + 1) * 128], [Hx], [tp], kc == 7)
                k.op("act", lambda: A.copy(dst[:, :, i * 128:(i + 1) * 128], tp[:].rearrange("p (k t) -> p k t", k=8)),
                     [tp], [dst])
                yield
            if not is_ctx and xstate["pos"] < len(xseq):
                nl, ns0 = xseq[xstate["pos"]]
                nsup = list(range(ns0, ns0 + ST))
                if nl == 0 or all(len(XSD[nl][g - NTC].w) > 0 for g in nsup):
                    x_load(nl, nsup)
                    xstate["loaded"] = xstate["pos"]

        def tick():
            if pend["gen"] is not None:
                try:
                    next(pend["gen"])
                except StopIteration:
                    pend["gen"] = None

        def drain():
            while pend["gen"] is not None:
                tick()

        def make_hT(l, sup, is_ctx, xsrc):
            if (not is_ctx) and pend["pos"] == xstate["pos"] - 1 and pend["dst"] is not None and pend["key"] == (l, sup[0]):
                drain()
            else:
                drain()
                dst = HT[1 - hcur["i"]]
                pend.update(gen=make_hT_gen(l, sup, is_ctx, dst, [F1, F2], [HB, HB2]), dst=dst, key=(l, sup[0]))
                pend["pos"] = xstate["pos"]
                drain()
            hcur["i"] = HT.index(pend["dst"])
            pend["dst"] = None

        def schedule_next_hT(l, tmps):
            if xstate["pos"] >= len(xseq):
                return
            nl, ns0 = xseq[xstate["pos"]]
            if nl != l:
                return
            nsup = list(range(ns0, ns0 + ST))
            dst = HT[1 - hcur["i"]]
            pend.update(gen=make_hT_gen(nl, nsup, False, dst, tmps, [HB, HB2]), dst=dst, key=(nl, ns0))
            pend["pos"] = xstate["pos"]
            tick()

        def load_rope(sup):
            j0 = sup[0] - NTC
            n = len(sup)
            k.dma("pool", RT[:, :, 0:n * 128], ropet[:, :, j0 * 128:(j0 + n) * 128], RT, writes=[RT])

        def proj_fm_group(ws, dst, NT, is_ctx, kind, split=False, dkt_sup=None, hook=None):
            ci, si = (0, 1) if kind == "ret" else (2, 3)
            pm = PMR if kind == "ret" else PMD
            if split:
                k.op("pool", lambda: P.memset(QK[64:128, 0:4, 0:NT], 0.0), [], [QK])
                k.op("pool", lambda: P.memset(QK[0:64, 4:8, 0:NT], 0.0), [], [QK])
            wts = {}

            def proj(hc):
                wi, c = divmod(hc, 2)
                if c == 0:
                    wts[wi] = ws.get()
                wt = wts[wi]
                tick()
                hT = HT[hcur["i"]]
                pz = PG[hc % 2]
                for kc in range(8):
                    mm(pz[:, 0:NT], wt[:, kc, c * 128:(c + 1) * 128], hT[:, kc, 0:NT], kc == 0, kc == 7, [wt, hT], [pz], kc == 7)
                return pz

            def finish(hc, pz):
                outs = [(slice(0, 128), dst + hc)] if not split else [(slice(0, 64), hc), (slice(64, 128), 4 + hc)]
                if is_ctx:
                    if dkt_sup is None:
                        for rs_, oc in outs:
                            k.op("act", lambda: A.copy(QK[rs_, oc, 0:NT], pz[rs_, 0:NT]), [pz], [QK])
                    else:
                        for i, g in enumerate(dkt_sup):
                            k.op("act", lambda: A.copy(DKT[g][:, hc, :], pz[:, i * 128:(i + 1) * 128]), [pz], [DKT[g]])
                    return
                zb = nxt("zb", ZB)
                k.op("act", lambda: A.copy(zb[:, 0:NT], pz[:, 0:NT]), [pz], [zb])
                pr = PG[2]
                mm(pr[:, 0:NT], pm[:], zb[:, 0:NT], True, True, [pm, zb], [pr], True)
                Fa, Fb = (F1, F2) if hc % 2 == 0 else (Y[2], Y[3])
                k.op("dve", lambda: V.tensor_tensor(Fa[:, 0:NT], pz[:, 0:NT], RT[:, ci, 0:NT], ALU.mult), [pz, RT], [Fa])
                k.op("dve", lambda: V.tensor_tensor(Fb[:, 0:NT], pr[:, 0:NT], RT[:, si, 0:NT], ALU.mult), [pr, RT], [Fb])
                if dkt_sup is None:
                    for rs_, oc in outs:
                        k.op("pool", lambda: P.tensor_tensor(QK[rs_, oc, 0:NT], Fa[rs_, 0:NT], Fb[rs_, 0:NT], ALU.add), [Fa, Fb], [QK])
                else:
                    for i, g in enumerate(dkt_sup):
                        ts = slice(i * 128, (i + 1) * 128)
                        k.op("pool", lambda: P.tensor_tensor(DKT[g][:, hc, :], Fa[:, ts], Fb[:, ts], ALU.add), [Fa, Fb], [DKT[g]])

            pz_prev = proj(0)
            if hook is not None:
                hook(0)
            for hc in range(1, 4):
                pz_cur = proj(hc)
                if hook is not None:
                    hook(hc)
                finish(hc - 1, pz_prev)
                pz_prev = pz_cur
            finish(3, pz_prev)

        dst_t = [None]
        VN = KTOK
        DFO = Y
        yT = QK

        def proj_tm(wt, i, nk=8, src=None):
            pg = nxt("pgx", PGX)
            s = HT[hcur["i"]] if src is None else src
            for kc in range(nk):
                mm(pg[:, 0:256], s[:, kc, i * 128:(i + 1) * 128], wt[:, kc, :], kc == 0, kc == nk - 1, [wt, s], [pg], kc == nk - 1)
            return pg

        def k_side(l, ws, sup, is_ctx, dirn):
            n = len(sup)
            NT = n * 128
            proj_fm_group(ws, 4, NT, is_ctx, "ret")
            for i in range(n):
                tp = nxt("tp", TP)
                for h in range(4):
                    tr(tp[:, h * 128:(h + 1) * 128], QK[:, 4 + h, i * 128:(i + 1) * 128], [QK], [tp], h == 3)
                k.op("dve", lambda: V.tensor_tensor(KTOK[i][:].rearrange("p (h d) -> p h d", h=4), tp[:, 0:512].rearrange("p (h d) -> p h d", h=4),
                                                    col(c_kdec + 4 * dirn, 4).unsqueeze(2).to_broadcast([128, 4, 128]), ALU.mult), [tp, SV], [KTOK[i]])
            for wi in range(2):
                wt = ws.get()
                for i in range(n):
                    pg = proj_tm(wt, i)
                    k.op("act", lambda: A.copy(RV[i][:, wi * 256:(wi + 1) * 256], pg[:, 0:256]), [pg], [RV[i]])

        def kv_update(i, state, dirn):
            for h in range(4):
                hs = slice(h * 128, (h + 1) * 128)
                mm(KVP[:, hs], KTOK[i][:, hs], RV[i][:, hs], True, True, [KTOK[i], RV[i]], [KVP], h == 3)
            for h in range(4):
                hs = slice(h * 128, (h + 1) * 128)
                k.op("dve", lambda: V.scalar_tensor_tensor(state[:, hs], state[:, hs], col(c_cdec + 4 * dirn + h), KVP[:, hs], ALU.mult, ALU.add),
                     [state, SV, KVP], [state])

        def headnorm_batch(items, const, dstT, do_tr=True):
            nI = len(items)
            for j, (i, src_ap, src_t, gate_t) in enumerate(items):
                half = F2[:, (j % 2) * 512:(j % 2 + 1) * 512]
                k.op("act", lambda: A.activation(half, src_ap, AF.Square), [src_t], [F2])
                k.op("dve", lambda: V.reduce_sum(SVH[:, 4 * j:4 * j + 4], half.rearrange("p (h e) -> p h e", h=4), AX.X), [F2], [SVH])
            k.op("act", lambda: A.activation(SVH[:, 0:4 * nI], SVH[:, 0:4 * nI], AF.Sqrt, bias=col(c_eps), scale=1.0 / 128), [SVH, SV], [SVH])
            k.op("dve", lambda: V.reciprocal(SVH[:, 0:4 * nI], SVH[:, 0:4 * nI]), [SVH], [SVH])
            for j, (i, src_ap, src_t, gate_t) in enumerate(items):
                ob = OBQ[i % 2]
                for h in range(4):
                    hs = slice(h * 128, (h + 1) * 128)
                    if const == 1.0:
                        k.op("dve", lambda: V.scalar_tensor_tensor(ob[h][:], src_ap[:, hs], SVH[:, 4 * j + h:4 * j + h + 1], gate_t[:, hs],
                                                                   ALU.mult, ALU.mult), [src_t, SVH, gate_t], [ob[h]])
                    else:
                        k.op("dve", lambda: V.tensor_scalar(F1[:, 512 + h * 128:512 + (h + 1) * 128], src_ap[:, hs], SVH[:, 4 * j + h:4 * j + h + 1], const,
                                                            ALU.mult, ALU.mult), [src_t, SVH], [F1])
                if const != 1.0:
                    for h in range(4):
                        k.op("pool", lambda: P.tensor_tensor(ob[h][:], F1[:, 512 + h * 128:512 + (h + 1) * 128], gate_t[:, h * 128:(h + 1) * 128], ALU.mult),
                             [F1, gate_t], [ob[h]])
                if do_tr:
                    transpose_into(ob, 4, dstT, i)

        def transpose_into(src_ts, nchunks, dstT, i):
            tp = nxt("tp", TP)
            for c in range(nchunks):
                tr(tp[:, c * 128:(c + 1) * 128], src_ts[c][:], [src_ts[c]], [tp], c == nchunks - 1)
            k.op("act", lambda: A.copy(dstT[:, :, i * 128:(i + 1) * 128], tp[:, 0:nchunks * 128].rearrange("p (c t) -> p c t", c=nchunks)),
                 [tp], [dstT])

        def pass1(l, sup, is_ctx, xsrc):
            n = len(sup)
            NT = n * 128
            make_hT(l, sup, is_ctx, xsrc)
            if not is_ctx:
                load_rope(sup)
                schedule_next_hT(l, [Y[0], Y[1]])
            ws = WStream(p1_specs(l))
            k_side(l, ws, sup, is_ctx, 1)
            order = list(reversed(range(n)))

            def chain_step(j):
                if j < n:
                    i = order[j]
                    g = sup[i]
                    k.op("act", lambda: A.copy(TBS[g][:], TC[:]), [TC], [TBS[g]])
                    kv_update(i, TC, 1)

            proj_fm_group(ws, 0, NT, is_ctx, "diff", dkt_sup=sup, hook=chain_step)
            for wi in range(2):
                wt = ws.get()
                for i, g in enumerate(sup):
                    pg = proj_tm(wt, i)
                    k.op("act", lambda: A.copy(VA[g][:, 2 * wi:2 * wi + 2, 0:128], pg[:, 0:256].rearrange("p (h e) -> p h e", h=2)),
                         [pg], [VA[g]])

        def stage_R(l, sup, is_ctx, states_only):
            n = len(sup)
            NT = n * 128
            ws = WStream(R_specs(l, states_only))
            k_side(l, ws, sup, is_ctx, 0)
            if not states_only:
                proj_fm_group(ws, 0, NT, is_ctx, "ret")
                for wi in range(2):
                    wt = ws.get()
                    for i in range(n):
                        pg = proj_tm(wt, i)
                        k.op("act", lambda: A.activation(SG_[i][:, wi * 256:(wi + 1) * 256], pg[:, 0:256], AF.Silu), [pg], [SG_[i]])
            def prep(i):
                ts = slice(i * 128, (i + 1) * 128)
                ps = nxt("pg", PG)
                for h in range(4):
                    hs = slice(h * 128, (h + 1) * 128)
                    mm(ps[:, hs], QK[:, 4 + h, ts], QK[:, h, ts], True, True, [QK], [ps], h == 3)
                SM, QF, QB = SMs[i % 2], QFs[i % 2], QBs[i % 2]
                k.op("dve", lambda: V.tensor_tensor(SM[:], ps[:], MCT[:], ALU.mult), [ps, MCT], [SM])
                k.op("pool", lambda: P.tensor_tensor(QF[:].rearrange("p (h t) -> p h t", h=4), QK[:, 0:4, ts],
                                                     QDF[:].rearrange("p (h t) -> p h t", h=4), ALU.mult), [QK, QDF], [QF])
                k.op("pool", lambda: P.tensor_tensor(QB[:].rearrange("p (h t) -> p h t", h=4), QK[:, 0:4, ts],
                                                     QDB[:].rearrange("p (h t) -> p h t", h=4), ALU.mult), [QK, QDB], [QB])

            if not states_only:
                prep(0)
            for i, g in enumerate(sup):
                if not states_only and i + 1 < n:
                    prep(i + 1)
                kv_update(i, SF, 0)
                if not states_only:
                    SM, QF, QB = SMs[i % 2], QFs[i % 2], QBs[i % 2]
                    po = nxt("pg", PG)
                    for h in range(4):
                        hs = slice(h * 128, (h + 1) * 128)
                        mm(po[:, hs], SM[:, hs], RV[i][:, hs], True, False, [SM, RV[i]], [po], False)
                        mm(po[:, hs], QF[:, hs], SFB[:, hs], False, False, [QF, SFB], [po], False)
                        mm(po[:, hs], QB[:, hs], TBS[g][:, hs], False, True, [QB, TBS[g]], [po], h == 3)
                k.op("act", lambda: A.copy(SFB[:], SF[:]), [SF], [SFB])
                if not states_only:
                    headnorm_batch([(i, po[:], po, SG_[i])], 1.0, OT[0], do_tr=False)
                    if i > 0:
                        transpose_into(OBQ[(i - 1) % 2], 4, OT[0], i - 1)
            if not states_only:
                transpose_into(OBQ[(n - 1) % 2], 4, OT[0], n - 1)

        def stage_D(l, sup, is_ctx, before_tail=None):
            n = len(sup)
            NT = n * 128
            lam_init = 0.8 - 0.6 * math.exp(-0.3 * l)
            krange = list(range(NTC)) if is_ctx else list(range(NTT))
            ws = WStream(D_specs(l))
            proj_fm_group(ws, 0, NT, is_ctx, "diff", split=True)
            for wi in range(2):
                wt = ws.get()
                for i in range(n):
                    pg = proj_tm(wt, i)
                    k.op("act", lambda: A.activation(SG_[i][:, wi * 256:(wi + 1) * 256], pg[:, 0:256], AF.Silu), [pg], [SG_[i]])
            if not is_ctx:
                schedule_next_hT(l, [F2])
            steps = [(h, m, ki, kt) for h in range(4) for m in range(2) for ki, kt in enumerate(krange)]
            ps_of = {}
            LOOK = 2

            def emit_st(s_):
                h_, m_, ki_, kt_ = steps[s_]
                ps_ = nxt("pg", PG)
                mm(ps_[:, 0:NT], DKT[kt_][:, h_, :], QK[:, 4 * m_ + h_, 0:NT], True, True, [DKT[kt_], QK], [ps_], True)
                ps_of[s_] = ps_

            for s_ in range(min(LOOK, len(steps))):
                emit_st(s_)
            for s_ in range(len(steps)):
                h, m, ki, kt = steps[s_]
                ps = ps_of.pop(s_)
                e = nxt("e", E_)
                k.op("act", lambda: A.activation(e[:, 0:NT], ps[:, 0:NT], AF.Exp, scale=0.125), [ps], [e])
                if s_ + LOOK < len(steps):
                    emit_st(s_ + LOOK)
                if s_ % 8 == 7:
                    tick()
                for i in range(n):
                    oa = OA[i // 2]
                    o0 = (i % 2) * 129
                    mm(oa[:, o0:o0 + 129], e[:, i * 128:(i + 1) * 128], VA[kt][:, h, 0:129], ki == 0 and i % 2 == 0,
                       ki == len(krange) - 1, [e, VA[kt]], [oa], i == n - 1, sgc=True)
                if ki != len(krange) - 1:
                    continue
                nb = (n + 1) // 2
                for b_ in range(nb):
                    stg = Y[2 * m + b_]
                    k.op("dve", lambda: V.tensor_copy(stg[:, 512:512 + 258], OA[b_][:, 0:258]), [OA[b_]], [stg])
                if m == 0:
                    continue
                for b_ in range(nb):
                    nb2 = min(2, n - 2 * b_)
                    sv0 = Y[b_][:, 512:512 + 258].rearrange("p (t c) -> p t c", c=129)[:, 0:nb2, 128]
                    sv1 = Y[2 + b_][:, 512:512 + 258].rearrange("p (t c) -> p t c", c=129)[:, 0:nb2, 128]
                    k.op("dve", lambda: V.reciprocal(R0[:, 2 * b_:2 * b_ + nb2], sv0), [Y[b_]], [R0])
                    k.op("dve", lambda: V.reciprocal(R1[:, 2 * b_:2 * b_ + nb2], sv1), [Y[2 + b_]], [R1])
                k.op("dve", lambda: V.tensor_scalar_mul(R1[:, 0:n], R1[:, 0:n], col(c_nlam)), [R1, SV], [R1])
                for i in range(n):
                    s0 = Y[i // 2]
                    s1 = Y[2 + i // 2]
                    o0 = 512 + (i % 2) * 129
                    tt = SGM[i % 2]
                    ttv = tt[:].bitcast(F32)
                    k.op("dve", lambda: V.tensor_scalar_mul(ttv, s0[:, o0:o0 + 128], R0[:, i:i + 1]), [s0, R0], [tt])
                    k.op("dve", lambda: V.scalar_tensor_tensor(DFO[i][:, h * 128:(h + 1) * 128], s1[:, o0:o0 + 128], R1[:, i:i + 1], ttv,
                                                               ALU.mult, ALU.add), [s1, R1, tt], [DFO[i]])
            drain()
            if before_tail is not None:
                before_tail()
            headnorm_batch([(i, DFO[i][:, 0:512], DFO[i], SG_[i]) for i in range(n)], 1.0 - lam_init, OT[2])

        def stage_M_begin(l, sup):
            n = len(sup)
            ws = WStream(M_specs(l))
            k.op("dve", lambda: V.memset(SVMa[:], 0.0), [], [SVMa])
            k.op("dve", lambda: V.memset(SVMb[:], 0.0), [], [SVMb])
            for wi in range(2):
                wt = ws.get()
                for i in range(n):
                    pg = proj_tm(wt, i)
                    k.op("act", lambda: A.activation(Y[i][:, 512 + wi * 256:512 + (wi + 1) * 256], pg[:, 0:256], AF.Copy,
                                                     accum_out=SVMa[:, i * 2 + wi:i * 2 + wi + 1]), [pg], [Y[i], SVMa])
                    k.op("act", lambda: A.activation(VN[i][:, wi * 256:(wi + 1) * 256], pg[:, 0:256], AF.Square,
                                                     accum_out=SVMb[:, i * 2 + wi:i * 2 + wi + 1]), [pg], [VN[i], SVMb])
            return lambda: stage_M_rest(l, sup, ws)

        def stage_M_rest(l, sup, ws):
            n = len(sup)
            k.op("dve", lambda: V.reduce_sum(SVM[:, 16:16 + n], SVMa[:, 0:2 * n].rearrange("p (i w) -> p i w", w=2), AX.X), [SVMa], [SVM])
            k.op("dve", lambda: V.reduce_sum(SVM[:, 20:20 + n], SVMb[:, 0:2 * n].rearrange("p (i w) -> p i w", w=2), AX.X), [SVMb], [SVM])
            k.op("dve", lambda: V.tensor_scalar_mul(SVM[:, 16:16 + n], SVM[:, 16:16 + n], 1.0 / 512), [SVM], [SVM])
            k.op("dve", lambda: V.tensor_tensor(SVM[:, 24:24 + n], SVM[:, 16:16 + n], SVM[:, 16:16 + n], ALU.mult), [SVM], [SVM])
            k.op("dve", lambda: V.scalar_tensor_tensor(SVM[:, 20:20 + n], SVM[:, 20:20 + n], 1.0 / 512, SVM[:, 24:24 + n], ALU.mult, ALU.subtract), [SVM], [SVM])
            k.op("act", lambda: A.activation(SVM[:, 20:20 + n], SVM[:, 20:20 + n], AF.Sqrt, bias=col(c_eps), scale=1.0), [SVM, SV], [SVM])
            k.op("dve", lambda: V.reciprocal(SVM[:, 20:20 + n], SVM[:, 20:20 + n]), [SVM], [SVM])
            k.op("dve", lambda: V.scalar_tensor_tensor(SVM[:, 24:24 + n], SVM[:, 16:16 + n], -1.0, SVM[:, 20:20 + n], ALU.mult, ALU.mult), [SVM], [SVM])
            for i in range(n):
                k.op("pool", lambda: P.tensor_scalar(VN[i][:], Y[i][:, 512:1024], SVM[:, 20 + i:21 + i], SVM[:, 24 + i:25 + i], ALU.mult, ALU.add),
                     [Y[i], SVM], [VN[i]])
            for wi in range(2):
                wt = ws.get()
                for i in range(n):
                    pg = proj_tm(wt, i)
                    k.op("act", lambda: A.activation(SG_[i][:, wi * 256:(wi + 1) * 256], pg[:, 0:256], AF.Silu), [pg], [SG_[i]])
            for wi in range(2):
                wt = ws.get()
                for i in range(n):
                    pg = proj_tm(wt, i)
                    cs = slice(wi * 256, (wi + 1) * 256)
                    k.op("dve", lambda: V.tensor_tensor(SG_[i][:, cs], pg[:, 0:256], SG_[i][:, cs], ALU.mult), [pg, SG_[i]], [SG_[i]])
            for i in range(n):
                pg = nxt("pg", PG)
                for g in range(4):
                    gs = slice(g * 128, (g + 1) * 128)
                    mm(pg[:, gs], WST[:, g, :], VN[i][:, gs], True, True, [WST, VN[i]], [pg], g == 3)
                for g in range(4):
                    gs = slice(g * 128, (g + 1) * 128)
                    k.op("dve", lambda: V.scalar_tensor_tensor(OBQ[i % 2][g][:], pg[:, gs], col(c_bs + g), SG_[i][:, gs], ALU.add, ALU.mult),
                         [pg, SV, SG_[i]], [OBQ[i % 2][g]])
                transpose_into(OBQ[i % 2], 4, OT[1], i)

        def stage_G(l, sup):
            n = len(sup)
            ws = WStream(G_specs(l))
            for br in range(3):
                for q in range(4):
                    wl = ws.get()
                    wb = ws.get()
                    qs = slice(q * 256, (q + 1) * 256)
                    for i in range(n):
                        pa = proj_tm(wl, i)
                        sg = nxt("sgm", SGM)
                        k.op("act", lambda: A.activation(sg[:], pa[:, 0:256], AF.Sigmoid), [pa], [sg])
                        pb = proj_tm(wb, i, nk=4, src=OT[br])
                        if br == 0:
                            k.op("dve", lambda: V.tensor_tensor(Y[i][:, qs], pb[:, 0:256], sg[:], ALU.mult), [pb, sg], [Y[i]])
                        else:
                            ft = F1 if i % 2 == 0 else F2
                            k.op("dve", lambda: V.tensor_tensor(ft[:, 0:256], pb[:, 0:256], sg[:], ALU.mult), [pb, sg], [ft])
                            k.op("pool", lambda: P.tensor_tensor(Y[i][:, qs], Y[i][:, qs], ft[:, 0:256], ALU.add), [ft, Y[i]], [Y[i]])
            for i in range(n):
                k.op("act", lambda: A.copy(HB[:], Y[i][:]), [Y[i]], [HB])
                tp = nxt("tp", TP)
                for kc in range(8):
                    tr(tp[:, kc * 128:(kc + 1) * 128], HB[:, kc * 128:(kc + 1) * 128], [HB], [tp], kc == 7)
                k.op("act", lambda: A.copy(yT[:, :, i * 128:(i + 1) * 128], tp[:].rearrange("p (k t) -> p k t", k=8)), [tp], [yT])

        def stage_O(l, sup, is_ctx, xdst, last):
            n = len(sup)
            xsrc_ = x_in if l == 0 else xs_d
            FX = [F1, F2]

            def xres_load(i):
                j = sup[i] - NTC
                rd = [XSD[l][j]] if l > 0 else []
                k.dma("pool", FX[i % 2][:], xsrc_[j * 128:(j + 1) * 128, :], FX[i % 2], reads=rd, writes=[FX[i % 2]])

            if not is_ctx:
                for i in range(min(2, n)):
                    xres_load(i)
            ws = WStream(O_specs(l))
            k.op("dve", lambda: V.memset(SVN[:, 16:32], 0.0), [], [SVN])
            for q in range(4):
                wo = ws.get()
                qs = slice(q * 256, (q + 1) * 256)
                for i in range(n):
                    pg = proj_tm(wo, i, src=yT)
                    k.op("act", lambda: A.activation(Y[i][:, qs], pg[:, 0:256], AF.Copy), [pg], [Y[i]])
                    k.op("act", lambda: A.activation(SG_[i][:, 0:256], pg[:, 0:256], AF.Square, accum_out=SVN[:, 16 + i * 4 + q:17 + i * 4 + q]),
                         [pg], [SG_[i], SVN])
            k.op("dve", lambda: V.reduce_sum(SVN[:, 0:n], SVN[:, 16:16 + 4 * n].rearrange("p (i q) -> p i q", q=4), AX.X), [SVN], [SVN])
            rstd_from(SVN[:, 0:n], n, 1.0 / DM, SVN[:, 8:8 + n], SVN)
            for i, g in enumerate(sup):
                k.op("dve", lambda: V.scalar_tensor_tensor(Y[i][:], Y[i][:], SVN[:, 8 + i:9 + i], GT[:], ALU.mult, ALU.mult), [Y[i], SVN, GT], [Y[i]])
                if is_ctx:
                    k.op("pool", lambda: P.tensor_tensor(Cx[g][:], Cx[g][:], Y[i][:], ALU.add), [Cx[g], Y[i]], [Cx[g]])
                else:
                    fx = FX[i % 2]
                    k.op("pool", lambda: P.tensor_tensor(Y[i][:], Y[i][:], fx[:], ALU.add), [Y[i], fx], [Y[i]])
                    if i + 2 < n:
                        xres_load(i + 2)
                    j = g - NTC
                    if last:
                        k.dma("pool", xdst[j * 128:(j + 1) * 128, :], Y[i][:], Y[i], reads=[Y[i]], is_out=True)
                    else:
                        k.dma("pool", xdst[j * 128:(j + 1) * 128, :], Y[i][:], Y[i], reads=[Y[i]], writes=[XSD[l + 1][j]])

        def pass2(l, sup, is_ctx, xsrc, xdst, last):
            n = len(sup)
            dg = debug and l == 0 and (is_ctx or sup[0] == NTC)
            tag = "c_" if is_ctx else "l_"
            make_hT(l, sup, is_ctx, xsrc)
            if dg:
                dbg(tag + "hT", HT[hcur["i"]][:, :, 0:n * 128], HT[hcur["i"]], [128, 8, n * 128])
            if not is_ctx:
                load_rope(sup)
            stage_R(l, sup, is_ctx, False)
            if dg:
                dbg(tag + "qk_r", QK[:, :, 0:n * 128], QK, [128, 8, n * 128])
                dbg(tag + "oT_ret", OT[0][:, :, 0:n * 128], OT[0], [128, 4, n * 128])
                dbg(tag + "sf", SF[:], SF, [128, 512])
            mrest = {}
            stage_D(l, sup, is_ctx, before_tail=lambda: mrest.update(f=stage_M_begin(l, sup)))
            if dg:
                dbg(tag + "qk_d", QK[:, :, 0:n * 128], QK, [128, 8, n * 128])
                dbg(tag + "oT_diff", OT[2][:, :, 0:n * 128], OT[2], [128, 4, n * 128])
                dbg(tag + "dfo", Y[0][:, 0:512], Y[0], [128, 512])
            mrest["f"]()
            if dg:
                dbg(tag + "oT_mlp", OT[1][:, :, 0:n * 128], OT[1], [128, 4, n * 128])
            stage_G(l, sup)
            if dg:
                dbg(tag + "y", Y[0][:], Y[0], [128, DM])
            stage_O(l, sup, is_ctx, xdst, last)

        csup = list(range(NTC))
        lsups = [list(range(NTC + s * ST, NTC + (s + 1) * ST)) for s in range(NTL // ST)]
        for l in range(depth):
            xseq.extend([(l, sp_[0]) for sp_ in reversed(lsups)])
            xseq.extend([(l, sp_[0]) for sp_ in lsups])
        for l in range(depth):
            last_ = l == depth - 1
            GW["keys"] += setup_specs(l) + p1_specs(l)
            GW["keys"] += R_specs(l, True) if (last_ and depth == DEPTH) else p2_specs(l)
            for _ in lsups:
                GW["keys"] += p1_specs(l)
            for _ in lsups:
                GW["keys"] += p2_specs(l)
        for l in range(depth):
            last = l == depth - 1
            xsrc = x_in if l == 0 else xs_d
            xdst = out_d if last else xs_d
            layer_setup(l)
            load_mod(l, 1)
            pass1(l, csup, True, None)
            if last and depth == DEPTH:
                stage_R(l, csup, True, True)
            else:
                pass2(l, csup, True, None, None, False)
            if debug and l == 0:
                dbg("tc_ctx", TC[:], TC, [128, 512])
                dbg("dkt0", DKT[0][:], DKT[0], [128, 4, 128])
                dbg("va0", VA[0][:], VA[0], [128, 4, 130])
                dbg("mct", MCT[:], MCT, [128, 512])
                dbg("sv", SV[:], SV, [128, 128])
            load_mod(l, 0)
            for sup in reversed(lsups):
                pass1(l, sup, False, xsrc)
            for sup in lsups:
                pass2(l, sup, False, xsrc, xdst, last)
        assert GW["ng"] == len(GW["keys"]), (GW["ng"], len(GW["keys"]))
        k.finish()
        print("instr counts", k.nins, "sems", len(k.sem), "sbuf KiB/partition", k.sb_bytes / 1024)
    nc._dbg_names = dbg_names
    return nc


def _host_consts():
    j = np.arange(128, dtype=np.float64)[:, None]
    i = np.arange(128, dtype=np.float64)[None, :]
    ident = np.eye(128)
    d1 = np.maximum(i - j, 0.0)
    u = (i >= j).astype(np.float64)
    d2 = np.maximum(j - i, 0.0)
    lo = (j >= i).astype(np.float64)
    p1 = np.broadcast_to(i + 1.0, (128, 128))
    p2 = np.broadcast_to(128.0 - i, (128, 128))
    pc = np.concatenate([127.0 - j, j], axis=1)
    cf = np.concatenate([ident, d1, u, d2, lo, p1, p2, pc], axis=1).astype(np.float32)
    sel = np.zeros((2, 2, 128), np.float32)
    sel[0, 0] = 1.0
    sel[1, 1] = 1.0
    n = np.arange(SEQ)
    row = (n // 64).astype(np.float64)
    colp = (n % 64).astype(np.float64)

    def ang(head_dim):
        nf = head_dim // 4
        inv = 10000.0 ** (-np.arange(nf, dtype=np.float64) / nf)
        return np.concatenate([row[:, None] * inv, colp[:, None] * inv], axis=-1)

    a_r = ang(128)
    a_d = ang(64)
    rt = np.zeros((128, 4, SEQ), np.float32)
    d = np.arange(128)
    rt[:, 0, :] = np.cos(a_r)[:, d % 64].T
    rt[:, 1, :] = (np.sin(a_r)[:, d % 64] * np.where(d < 64, -1.0, 1.0)[None, :]).T
    dd = d % 64
    rt[:, 2, :] = np.cos(a_d)[:, dd % 32].T
    rt[:, 3, :] = (np.sin(a_d)[:, dd % 32] * np.where(dd < 32, -1.0, 1.0)[None, :]).T
    pm = np.zeros((128, 256), np.float32)
    d = np.arange(128)
    pm[d, (d + 64) % 128] = 1.0
    pm[d, 128 + (d // 64) * 64 + ((d % 64) + 32) % 64] = 1.0
    return cf, sel, rt, pm


_NC_CACHE = {}


_DBG = {}


def kernel(x, c, ctx, c_ctx, w_mod, b_mod, g_pre, g_post, w_in, ret_decay_logit, mlp_w_s, mlp_b_s,
           diff_lambda_q, diff_lambda_k, w_branch_out, w_out, _cores=8, _depth=DEPTH, _debug=False):
    f = lambda a: np.ascontiguousarray(np.asarray(a, dtype=np.float32))
    x, c, ctx, c_ctx = f(x), f(c), f(ctx), f(c_ctx)
    cf, sel, rt, pm = _host_consts()
    shared = {
        "w_mod": f(w_mod), "b_mod": f(b_mod), "g_pre": f(g_pre), "g_post": f(g_post), "w_in": f(w_in),
        "rdl": f(ret_decay_logit).reshape(DEPTH, 8), "mws": f(mlp_w_s),
        "mbs": np.ascontiguousarray(f(mlp_b_s).transpose(0, 2, 1)),
        "dlq": f(diff_lambda_q).reshape(DEPTH, 128), "dlk": f(diff_lambda_k).reshape(DEPTH, 128),
        "w_bo": f(w_branch_out), "w_out": f(w_out), "cf": cf, "ropet": rt, "pm": pm,
    }
    key = (_depth, _debug)
    if key not in _NC_CACHE:
        _NC_CACHE[key] = build_program(_depth, _debug)
    nc = _NC_CACHE[key]
    in_maps = []
    for b in range(_cores):
        cc = np.stack([c[b], c_ctx], axis=0)
        cct = np.ascontiguousarray(cc.reshape(2, 8, 128).transpose(2, 1, 0))
        m = dict(shared)
        m.update({"x": x[b], "ctx": ctx[b], "cct": cct})
        in_maps.append(m)
    res = run_bass_kernel_spmd(nc, in_maps, core_ids=list(range(_cores)))
    if _debug:
        for nm in nc._dbg_names:
            _DBG[nm] = np.asarray(res.results[0]["dbg_" + nm])
    return np.stack([np.asarray(r["out"], dtype=np.float32) for r in res.results], axis=0)
```

```python
import math
from contextlib import ExitStack

import numpy as np
import concourse.bass as bass
import concourse.mybir as mybir
from concourse.bass_utils import run_bass_kernel_spmd

F32 = mybir.dt.float32
BF16 = mybir.dt.bfloat16
AF = mybir.ActivationFunctionType
ALU = mybir.AluOpType
AX = mybir.AxisListType

DEPTH = 2
DM = 1024
SEQ = 2048
CTX = 256
NTC = CTX // 128
NTL = SEQ // 128
NTT = NTC + NTL
ST = 4
IN_COLS = 8704
EPS = 1e-6
RET_SCALE = 128.0 ** -0.5
USE_WCACHE = True
C_RK, C_RV, C_DK, C_DV, C_RQ, C_RG, C_DQ, C_DG, C_MU, C_MV, C_MG, C_ML = [512 * i for i in range(12)]


class T:
    def __init__(self, h, name):
        self.h = h
        self.name = name
        self.w = {}
        self.r = {}
        self.dsem = None
        self.excl = False

    def __getitem__(self, k):
        return self.h[k]


class KB:
    def __init__(self, nc, stack):
        self.nc = nc
        self.stack = stack
        self.eng = {"pe": nc.tensor, "dve": nc.vector, "act": nc.scalar, "pool": nc.gpsimd, "sp": nc.sync}
        self.sem = {}
        self.cnt = {}
        for e in self.eng:
            self.sem[e] = stack.enter_context(nc.semaphore("s_" + e))
            self.cnt[e] = 0
        self.seen = {e: {} for e in self.eng}
        self.out_events = []
        self.nins = {e: 0 for e in self.eng}

    def sb(self, name, shape, dt):
        self.sb_bytes = getattr(self, "sb_bytes", 0) + int(np.prod(shape[1:])) * (2 if dt == BF16 else 4)
        return T(self.stack.enter_context(self.nc.sbuf_tensor(name, list(shape), dt)), name)

    def ps(self, name, shape, dt=F32):
        t = T(self.stack.enter_context(self.nc.psum_tensor(name, list(shape), dt)), name)
        t.excl = True
        return t

    def _wait(self, e, reads, writes):
        need = {}

        def add(k, v, raw):
            if k == e and e == "pe":
                return
            need[k] = max(need.get(k, 0), v)

        for t in reads:
            for k, v in t.w.items():
                add(k, v, True)
            if t.excl:
                for k, v in t.r.items():
                    if k != e:
                        add(k, v, False)
        for t in writes:
            for k, v in t.w.items():
                add(k, v, False)
            for k, v in t.r.items():
                add(k, v, False)
        for k, v in need.items():
            if k not in self.eng:
                v = self.cnt[k]
            if self.seen[e].get(k, 0) < v:
                assert v <= self.cnt[k], f"wait on unsignaled event {k} {v} {self.cnt[k]}"
                self.eng[e].wait_ge(self.sem[k], v)
                self.seen[e][k] = v

    def _post(self, ev, reads, writes):
        for t in reads:
            t.r[ev[0]] = max(t.r.get(ev[0], 0), ev[1])
        for t in writes:
            t.w[ev[0]] = max(t.w.get(ev[0], 0), ev[1])
            t.r = {}

    def op(self, e, fn, reads=(), writes=(), signal=True):
        self._wait(e, reads, writes)
        ins = fn()
        self.nins[e] += 1
        if signal:
            self.cnt[e] += 1
            ins.then_inc(self.sem[e], 1)
            ev = (e, self.cnt[e])
        else:
            ev = (e, self.cnt[e] + 1)
        self._post(ev, reads, writes)
        return ins

    def dma(self, q, out_ap, in_ap, sb_t, reads=(), writes=(), is_out=False):
        self._wait(q, reads, writes)
        if sb_t.dsem is None:
            sb_t.dsem = {}
        kind = "sw" if q == "pool" else "hw"
        if kind not in sb_t.dsem:
            nm = "d_" + sb_t.name + "_" + kind
            sb_t.dsem[kind] = nm
            self.sem[nm] = self.stack.enter_context(self.nc.semaphore(nm))
            self.cnt[nm] = 0
        k = sb_t.dsem[kind]
        ins = self.eng[q].dma_start(out=out_ap, in_=in_ap)
        self.cnt[k] += 16
        ins.then_inc(self.sem[k], 16)
        ev = (k, self.cnt[k])
        self._post(ev, reads, writes)
        if is_out:
            self.out_events.append(ev)
        return ins

    def finish(self):
        fin = {}
        for k, v in self.out_events:
            fin[k] = self.cnt[k]
        for k, v in fin.items():
            self.eng["sp"].wait_ge(self.sem[k], v)


def build_program(depth=DEPTH, debug=False):
    nc = bass.Bass("TRN2", target_bir_lowering=False)

    def din(name, shape):
        return nc.dram_tensor(name, list(shape), F32, kind="ExternalInput").ap()

    x_in = din("x", [SEQ, DM])
    ctx_in = din("ctx", [CTX, DM])
    cct = din("cct", [128, 8, 2])
    w_mod = din("w_mod", [DEPTH, DM, 3 * DM])
    b_mod = din("b_mod", [DEPTH, 3 * DM])
    g_pre = din("g_pre", [DEPTH, DM])
    g_post = din("g_post", [DEPTH, DM])
    w_in = din("w_in", [DEPTH, DM, IN_COLS])
    rdl = din("rdl", [DEPTH, 8])
    mws = din("mws", [DEPTH, 4, 128, 128])
    mbs = din("mbs", [DEPTH, 128, 4])
    dlq = din("dlq", [DEPTH, 128])
    dlk = din("dlk", [DEPTH, 128])
    w_bo = din("w_bo", [DEPTH, 3, 512, DM])
    w_out = din("w_out", [DEPTH, DM, DM])
    cf = din("cf", [128, 7 * 128 + 2])
    ropet = din("ropet", [128, 4, SEQ])
    pm_in = din("pm", [128, 256])
    out_d = nc.dram_tensor("out", [SEQ, DM], F32, kind="ExternalOutput").ap()
    xs_d = nc.dram_tensor("xs", [SEQ, DM], F32, kind="Internal").ap()
    mr_d = nc.dram_tensor("mrows", [DEPTH, 2, 3 * DM], F32, kind="Internal").ap()
    wbf_in = nc.dram_tensor("wbf_in", [DEPTH, DM, IN_COLS], BF16, kind="Internal").ap()
    wbf_bo = nc.dram_tensor("wbf_bo", [DEPTH, 3, 512, DM], BF16, kind="Internal").ap()
    wbf_out = nc.dram_tensor("wbf_out", [DEPTH, DM, DM], BF16, kind="Internal").ap()

    with ExitStack() as st:
        k = KB(nc, st)
        XSD = [[T(None, f"xsd{l}_{j}") for j in range(NTL)] for l in range(2)]
        MRD = [[T(None, f"mrd{l}_{j}") for j in range(12)] for l in range(DEPTH)]

        Cx = [k.sb(f"cx{g}", [128, DM], F32) for g in range(NTC)]
        Xs = [k.sb(f"xsb{i}", [128, DM], F32) for i in range(ST)]
        Y = [k.sb(f"y{i}", [128, DM], F32) for i in range(ST)]
        F1 = k.sb("f1", [128, DM], F32)
        F2 = k.sb("f2", [128, DM], F32)
        G1 = k.sb("g1", [128, DM], BF16)
        SH = k.sb("sh", [128, DM], BF16)
        GT = k.sb("gt", [128, DM], BF16)
        DKT = [k.sb(f"dkt{g}", [128, 4, 128], BF16) for g in range(NTT)]
        VA = [k.sb(f"va{g}", [128, 4, 130], BF16) for g in range(NTT)]
        TBS = [k.sb(f"tbs{g}", [128, 512], BF16) for g in range(NTT)]
        HT = [k.sb("hTa", [128, 8, ST * 128], BF16), k.sb("hTb", [128, 8, ST * 128], BF16)]
        hcur = {"i": 0}
        pend = {"gen": None, "pos": -1, "dst": None}
        HB = k.sb("hb", [128, DM], BF16)
        W = [k.sb(f"w{i}", [128, 8, 256], BF16) for i in range(4)]
        ZB = [k.sb(f"zb{i}", [128, ST * 128], BF16) for i in range(2)]
        PMR = k.sb("pmr", [128, 128], BF16)
        PMD = k.sb("pmd", [128, 128], BF16)
        RT = k.sb("ropes", [128, 4, ST * 128], BF16)
        QK = k.sb("qk", [128, 8, ST * 128], BF16)
        KTOK = [k.sb(f"ktok{i}", [128, 512], BF16) for i in range(ST)]
        RV = [k.sb(f"rv{i}", [128, 512], BF16) for i in range(ST)]
        SG_ = [k.sb(f"sg{i}", [128, 512], BF16) for i in range(ST)]
        OT = [k.sb(f"oT{b}", [128, 4, ST * 128], BF16) for b in range(3)]
        SMs = [k.sb(f"sm{i}", [128, 512], BF16) for i in range(2)]
        QFs = [k.sb(f"qf{i}", [128, 512], BF16) for i in range(2)]
        QBs = [k.sb(f"qb{i}", [128, 512], BF16) for i in range(2)]
        OBQ = [[k.sb(f"ob{b_}{h_}", [128, 128], BF16) for h_ in range(4)] for b_ in range(2)]
        E_ = [k.sb(f"e{i}", [128, 512], BF16) for i in range(3)]
        SGM = [k.sb(f"sgm{i}", [128, 256], BF16) for i in range(2)]
        SF = k.sb("sf", [128, 512], F32)
        SFB = k.sb("sfb", [128, 512], BF16)
        TC = k.sb("tcur", [128, 512], F32)
        MCT = k.sb("mct", [128, 512], BF16)
        QDF = k.sb("qdf", [128, 512], BF16)
        QDB = k.sb("qdb", [128, 512], BF16)
        IDB = k.sb("idb", [128, 128], BF16)
        WST = k.sb("wst", [128, 4, 128], BF16)
        SV = k.sb("sv", [128, 128], F32)
        R0 = k.sb("r0", [128, 4], F32)
        R1 = k.sb("r1", [128, 4], F32)
        SVMa = k.sb("svma", [128, 8], F32)
        SVMb = k.sb("svmb", [128, 8], F32)
        SVM = k.sb("svm", [128, 32], F32)
        SVH = k.sb("svh", [128, 32], F32)
        SVN = k.sb("svn", [128, 48], F32)
        HB2 = k.sb("hb2", [128, DM], BF16)
        SCC = k.sb("scc", [128, 8, 2], BF16)
        CCF = k.sb("ccf", [128, 8, 2], F32)
        c_eps, c_nlam, c_ss, c_rs, c_lg, c_kdec, c_cdec, c_bs, c_tmp = 0, 1, 4, 12, 20, 28, 36, 44, 48
        c_ss4, c_e2, c_mean, c_sum2 = 56, 76, 80, 84

        PG = [k.ps(f"pg{i}", [128, 512]) for i in range(3)]
        OA = [k.ps(f"oa{i}", [128, 512]) for i in range(2)]
        KVP = k.ps("kvp", [128, 512])
        TP = [k.ps(f"tp{i}", [128, 1024], BF16) for i in range(2)]
        rr = {"pg": 0, "tp": 0, "w": 0, "e": 0, "sgm": 0, "mrb": 0, "zb": 0, "pgx": 0}
        PGX = PG + OA

        def nxt(key, lst):
            t = lst[rr[key] % len(lst)]
            rr[key] += 1
            return t

        dbg_names = []

        def dbg(name, ap, t, shape):
            if not debug:
                return
            d = nc.dram_tensor("dbg_" + name, list(shape), F32, kind="ExternalOutput").ap()
            k.dma("pool", d, ap, t, reads=[t], is_out=True)
            dbg_names.append(name)

        V = nc.vector
        A = nc.scalar
        P = nc.gpsimd
        PE = nc.tensor

        def mm(out, lhsT, rhs, start, stop, reads, writes, signal, sgc=False):
            k.op("pe", lambda: PE.matmul(out, lhsT, rhs, start=start, stop=stop, skip_group_check=sgc), reads, writes, signal)

        def tr(out, in_, reads, writes, signal):
            k.op("pe", lambda: PE.transpose(out, in_, IDB[:]), list(reads) + [IDB], writes, signal)

        def col(c, n=1):
            return SV[:, c:c + n]

        k.dma("sp", F2[:, 0:898], cf, F2, writes=[F2])
        k.dma("sp", CCF[:], cct, CCF, writes=[CCF])
        for g in range(NTC):
            k.dma("sp", Cx[g][:], ctx_in[g * 128:(g + 1) * 128, :], Cx[g], writes=[Cx[g]])
        k.dma("pool", PMR[:], pm_in[:, 0:128], PMR, writes=[PMR])
        k.dma("pool", PMD[:], pm_in[:, 128:256], PMD, writes=[PMD])
        k.op("dve", lambda: V.tensor_copy(IDB[:], F2[:, 0:128]), [F2], [IDB])
        k.op("dve", lambda: V.memset(SV[:], 0.0), [], [SV])
        k.op("dve", lambda: V.memset(col(c_eps), EPS), [], [SV])
        for g in range(NTT):
            k.op("pool", lambda: P.memset(VA[g][:], 1.0), [], [VA[g]])
        k.op("act", lambda: A.activation(CCF[:], CCF[:], AF.Silu), [CCF], [CCF])
        k.op("act", lambda: A.copy(SCC[:], CCF[:]), [CCF], [SCC])
        CF = F2
        D1, U_, D2, L_, P1, P2 = [F2[:, (i + 1) * 128:(i + 2) * 128] for i in range(6)]
        PC = F2[:, 7 * 128:7 * 128 + 2]

        GW = {"keys": [], "tiles": {}, "ni": 0, "ng": 0}
        WDEPTH = 2
        wcache = {}

        def resolve(key):
            kind = key[0]
            if kind == "mod":
                _, l_, b_ = key
                return w_mod[l_][:, b_ * 256:(b_ + 1) * 256], 8, None
            if kind == "in":
                _, l_, c0 = key
                return w_in[l_][:, c0:c0 + 256], 8, wbf_in[l_][:, c0:c0 + 256]
            if kind == "bo":
                _, l_, br_, q_ = key
                return w_bo[l_, br_][:, q_ * 256:(q_ + 1) * 256], 4, wbf_bo[l_, br_][:, q_ * 256:(q_ + 1) * 256]
            _, l_, q_ = key
            return w_out[l_][:, q_ * 256:(q_ + 1) * 256], 8, wbf_out[l_][:, q_ * 256:(q_ + 1) * 256]

        def gw_issue(key):
            src2d, nk, dst2d = resolve(key)
            wt = nxt("w", W)
            if dst2d is not None and key in wcache:
                k.dma("sp", wt[:, 0:nk, :], dst2d.rearrange("(kc p) c -> p kc c", p=128), wt, reads=[wcache[key]], writes=[wt])
            else:
                k.dma("pool", wt[:, 0:nk, :], src2d.rearrange("(kc p) c -> p kc c", p=128), wt, writes=[wt])
                if dst2d is not None and USE_WCACHE:
                    tobj = T(None, "wc")
                    k.dma("sp", dst2d.rearrange("(kc p) c -> p kc c", p=128), wt[:, 0:nk, :], wt, reads=[wt], writes=[tobj])
                    wcache[key] = tobj
            return wt

        def gw_issue_upto(n_):
            while GW["ni"] < min(n_, len(GW["keys"])):
                GW["tiles"][GW["ni"]] = gw_issue(GW["keys"][GW["ni"]])
                GW["ni"] += 1

        class WStream:
            def __init__(self, specs):
                self.specs = specs
                self.i = 0
                assert GW["keys"][GW["ng"]:GW["ng"] + len(specs)] == list(specs), (specs[:2], GW["keys"][GW["ng"]:GW["ng"] + 2])
                gw_issue_upto(GW["ng"] + WDEPTH)

            def get(self):
                key = self.specs[self.i]
                self.i += 1
                i_ = GW["ng"]
                assert GW["keys"][i_] == key
                gw_issue_upto(i_ + 1 + WDEPTH)
                GW["ng"] += 1
                return GW["tiles"].pop(i_)

        def win_spec(l, c0):
            return ("in", l, c0)

        def setup_specs(l):
            return [("mod", l, b) for b in range(12)]

        def p1_specs(l):
            return [win_spec(l, C_RK), win_spec(l, C_RK + 256), win_spec(l, C_RV), win_spec(l, C_RV + 256),
                    win_spec(l, C_DK), win_spec(l, C_DK + 256), win_spec(l, C_DV), win_spec(l, C_DV + 256)]

        def R_specs(l, states_only):
            specs = [win_spec(l, C_RK), win_spec(l, C_RK + 256), win_spec(l, C_RV), win_spec(l, C_RV + 256)]
            if not states_only:
                specs += [win_spec(l, C_RQ), win_spec(l, C_RQ + 256), win_spec(l, C_RG), win_spec(l, C_RG + 256)]
            return specs

        def D_specs(l):
            return [win_spec(l, C_DQ), win_spec(l, C_DQ + 256), win_spec(l, C_DG), win_spec(l, C_DG + 256)]

        def M_specs(l):
            return [win_spec(l, C_MV), win_spec(l, C_MV + 256), win_spec(l, C_MG), win_spec(l, C_MG + 256),
                    win_spec(l, C_MU), win_spec(l, C_MU + 256)]

        def G_specs(l):
            specs = []
            for br in range(3):
                for q in range(4):
                    specs.append(win_spec(l, C_ML + br * DM + q * 256))
                    specs.append(("bo", l, br, q))
            return specs

        def O_specs(l):
            return [("out", l, q) for q in range(4)]

        def p2_specs(l):
            return R_specs(l, False) + D_specs(l) + M_specs(l) + G_specs(l) + O_specs(l)

        def layer_setup(l):
            ws = WStream(setup_specs(l))
            for b in range(12):
                wt = ws.get()
                pg = nxt("pg", PG)
                for kc in range(8):
                    mm(pg[0:2, 0:256], SCC[:, kc, :], wt[:, kc, :], kc == 0, kc == 7, [SCC, wt], [pg], kc == 7)
                k.dma("sp", F1[0:2, 512:768], b_mod[l, b * 256:(b + 1) * 256].partition_broadcast(2), F1, writes=[F1])
                k.op("dve", lambda: V.tensor_tensor(F1[0:2, 768:1024], pg[0:2, 0:256], F1[0:2, 512:768], ALU.add), [pg, F1], [F1])
                k.dma("sp", mr_d[l][:, b * 256:(b + 1) * 256], F1[0:2, 768:1024], F1, reads=[F1], writes=[MRD[l][b]])
            k.dma("sp", F2[:, 0:898], cf, F2, writes=[F2])
            k.dma("sp", col(c_lg, 8), rdl[l].partition_broadcast(128), SV, writes=[SV])
            k.op("act", lambda: A.activation(col(c_lg, 8), col(c_lg, 8), AF.Sigmoid), [SV], [SV])
            k.op("act", lambda: A.activation(col(c_lg, 8), col(c_lg, 8), AF.Ln), [SV], [SV])
            k.op("act", lambda: A.activation(col(c_cdec, 8), col(c_lg, 8), AF.Exp, scale=128.0), [SV], [SV])
            for h in range(4):
                hs = slice(h * 128, (h + 1) * 128)
                k.op("act", lambda: A.activation(F1[:, 0:128], D1, AF.Exp, scale=col(c_lg + h)), [CF, SV], [F1])
                k.op("act", lambda: A.activation(F1[:, 128:256], D2, AF.Exp, scale=col(c_lg + 4 + h)), [CF, SV], [F1])
                k.op("dve", lambda: V.tensor_tensor(F1[:, 0:128], F1[:, 0:128], U_, ALU.mult), [F1, CF], [F1])
                k.op("dve", lambda: V.tensor_tensor(F1[:, 128:256], F1[:, 128:256], L_, ALU.mult), [F1, CF], [F1])
                k.op("dve", lambda: V.tensor_tensor(F1[:, 0:128], F1[:, 0:128], F1[:, 128:256], ALU.add), [F1], [F1])
                k.op("dve", lambda: V.tensor_scalar_mul(MCT[:, hs], F1[:, 0:128], RET_SCALE), [F1], [MCT])
                k.op("act", lambda: A.activation(QDF[:, hs], P1, AF.Exp, scale=col(c_lg + h)), [CF, SV], [QDF])
                k.op("act", lambda: A.activation(QDB[:, hs], P2, AF.Exp, scale=col(c_lg + 4 + h)), [CF, SV], [QDB])
                k.op("act", lambda: A.activation(col(c_kdec + h), PC[:, 0:1], AF.Exp, scale=col(c_lg + h)), [CF, SV], [SV])
                k.op("act", lambda: A.activation(col(c_kdec + 4 + h), PC[:, 1:2], AF.Exp, scale=col(c_lg + 4 + h)), [CF, SV], [SV])
            k.op("dve", lambda: V.tensor_scalar_mul(col(c_kdec, 8), col(c_kdec, 8), RET_SCALE), [SV], [SV])
            lam_init = 0.8 - 0.6 * math.exp(-0.3 * l)
            LQ = F1[:, 256:384]
            LK = F1[:, 384:512]
            k.dma("sp", LQ, dlq[l].partition_broadcast(128), F1, writes=[F1])
            k.dma("sp", LK, dlk[l].partition_broadcast(128), F1, writes=[F1])
            k.op("dve", lambda: V.tensor_tensor(LQ, LQ, LK, ALU.mult), [F1], [F1])
            k.op("dve", lambda: V.reduce_sum(col(c_e2, 2), LQ.rearrange("p (a b) -> p a b", a=2), AX.X), [F1], [SV])
            k.op("act", lambda: A.activation(col(c_e2, 2), col(c_e2, 2), AF.Exp), [SV], [SV])
            k.op("dve", lambda: V.tensor_tensor(col(c_nlam), col(c_e2 + 1), col(c_e2), ALU.subtract), [SV], [SV])
            k.op("dve", lambda: V.tensor_scalar_add(col(c_nlam), col(c_nlam), -lam_init), [SV], [SV])
            WSN_t = nxt("w", W)
            WSN = WSN_t[:, 0:2, :].rearrange("p k (g j) -> p (k g) j", j=128)
            k.dma("pool", WSN, mws[l].rearrange("g i j -> i g j"), WSN_t, writes=[WSN_t])
            k.dma("sp", col(c_bs, 4), mbs[l], SV, writes=[SV])
            tp = nxt("tp", TP)
            for g in range(4):
                tr(tp[:, g * 128:(g + 1) * 128], WSN[:, g, :], [WSN_t], [tp], g == 3)
            k.op("dve", lambda: V.tensor_copy(WST[:].rearrange("p g j -> p (g j)"), tp[:, 0:512]), [tp], [WST])
            k.op("dve", lambda: V.memset(SF[:], 0.0), [], [SF])
            k.op("dve", lambda: V.memset(TC[:], 0.0), [], [TC])
            k.op("pool", lambda: P.memset(SFB[:], 0.0), [], [SFB])

        def load_mod(l, s):
            st_ = [Y[0], Y[1], Y[2], Y[3], F1]
            k.dma("sp", st_[0][:], mr_d[l][s, 0:DM].partition_broadcast(128), st_[0], reads=MRD[l][0:4], writes=[st_[0]])
            k.dma("sp", st_[1][:], mr_d[l][s, DM:2 * DM].partition_broadcast(128), st_[1], reads=MRD[l][4:8], writes=[st_[1]])
            k.dma("sp", st_[2][:], g_pre[l].partition_broadcast(128), st_[2], writes=[st_[2]])
            k.dma("sp", st_[3][:], mr_d[l][s, 2 * DM:3 * DM].partition_broadcast(128), st_[3], reads=MRD[l][8:12], writes=[st_[3]])
            k.dma("sp", st_[4][:], g_post[l].partition_broadcast(128), st_[4], writes=[st_[4]])
            k.op("act", lambda: A.copy(SH[:], st_[0][:]), [st_[0]], [SH])
            k.op("dve", lambda: V.scalar_tensor_tensor(G1[:], st_[1][:], 1.0, st_[2][:], ALU.add, ALU.mult), [st_[1], st_[2]], [G1])
            k.op("dve", lambda: V.tensor_tensor(GT[:], st_[3][:], st_[4][:], ALU.mult), [st_[3], st_[4]], [GT])

        def rstd_from(ss_ap, n, inv_n, out_ap, t=None):
            t = SV if t is None else t
            k.op("act", lambda: A.activation(out_ap, ss_ap, AF.Sqrt, bias=col(c_eps), scale=inv_n), [t, SV], [t])
            k.op("dve", lambda: V.reciprocal(out_ap, out_ap), [t], [t])

        xseq = []
        xstate = {"pos": 0, "loaded": None}

        def x_load(l, sup):
            xsrc_ = x_in if l == 0 else xs_d
            for i, g in enumerate(sup):
                j = g - NTC
                rd = [XSD[l][j]] if l > 0 else []
                k.dma("sp", Xs[i][:], xsrc_[j * 128:(j + 1) * 128, :], Xs[i], reads=rd, writes=[Xs[i]])

        def make_hT_gen(l, sup, is_ctx, dst, tmps, hbs):
            n = len(sup)
            src = []
            if not is_ctx:
                assert xseq[xstate["pos"]] == (l, sup[0])
                if xstate["loaded"] != xstate["pos"]:
                    x_load(l, sup)
                xstate["pos"] += 1
            for i, g in enumerate(sup):
                src.append(Cx[g] if is_ctx else Xs[i])
            k.op("dve", lambda: V.memset(SVN[:, 0:n], 0.0), [], [SVN])
            for i in range(n):
                Hj = hbs[i % len(hbs)]
                k.op("act", lambda: A.activation(Hj[:], src[i][:], AF.Square, accum_out=SVN[:, i:i + 1]), [src[i]], [Hj, SVN])
            rstd_from(SVN[:, 0:n], n, 1.0 / DM, SVN[:, 8:8 + n], SVN)
            yield
            for i in range(n):
                Fx = tmps[i % len(tmps)]
                Hx = hbs[i % len(hbs)]
                k.op("dve", lambda: V.scalar_tensor_tensor(Fx[:], src[i][:], SVN[:, 8 + i:9 + i], G1[:], ALU.mult, ALU.mult),
                     [src[i], SVN, G1], [Fx])
                k.op("pool", lambda: P.tensor_tensor(Hx[:], Fx[:], SH[:], ALU.add), [Fx, SH], [Hx])
                yield
                tp = nxt("tp", TP)
                for kc in range(8):
                    tr(tp[:, kc * 128:(kc + 1) * 128], Hx[:, kc * 128:(kc + 1) * 128], [Hx], [tp], kc == 7)
                k.op("act", lambda: A.copy(dst[:, :, i * 128:(i + 1) * 128], tp[:].rearrange("p (k t) -> p k t", k=8)),
                     [tp], [dst])
                yield
            if not is_ctx and xstate["pos"] < len(xseq):
                nl, ns0 = xseq[xstate["pos"]]
                nsup = list(range(ns0, ns0 + ST))
                if nl == 0 or all(len(XSD[nl][g - NTC].w) > 0 for g in nsup):
                    x_load(nl, nsup)
                    xstate["loaded"] = xstate["pos"]

        def tick():
            if pend["gen"] is not None:
                try:
                    next(pend["gen"])
                except StopIteration:
                    pend["gen"] = None

        def drain():
            while pend["gen"] is not None:
                tick()

        def make_hT(l, sup, is_ctx, xsrc):
            if (not is_ctx) and pend["pos"] == xstate["pos"] - 1 and pend["dst"] is not None and pend["key"] == (l, sup[0]):
                drain()
            else:
                drain()
                dst = HT[1 - hcur["i"]]
                pend.update(gen=make_hT_gen(l, sup, is_ctx, dst, [F1, F2], [HB, HB2]), dst=dst, key=(l, sup[0]))
                pend["pos"] = xstate["pos"]
                drain()
            hcur["i"] = HT.index(pend["dst"])
            pend["dst"] = None

        def schedule_next_hT(l, tmps):
            if xstate["pos"] >= len(xseq):
                return
            nl, ns0 = xseq[xstate["pos"]]
            if nl != l:
                return
            nsup = list(range(ns0, ns0 + ST))
            dst = HT[1 - hcur["i"]]
            pend.update(gen=make_hT_gen(nl, nsup, False, dst, tmps, [HB, HB2]), dst=dst, key=(nl, ns0))
            pend["pos"] = xstate["pos"]
            tick()

        rope_state = {"loaded": None}

        def prefetch_rope_next(sup, step):
            idx = lsups.index(sup) + step
            if 0 <= idx < len(lsups):
                load_rope(lsups[idx])

        def load_rope(sup):
            if rope_state["loaded"] == sup[0]:
                return
            rope_state["loaded"] = sup[0]
            j0 = sup[0] - NTC
            n = len(sup)
            k.dma("pool", RT[:, :, 0:n * 128], ropet[:, :, j0 * 128:(j0 + n) * 128], RT, writes=[RT])

        def proj_fm_group(ws, dst, NT, is_ctx, kind, split=False, dkt_sup=None, hook=None):
            ci, si = (0, 1) if kind == "ret" else (2, 3)
            pm = PMR if kind == "ret" else PMD
            if split:
                k.op("pool", lambda: P.memset(QK[64:128, 0:4, 0:NT], 0.0), [], [QK])
                k.op("pool", lambda: P.memset(QK[0:64, 4:8, 0:NT], 0.0), [], [QK])
            wts = {}

            def proj(hc):
                wi, c = divmod(hc, 2)
                if c == 0:
                    wts[wi] = ws.get()
                wt = wts[wi]
                tick()
                hT = HT[hcur["i"]]
                pz = PG[hc % 2]
                for kc in range(8):
                    mm(pz[:, 0:NT], wt[:, kc, c * 128:(c + 1) * 128], hT[:, kc, 0:NT], kc == 0, kc == 7, [wt, hT], [pz], kc == 7)
                return pz

            def finish(hc, pz):
                outs = [(slice(0, 128), dst + hc)] if not split else [(slice(0, 64), hc), (slice(64, 128), 4 + hc)]
                if is_ctx:
                    if dkt_sup is None:
                        for rs_, oc in outs:
                            k.op("act", lambda: A.copy(QK[rs_, oc, 0:NT], pz[rs_, 0:NT]), [pz], [QK])
                    else:
                        for i, g in enumerate(dkt_sup):
                            k.op("act", lambda: A.copy(DKT[g][:, hc, :], pz[:, i * 128:(i + 1) * 128]), [pz], [DKT[g]])
                    return
                zb = nxt("zb", ZB)
                k.op("act", lambda: A.copy(zb[:, 0:NT], pz[:, 0:NT]), [pz], [zb])
                pr = PG[2]
                mm(pr[:, 0:NT], pm[:], zb[:, 0:NT], True, True, [pm, zb], [pr], True)
                Fa, Fb = (F1, F2) if hc % 2 == 0 else (Y[2], Y[3])
                k.op("dve", lambda: V.tensor_tensor(Fa[:, 0:NT], pz[:, 0:NT], RT[:, ci, 0:NT], ALU.mult), [pz, RT], [Fa])
                k.op("dve", lambda: V.tensor_tensor(Fb[:, 0:NT], pr[:, 0:NT], RT[:, si, 0:NT], ALU.mult), [pr, RT], [Fb])
                if dkt_sup is None:
                    for rs_, oc in outs:
                        k.op("pool", lambda: P.tensor_tensor(QK[rs_, oc, 0:NT], Fa[rs_, 0:NT], Fb[rs_, 0:NT], ALU.add), [Fa, Fb], [QK])
                else:
                    for i, g in enumerate(dkt_sup):
                        ts = slice(i * 128, (i + 1) * 128)
                        k.op("pool", lambda: P.tensor_tensor(DKT[g][:, hc, :], Fa[:, ts], Fb[:, ts], ALU.add), [Fa, Fb], [DKT[g]])

            pz_prev = proj(0)
            if hook is not None:
                hook(0)
            for hc in range(1, 4):
                pz_cur = proj(hc)
                if hook is not None:
                    hook(hc)
                finish(hc - 1, pz_prev)
                pz_prev = pz_cur
            finish(3, pz_prev)

        dst_t = [None]
        VN = KTOK
        DFO = Y
        yT = QK

        def proj_tm(wt, i, nk=8, src=None):
            pg = nxt("pgx", PGX)
            s = HT[hcur["i"]] if src is None else src
            for kc in range(nk):
                mm(pg[:, 0:256], s[:, kc, i * 128:(i + 1) * 128], wt[:, kc, :], kc == 0, kc == nk - 1, [wt, s], [pg], kc == nk - 1)
            return pg

        def k_side(l, ws, sup, is_ctx, dirn):
            n = len(sup)
            NT = n * 128
            proj_fm_group(ws, 4, NT, is_ctx, "ret")
            for i in range(n):
                tp = nxt("tp", TP)
                for h in range(4):
                    tr(tp[:, h * 128:(h + 1) * 128], QK[:, 4 + h, i * 128:(i + 1) * 128], [QK], [tp], h == 3)
                k.op("dve", lambda: V.tensor_tensor(KTOK[i][:].rearrange("p (h d) -> p h d", h=4), tp[:, 0:512].rearrange("p (h d) -> p h d", h=4),
                                                    col(c_kdec + 4 * dirn, 4).unsqueeze(2).to_broadcast([128, 4, 128]), ALU.mult), [tp, SV], [KTOK[i]])
            for wi in range(2):
                wt = ws.get()
                for i in range(n):
                    pg = proj_tm(wt, i)
                    k.op("act", lambda: A.copy(RV[i][:, wi * 256:(wi + 1) * 256], pg[:, 0:256]), [pg], [RV[i]])

        def kv_update(i, state, dirn):
            for h in range(4):
                hs = slice(h * 128, (h + 1) * 128)
                mm(KVP[:, hs], KTOK[i][:, hs], RV[i][:, hs], True, True, [KTOK[i], RV[i]], [KVP], h == 3)
            for h in range(4):
                hs = slice(h * 128, (h + 1) * 128)
                k.op("dve", lambda: V.scalar_tensor_tensor(state[:, hs], state[:, hs], col(c_cdec + 4 * dirn + h), KVP[:, hs], ALU.mult, ALU.add),
                     [state, SV, KVP], [state])

        def headnorm_batch(items, const, dstT, do_tr=True):
            nI = len(items)
            for j, (i, src_ap, src_t, gate_t) in enumerate(items):
                half = F2[:, (j % 2) * 512:(j % 2 + 1) * 512]
                k.op("act", lambda: A.activation(half, src_ap, AF.Square), [src_t], [F2])
                k.op("dve", lambda: V.reduce_sum(SVH[:, 4 * j:4 * j + 4], half.rearrange("p (h e) -> p h e", h=4), AX.X), [F2], [SVH])
            k.op("act", lambda: A.activation(SVH[:, 0:4 * nI], SVH[:, 0:4 * nI], AF.Sqrt, bias=col(c_eps), scale=1.0 / 128), [SVH, SV], [SVH])
            k.op("dve", lambda: V.reciprocal(SVH[:, 0:4 * nI], SVH[:, 0:4 * nI]), [SVH], [SVH])
            for j, (i, src_ap, src_t, gate_t) in enumerate(items):
                ob = OBQ[i % 2]
                for h in range(4):
                    hs = slice(h * 128, (h + 1) * 128)
                    if const == 1.0:
                        k.op("dve", lambda: V.scalar_tensor_tensor(ob[h][:], src_ap[:, hs], SVH[:, 4 * j + h:4 * j + h + 1], gate_t[:, hs],
                                                                   ALU.mult, ALU.mult), [src_t, SVH, gate_t], [ob[h]])
                    else:
                        k.op("dve", lambda: V.tensor_scalar(F1[:, 512 + h * 128:512 + (h + 1) * 128], src_ap[:, hs], SVH[:, 4 * j + h:4 * j + h + 1], const,
                                                            ALU.mult, ALU.mult), [src_t, SVH], [F1])
                if const != 1.0:
                    for h in range(4):
                        k.op("pool", lambda: P.tensor_tensor(ob[h][:], F1[:, 512 + h * 128:512 + (h + 1) * 128], gate_t[:, h * 128:(h + 1) * 128], ALU.mult),
                             [F1, gate_t], [ob[h]])
                if do_tr:
                    transpose_into(ob, 4, dstT, i)

        def transpose_into(src_ts, nchunks, dstT, i):
            tp = nxt("tp", TP)
            for c in range(nchunks):
                tr(tp[:, c * 128:(c + 1) * 128], src_ts[c][:], [src_ts[c]], [tp], c == nchunks - 1)
            k.op("act", lambda: A.copy(dstT[:, :, i * 128:(i + 1) * 128], tp[:, 0:nchunks * 128].rearrange("p (c t) -> p c t", c=nchunks)),
                 [tp], [dstT])

        def pass1(l, sup, is_ctx, xsrc):
            n = len(sup)
            NT = n * 128
            make_hT(l, sup, is_ctx, xsrc)
            if not is_ctx:
                load_rope(sup)
                schedule_next_hT(l, [Y[0], Y[1]])
            ws = WStream(p1_specs(l))
            k_side(l, ws, sup, is_ctx, 1)
            order = list(reversed(range(n)))

            def chain_step(j):
                if j < n:
                    i = order[j]
                    g = sup[i]
                    k.op("act", lambda: A.copy(TBS[g][:], TC[:]), [TC], [TBS[g]])
                    kv_update(i, TC, 1)

            proj_fm_group(ws, 0, NT, is_ctx, "diff", dkt_sup=sup, hook=chain_step)
            if not is_ctx:
                prefetch_rope_next(sup, -1)
            for wi in range(2):
                wt = ws.get()
                for i, g in enumerate(sup):
                    pg = proj_tm(wt, i)
                    k.op("act", lambda: A.copy(VA[g][:, 2 * wi:2 * wi + 2, 0:128], pg[:, 0:256].rearrange("p (h e) -> p h e", h=2)),
                         [pg], [VA[g]])

        def stage_R(l, sup, is_ctx, states_only):
            n = len(sup)
            NT = n * 128
            ws = WStream(R_specs(l, states_only))
            k_side(l, ws, sup, is_ctx, 0)
            if not states_only:
                proj_fm_group(ws, 0, NT, is_ctx, "ret")
                for wi in range(2):
                    wt = ws.get()
                    for i in range(n):
                        pg = proj_tm(wt, i)
                        k.op("act", lambda: A.activation(SG_[i][:, wi * 256:(wi + 1) * 256], pg[:, 0:256], AF.Silu), [pg], [SG_[i]])
            def prep(i):
                ts = slice(i * 128, (i + 1) * 128)
                ps = nxt("pg", PG)
                for h in range(4):
                    hs = slice(h * 128, (h + 1) * 128)
                    mm(ps[:, hs], QK[:, 4 + h, ts], QK[:, h, ts], True, True, [QK], [ps], h == 3)
                SM, QF, QB = SMs[i % 2], QFs[i % 2], QBs[i % 2]
                k.op("dve", lambda: V.tensor_tensor(SM[:], ps[:], MCT[:], ALU.mult), [ps, MCT], [SM])
                k.op("pool", lambda: P.tensor_tensor(QF[:].rearrange("p (h t) -> p h t", h=4), QK[:, 0:4, ts],
                                                     QDF[:].rearrange("p (h t) -> p h t", h=4), ALU.mult), [QK, QDF], [QF])
                k.op("pool", lambda: P.tensor_tensor(QB[:].rearrange("p (h t) -> p h t", h=4), QK[:, 0:4, ts],
                                                     QDB[:].rearrange("p (h t) -> p h t", h=4), ALU.mult), [QK, QDB], [QB])

            if not states_only:
                prep(0)
            for i, g in enumerate(sup):
                if not states_only and i + 1 < n:
                    prep(i + 1)
                kv_update(i, SF, 0)
                if not states_only:
                    SM, QF, QB = SMs[i % 2], QFs[i % 2], QBs[i % 2]
                    po = nxt("pg", PG)
                    for h in range(4):
                        hs = slice(h * 128, (h + 1) * 128)
                        mm(po[:, hs], SM[:, hs], RV[i][:, hs], True, False, [SM, RV[i]], [po], False)
                        mm(po[:, hs], QF[:, hs], SFB[:, hs], False, False, [QF, SFB], [po], False)
                        mm(po[:, hs], QB[:, hs], TBS[g][:, hs], False, True, [QB, TBS[g]], [po], h == 3)
                k.op("act", lambda: A.copy(SFB[:], SF[:]), [SF], [SFB])
                if not states_only:
                    headnorm_batch([(i, po[:], po, SG_[i])], 1.0, OT[0], do_tr=False)
                    if i > 0:
                        transpose_into(OBQ[(i - 1) % 2], 4, OT[0], i - 1)
            if not states_only:
                transpose_into(OBQ[(n - 1) % 2], 4, OT[0], n - 1)

        def stage_D(l, sup, is_ctx, before_tail=None):
            n = len(sup)
            NT = n * 128
            lam_init = 0.8 - 0.6 * math.exp(-0.3 * l)
            krange = list(range(NTC)) if is_ctx else list(range(NTT))
            ws = WStream(D_specs(l))
            proj_fm_group(ws, 0, NT, is_ctx, "diff", split=True)
            if not is_ctx:
                prefetch_rope_next(sup, +1)
            for wi in range(2):
                wt = ws.get()
                for i in range(n):
                    pg = proj_tm(wt, i)
                    k.op("act", lambda: A.activation(SG_[i][:, wi * 256:(wi + 1) * 256], pg[:, 0:256], AF.Silu), [pg], [SG_[i]])
            if not is_ctx:
                schedule_next_hT(l, [F2])
            steps = [(h, m, ki, kt) for h in range(4) for m in range(2) for ki, kt in enumerate(krange)]
            ps_of = {}
            LOOK = 2

            def emit_st(s_):
                h_, m_, ki_, kt_ = steps[s_]
                ps_ = nxt("pg", PG)
                mm(ps_[:, 0:NT], DKT[kt_][:, h_, :], QK[:, 4 * m_ + h_, 0:NT], True, True, [DKT[kt_], QK], [ps_], True)
                ps_of[s_] = ps_

            for s_ in range(min(LOOK, len(steps))):
                emit_st(s_)
            for s_ in range(len(steps)):
                h, m, ki, kt = steps[s_]
                ps = ps_of.pop(s_)
                e = nxt("e", E_)
                k.op("act", lambda: A.activation(e[:, 0:NT], ps[:, 0:NT], AF.Exp, scale=0.125), [ps], [e])
                if s_ + LOOK < len(steps):
                    emit_st(s_ + LOOK)
                if s_ % 8 == 7:
                    tick()
                for i in range(n):
                    oa = OA[i // 2]
                    o0 = (i % 2) * 129
                    mm(oa[:, o0:o0 + 129], e[:, i * 128:(i + 1) * 128], VA[kt][:, h, 0:129], ki == 0 and i % 2 == 0,
                       ki == len(krange) - 1, [e, VA[kt]], [oa], i == n - 1, sgc=True)
                if ki != len(krange) - 1:
                    continue
                nb = (n + 1) // 2
                for b_ in range(nb):
                    stg = Y[2 * m + b_]
                    k.op("dve", lambda: V.tensor_copy(stg[:, 512:512 + 258], OA[b_][:, 0:258]), [OA[b_]], [stg])
                if m == 0:
                    continue
                for b_ in range(nb):
                    nb2 = min(2, n - 2 * b_)
                    sv0 = Y[b_][:, 512:512 + 258].rearrange("p (t c) -> p t c", c=129)[:, 0:nb2, 128]
                    sv1 = Y[2 + b_][:, 512:512 + 258].rearrange("p (t c) -> p t c", c=129)[:, 0:nb2, 128]
                    k.op("dve", lambda: V.reciprocal(R0[:, 2 * b_:2 * b_ + nb2], sv0), [Y[b_]], [R0])
                    k.op("dve", lambda: V.reciprocal(R1[:, 2 * b_:2 * b_ + nb2], sv1), [Y[2 + b_]], [R1])
                k.op("dve", lambda: V.tensor_scalar_mul(R1[:, 0:n], R1[:, 0:n], col(c_nlam)), [R1, SV], [R1])
                for i in range(n):
                    s0 = Y[i // 2]
                    s1 = Y[2 + i // 2]
                    o0 = 512 + (i % 2) * 129
                    tt = SGM[i % 2]
                    ttv = tt[:].bitcast(F32)
                    k.op("dve", lambda: V.tensor_scalar_mul(ttv, s0[:, o0:o0 + 128], R0[:, i:i + 1]), [s0, R0], [tt])
                    k.op("dve", lambda: V.scalar_tensor_tensor(DFO[i][:, h * 128:(h + 1) * 128], s1[:, o0:o0 + 128], R1[:, i:i + 1], ttv,
                                                               ALU.mult, ALU.add), [s1, R1, tt], [DFO[i]])
            drain()
            if before_tail is not None:
                before_tail()
            headnorm_batch([(i, DFO[i][:, 0:512], DFO[i], SG_[i]) for i in range(n)], 1.0 - lam_init, OT[2])

        def stage_M_begin(l, sup):
            n = len(sup)
            ws = WStream(M_specs(l))
            k.op("dve", lambda: V.memset(SVMa[:], 0.0), [], [SVMa])
            k.op("dve", lambda: V.memset(SVMb[:], 0.0), [], [SVMb])
            for wi in range(2):
                wt = ws.get()
                for i in range(n):
                    pg = proj_tm(wt, i)
                    k.op("act", lambda: A.activation(Y[i][:, 512 + wi * 256:512 + (wi + 1) * 256], pg[:, 0:256], AF.Copy,
                                                     accum_out=SVMa[:, i * 2 + wi:i * 2 + wi + 1]), [pg], [Y[i], SVMa])
                    k.op("act", lambda: A.activation(VN[i][:, wi * 256:(wi + 1) * 256], pg[:, 0:256], AF.Square,
                                                     accum_out=SVMb[:, i * 2 + wi:i * 2 + wi + 1]), [pg], [VN[i], SVMb])
            return lambda: stage_M_rest(l, sup, ws)

        def stage_M_rest(l, sup, ws):
            n = len(sup)
            k.op("dve", lambda: V.reduce_sum(SVM[:, 16:16 + n], SVMa[:, 0:2 * n].rearrange("p (i w) -> p i w", w=2), AX.X), [SVMa], [SVM])
            k.op("dve", lambda: V.reduce_sum(SVM[:, 20:20 + n], SVMb[:, 0:2 * n].rearrange("p (i w) -> p i w", w=2), AX.X), [SVMb], [SVM])
            k.op("dve", lambda: V.tensor_scalar_mul(SVM[:, 16:16 + n], SVM[:, 16:16 + n], 1.0 / 512), [SVM], [SVM])
            k.op("dve", lambda: V.tensor_tensor(SVM[:, 24:24 + n], SVM[:, 16:16 + n], SVM[:, 16:16 + n], ALU.mult), [SVM], [SVM])
            k.op("dve", lambda: V.scalar_tensor_tensor(SVM[:, 20:20 + n], SVM[:, 20:20 + n], 1.0 / 512, SVM[:, 24:24 + n], ALU.mult, ALU.subtract), [SVM], [SVM])
            k.op("act", lambda: A.activation(SVM[:, 20:20 + n], SVM[:, 20:20 + n], AF.Sqrt, bias=col(c_eps), scale=1.0), [SVM, SV], [SVM])
            k.op("dve", lambda: V.reciprocal(SVM[:, 20:20 + n], SVM[:, 20:20 + n]), [SVM], [SVM])
            k.op("dve", lambda: V.scalar_tensor_tensor(SVM[:, 24:24 + n], SVM[:, 16:16 + n], -1.0, SVM[:, 20:20 + n], ALU.mult, ALU.mult), [SVM], [SVM])
            for i in range(n):
                k.op("pool", lambda: P.tensor_scalar(VN[i][:], Y[i][:, 512:1024], SVM[:, 20 + i:21 + i], SVM[:, 24 + i:25 + i], ALU.mult, ALU.add),
                     [Y[i], SVM], [VN[i]])
            for wi in range(2):
                wt = ws.get()
                for i in range(n):
                    pg = proj_tm(wt, i)
                    k.op("act", lambda: A.activation(SG_[i][:, wi * 256:(wi + 1) * 256], pg[:, 0:256], AF.Silu), [pg], [SG_[i]])
            for wi in range(2):
                wt = ws.get()
                for i in range(n):
                    pg = proj_tm(wt, i)
                    cs = slice(wi * 256, (wi + 1) * 256)
                    k.op("dve", lambda: V.tensor_tensor(SG_[i][:, cs], pg[:, 0:256], SG_[i][:, cs], ALU.mult), [pg, SG_[i]], [SG_[i]])
            for i in range(n):
                pg = nxt("pg", PG)
                for g in range(4):
                    gs = slice(g * 128, (g + 1) * 128)
                    mm(pg[:, gs], WST[:, g, :], VN[i][:, gs], True, True, [WST, VN[i]], [pg], g == 3)
                for g in range(4):
                    gs = slice(g * 128, (g + 1) * 128)
                    k.op("dve", lambda: V.scalar_tensor_tensor(OBQ[i % 2][g][:], pg[:, gs], col(c_bs + g), SG_[i][:, gs], ALU.add, ALU.mult),
                         [pg, SV, SG_[i]], [OBQ[i % 2][g]])
                transpose_into(OBQ[i % 2], 4, OT[1], i)

        def stage_G(l, sup):
            n = len(sup)
            ws = WStream(G_specs(l))
            for br in range(3):
                for q in range(4):
                    wl = ws.get()
                    wb = ws.get()
                    qs = slice(q * 256, (q + 1) * 256)
                    for i in range(n):
                        pa = proj_tm(wl, i)
                        sg = nxt("sgm", SGM)
                        k.op("act", lambda: A.activation(sg[:], pa[:, 0:256], AF.Sigmoid), [pa], [sg])
                        pb = proj_tm(wb, i, nk=4, src=OT[br])
                        if br == 0:
                            k.op("dve", lambda: V.tensor_tensor(Y[i][:, qs], pb[:, 0:256], sg[:], ALU.mult), [pb, sg], [Y[i]])
                        else:
                            ft = F1 if i % 2 == 0 else F2
                            k.op("dve", lambda: V.tensor_tensor(ft[:, 0:256], pb[:, 0:256], sg[:], ALU.mult), [pb, sg], [ft])
                            k.op("pool", lambda: P.tensor_tensor(Y[i][:, qs], Y[i][:, qs], ft[:, 0:256], ALU.add), [ft, Y[i]], [Y[i]])
            for i in range(n):
                k.op("act", lambda: A.copy(HB[:], Y[i][:]), [Y[i]], [HB])
                tp = nxt("tp", TP)
                for kc in range(8):
                    tr(tp[:, kc * 128:(kc + 1) * 128], HB[:, kc * 128:(kc + 1) * 128], [HB], [tp], kc == 7)
                k.op("act", lambda: A.copy(yT[:, :, i * 128:(i + 1) * 128], tp[:].rearrange("p (k t) -> p k t", k=8)), [tp], [yT])

        def stage_O(l, sup, is_ctx, xdst, last):
            n = len(sup)
            xsrc_ = x_in if l == 0 else xs_d
            FX = [F1, F2]

            def xres_load(i):
                j = sup[i] - NTC
                rd = [XSD[l][j]] if l > 0 else []
                k.dma("pool", FX[i % 2][:], xsrc_[j * 128:(j + 1) * 128, :], FX[i % 2], reads=rd, writes=[FX[i % 2]])

            if not is_ctx:
                for i in range(min(2, n)):
                    xres_load(i)
            ws = WStream(O_specs(l))
            k.op("dve", lambda: V.memset(SVN[:, 16:32], 0.0), [], [SVN])
            for q in range(4):
                wo = ws.get()
                qs = slice(q * 256, (q + 1) * 256)
                for i in range(n):
                    pg = proj_tm(wo, i, src=yT)
                    k.op("act", lambda: A.activation(Y[i][:, qs], pg[:, 0:256], AF.Copy), [pg], [Y[i]])
                    k.op("act", lambda: A.activation(SG_[i][:, 0:256], pg[:, 0:256], AF.Square, accum_out=SVN[:, 16 + i * 4 + q:17 + i * 4 + q]),
                         [pg], [SG_[i], SVN])
            k.op("dve", lambda: V.reduce_sum(SVN[:, 0:n], SVN[:, 16:16 + 4 * n].rearrange("p (i q) -> p i q", q=4), AX.X), [SVN], [SVN])
            rstd_from(SVN[:, 0:n], n, 1.0 / DM, SVN[:, 8:8 + n], SVN)
            for i, g in enumerate(sup):
                k.op("dve", lambda: V.scalar_tensor_tensor(Y[i][:], Y[i][:], SVN[:, 8 + i:9 + i], GT[:], ALU.mult, ALU.mult), [Y[i], SVN, GT], [Y[i]])
                if is_ctx:
                    k.op("pool", lambda: P.tensor_tensor(Cx[g][:], Cx[g][:], Y[i][:], ALU.add), [Cx[g], Y[i]], [Cx[g]])
                else:
                    fx = FX[i % 2]
                    k.op("pool", lambda: P.tensor_tensor(Y[i][:], Y[i][:], fx[:], ALU.add), [Y[i], fx], [Y[i]])
                    if i + 2 < n:
                        xres_load(i + 2)
                    j = g - NTC
                    if last:
                        k.dma("pool", xdst[j * 128:(j + 1) * 128, :], Y[i][:], Y[i], reads=[Y[i]], is_out=True)
                    else:
                        k.dma("pool", xdst[j * 128:(j + 1) * 128, :], Y[i][:], Y[i], reads=[Y[i]], writes=[XSD[l + 1][j]])

        def pass2(l, sup, is_ctx, xsrc, xdst, last):
            n = len(sup)
            dg = debug and l == 0 and (is_ctx or sup[0] == NTC)
            tag = "c_" if is_ctx else "l_"
            make_hT(l, sup, is_ctx, xsrc)
            if dg:
                dbg(tag + "hT", HT[hcur["i"]][:, :, 0:n * 128], HT[hcur["i"]], [128, 8, n * 128])
            if not is_ctx:
                load_rope(sup)
            stage_R(l, sup, is_ctx, False)
            if dg:
                dbg(tag + "qk_r", QK[:, :, 0:n * 128], QK, [128, 8, n * 128])
                dbg(tag + "oT_ret", OT[0][:, :, 0:n * 128], OT[0], [128, 4, n * 128])
                dbg(tag + "sf", SF[:], SF, [128, 512])
            mrest = {}
            stage_D(l, sup, is_ctx, before_tail=lambda: mrest.update(f=stage_M_begin(l, sup)))
            if dg:
                dbg(tag + "qk_d", QK[:, :, 0:n * 128], QK, [128, 8, n * 128])
                dbg(tag + "oT_diff", OT[2][:, :, 0:n * 128], OT[2], [128, 4, n * 128])
                dbg(tag + "dfo", Y[0][:, 0:512], Y[0], [128, 512])
            mrest["f"]()
            if dg:
                dbg(tag + "oT_mlp", OT[1][:, :, 0:n * 128], OT[1], [128, 4, n * 128])
            stage_G(l, sup)
            if dg:
                dbg(tag + "y", Y[0][:], Y[0], [128, DM])
            stage_O(l, sup, is_ctx, xdst, last)

        csup = list(range(NTC))
        lsups = [list(range(NTC + s * ST, NTC + (s + 1) * ST)) for s in range(NTL // ST)]
        for l in range(depth):
            xseq.extend([(l, sp_[0]) for sp_ in reversed(lsups)])
            xseq.extend([(l, sp_[0]) for sp_ in lsups])
        for l in range(depth):
            last_ = l == depth - 1
            GW["keys"] += setup_specs(l) + p1_specs(l)
            GW["keys"] += R_specs(l, True) if (last_ and depth == DEPTH) else p2_specs(l)
            for _ in lsups:
                GW["keys"] += p1_specs(l)
            for _ in lsups:
                GW["keys"] += p2_specs(l)
        for l in range(depth):
            last = l == depth - 1
            xsrc = x_in if l == 0 else xs_d
            xdst = out_d if last else xs_d
            layer_setup(l)
            load_mod(l, 1)
            pass1(l, csup, True, None)
            if last and depth == DEPTH:
                stage_R(l, csup, True, True)
            else:
                pass2(l, csup, True, None, None, False)
            if debug and l == 0:
                dbg("tc_ctx", TC[:], TC, [128, 512])
                dbg("dkt0", DKT[0][:], DKT[0], [128, 4, 128])
                dbg("va0", VA[0][:], VA[0], [128, 4, 130])
                dbg("mct", MCT[:], MCT, [128, 512])
                dbg("sv", SV[:], SV, [128, 128])
            load_mod(l, 0)
            for sup in reversed(lsups):
                pass1(l, sup, False, xsrc)
            for sup in lsups:
                pass2(l, sup, False, xsrc, xdst, last)
        assert GW["ng"] == len(GW["keys"]), (GW["ng"], len(GW["keys"]))
        k.finish()
        print("instr counts", k.nins, "sems", len(k.sem), "sbuf KiB/partition", k.sb_bytes / 1024)
    nc._dbg_names = dbg_names
    return nc


def _host_consts():
    j = np.arange(128, dtype=np.float64)[:, None]
    i = np.arange(128, dtype=np.float64)[None, :]
    ident = np.eye(128)
    d1 = np.maximum(i - j, 0.0)
    u = (i >= j).astype(np.float64)
    d2 = np.maximum(j - i, 0.0)
    lo = (j >= i).astype(np.float64)
    p1 = np.broadcast_to(i + 1.0, (128, 128))
    p2 = np.broadcast_to(128.0 - i, (128, 128))
    pc = np.concatenate([127.0 - j, j], axis=1)
    cf = np.concatenate([ident, d1, u, d2, lo, p1, p2, pc], axis=1).astype(np.float32)
    sel = np.zeros((2, 2, 128), np.float32)
    sel[0, 0] = 1.0
    sel[1, 1] = 1.0
    n = np.arange(SEQ)
    row = (n // 64).astype(np.float64)
    colp = (n % 64).astype(np.float64)

    def ang(head_dim):
        nf = head_dim // 4
        inv = 10000.0 ** (-np.arange(nf, dtype=np.float64) / nf)
        return np.concatenate([row[:, None] * inv, colp[:, None] * inv], axis=-1)

    a_r = ang(128)
    a_d = ang(64)
    rt = np.zeros((128, 4, SEQ), np.float32)
    d = np.arange(128)
    rt[:, 0, :] = np.cos(a_r)[:, d % 64].T
    rt[:, 1, :] = (np.sin(a_r)[:, d % 64] * np.where(d < 64, -1.0, 1.0)[None, :]).T
    dd = d % 64
    rt[:, 2, :] = np.cos(a_d)[:, dd % 32].T
    rt[:, 3, :] = (np.sin(a_d)[:, dd % 32] * np.where(dd < 32, -1.0, 1.0)[None, :]).T
    pm = np.zeros((128, 256), np.float32)
    d = np.arange(128)
    pm[d, (d + 64) % 128] = 1.0
    pm[d, 128 + (d // 64) * 64 + ((d % 64) + 32) % 64] = 1.0
    return cf, sel, rt, pm


_NC_CACHE = {}


_DBG = {}


def kernel(x, c, ctx, c_ctx, w_mod, b_mod, g_pre, g_post, w_in, ret_decay_logit, mlp_w_s, mlp_b_s,
           diff_lambda_q, diff_lambda_k, w_branch_out, w_out, _cores=8, _depth=DEPTH, _debug=False):
    f = lambda a: np.ascontiguousarray(np.asarray(a, dtype=np.float32))
    x, c, ctx, c_ctx = f(x), f(c), f(ctx), f(c_ctx)
    cf, sel, rt, pm = _host_consts()
    shared = {
        "w_mod": f(w_mod), "b_mod": f(b_mod), "g_pre": f(g_pre), "g_post": f(g_post), "w_in": f(w_in),
        "rdl": f(ret_decay_logit).reshape(DEPTH, 8), "mws": f(mlp_w_s),
        "mbs": np.ascontiguousarray(f(mlp_b_s).transpose(0, 2, 1)),
        "dlq": f(diff_lambda_q).reshape(DEPTH, 128), "dlk": f(diff_lambda_k).reshape(DEPTH, 128),
        "w_bo": f(w_branch_out), "w_out": f(w_out), "cf": cf, "ropet": rt, "pm": pm,
    }
    key = (_depth, _debug)
    if key not in _NC_CACHE:
        _NC_CACHE[key] = build_program(_depth, _debug)
    nc = _NC_CACHE[key]
    in_maps = []
    for b in range(_cores):
        cc = np.stack([c[b], c_ctx], axis=0)
        cct = np.ascontiguousarray(cc.reshape(2, 8, 128).transpose(2, 1, 0))
        m = dict(shared)
        m.update({"x": x[b], "ctx": ctx[b], "cct": cct})
        in_maps.append(m)
    res = run_bass_kernel_spmd(nc, in_maps, core_ids=list(range(_cores)))
    if _debug:
        for nm in nc._dbg_names:
            _DBG[nm] = np.asarray(res.results[0]["dbg_" + nm])
    return np.stack([np.asarray(r["out"], dtype=np.float32) for r in res.results], axis=0)
```

```python
import math
from contextlib import ExitStack

import numpy as np
import concourse.bass as bass
import concourse.mybir as mybir
from concourse.bass_utils import run_bass_kernel_spmd

F32 = mybir.dt.float32
BF16 = mybir.dt.bfloat16
AF = mybir.ActivationFunctionType
ALU = mybir.AluOpType
AX = mybir.AxisListType

DEPTH = 2
DM = 1024
SEQ = 2048
CTX = 256
NTC = CTX // 128
NTL = SEQ // 128
NTT = NTC + NTL
ST = 4
IN_COLS = 8704
EPS = 1e-6
RET_SCALE = 128.0 ** -0.5
USE_WCACHE = True
C_RK, C_RV, C_DK, C_DV, C_RQ, C_RG, C_DQ, C_DG, C_MU, C_MV, C_MG, C_ML = [512 * i for i in range(12)]


class T:
    def __init__(self, h, name):
        self.h = h
        self.name = name
        self.w = {}
        self.r = {}
        self.dsem = None
        self.excl = False

    def __getitem__(self, k):
        return self.h[k]


class KB:
    def __init__(self, nc, stack):
        self.nc = nc
        self.stack = stack
        self.eng = {"pe": nc.tensor, "dve": nc.vector, "act": nc.scalar, "pool": nc.gpsimd, "sp": nc.sync}
        self.sem = {}
        self.cnt = {}
        for e in self.eng:
            self.sem[e] = stack.enter_context(nc.semaphore("s_" + e))
            self.cnt[e] = 0
        self.seen = {e: {} for e in self.eng}
        self.out_events = []
        self.nins = {e: 0 for e in self.eng}

    def sb(self, name, shape, dt):
        self.sb_bytes = getattr(self, "sb_bytes", 0) + int(np.prod(shape[1:])) * (2 if dt == BF16 else 4)
        return T(self.stack.enter_context(self.nc.sbuf_tensor(name, list(shape), dt)), name)

    def ps(self, name, shape, dt=F32):
        t = T(self.stack.enter_context(self.nc.psum_tensor(name, list(shape), dt)), name)
        t.excl = True
        return t

    def _wait(self, e, reads, writes):
        need = {}

        def add(k, v, raw):
            if k == e and e == "pe":
                return
            need[k] = max(need.get(k, 0), v)

        for t in reads:
            for k, v in t.w.items():
                add(k, v, True)
            if t.excl:
                for k, v in t.r.items():
                    if k != e:
                        add(k, v, False)
        for t in writes:
            for k, v in t.w.items():
                add(k, v, False)
            for k, v in t.r.items():
                add(k, v, False)
        for k, v in need.items():
            if k not in self.eng:
                v = self.cnt[k]
            if self.seen[e].get(k, 0) < v:
                assert v <= self.cnt[k], f"wait on unsignaled event {k} {v} {self.cnt[k]}"
                self.eng[e].wait_ge(self.sem[k], v)
                self.seen[e][k] = v

    def _post(self, ev, reads, writes):
        for t in reads:
            t.r[ev[0]] = max(t.r.get(ev[0], 0), ev[1])
        for t in writes:
            t.w[ev[0]] = max(t.w.get(ev[0], 0), ev[1])
            t.r = {}

    def op(self, e, fn, reads=(), writes=(), signal=True):
        self._wait(e, reads, writes)
        ins = fn()
        self.nins[e] += 1
        if signal:
            self.cnt[e] += 1
            ins.then_inc(self.sem[e], 1)
            ev = (e, self.cnt[e])
        else:
            ev = (e, self.cnt[e] + 1)
        self._post(ev, reads, writes)
        return ins

    def dma(self, q, out_ap, in_ap, sb_t, reads=(), writes=(), is_out=False):
        self._wait(q, reads, writes)
        if sb_t.dsem is None:
            sb_t.dsem = {}
        kind = "sw" if q == "pool" else "hw"
        if kind not in sb_t.dsem:
            nm = "d_" + sb_t.name + "_" + kind
            sb_t.dsem[kind] = nm
            self.sem[nm] = self.stack.enter_context(self.nc.semaphore(nm))
            self.cnt[nm] = 0
        k = sb_t.dsem[kind]
        ins = self.eng[q].dma_start(out=out_ap, in_=in_ap)
        self.cnt[k] += 16
        ins.then_inc(self.sem[k], 16)
        ev = (k, self.cnt[k])
        self._post(ev, reads, writes)
        if is_out:
            self.out_events.append(ev)
        return ins

    def finish(self):
        fin = {}
        for k, v in self.out_events:
            fin[k] = self.cnt[k]
        for k, v in fin.items():
            self.eng["sp"].wait_ge(self.sem[k], v)


def build_program(depth=DEPTH, debug=False):
    nc = bass.Bass("TRN2", target_bir_lowering=False)

    def din(name, shape):
        return nc.dram_tensor(name, list(shape), F32, kind="ExternalInput").ap()

    x_in = din("x", [SEQ, DM])
    ctx_in = din("ctx", [CTX, DM])
    cct = din("cct", [128, 8, 2])
    w_mod = din("w_mod", [DEPTH, DM, 3 * DM])
    b_mod = din("b_mod", [DEPTH, 3 * DM])
    g_pre = din("g_pre", [DEPTH, DM])
    g_post = din("g_post", [DEPTH, DM])
    w_in = din("w_in", [DEPTH, DM, IN_COLS])
    rdl = din("rdl", [DEPTH, 8])
    mws = din("mws", [DEPTH, 4, 128, 128])
    mbs = din("mbs", [DEPTH, 128, 4])
    dlq = din("dlq", [DEPTH, 128])
    dlk = din("dlk", [DEPTH, 128])
    w_bo = din("w_bo", [DEPTH, 3, 512, DM])
    w_out = din("w_out", [DEPTH, DM, DM])
    cf = din("cf", [128, 7 * 128 + 2])
    ropet = din("ropet", [128, 4, SEQ])
    pm_in = din("pm", [128, 256])
    out_d = nc.dram_tensor("out", [SEQ, DM], F32, kind="ExternalOutput").ap()
    xs_d = nc.dram_tensor("xs", [SEQ, DM], F32, kind="Internal").ap()
    mr_d = nc.dram_tensor("mrows", [DEPTH, 2, 3 * DM], F32, kind="Internal").ap()
    wbf_in = nc.dram_tensor("wbf_in", [DEPTH, DM, IN_COLS], BF16, kind="Internal").ap()
    wbf_bo = nc.dram_tensor("wbf_bo", [DEPTH, 3, 512, DM], BF16, kind="Internal").ap()
    wbf_out = nc.dram_tensor("wbf_out", [DEPTH, DM, DM], BF16, kind="Internal").ap()

    with ExitStack() as st:
        k = KB(nc, st)
        XSD = [[T(None, f"xsd{l}_{j}") for j in range(NTL)] for l in range(2)]
        MRD = [[T(None, f"mrd{l}_{j}") for j in range(12)] for l in range(DEPTH)]

        Cx = [k.sb(f"cx{g}", [128, DM], F32) for g in range(NTC)]
        Xs = [k.sb(f"xsb{i}", [128, DM], F32) for i in range(ST)]
        Y = [k.sb(f"y{i}", [128, DM], F32) for i in range(ST)]
        F1 = k.sb("f1", [128, DM], F32)
        F2 = k.sb("f2", [128, DM], F32)
        G1 = k.sb("g1", [128, DM], BF16)
        SH = k.sb("sh", [128, DM], BF16)
        GT = k.sb("gt", [128, DM], BF16)
        DKT = [k.sb(f"dkt{g}", [128, 4, 128], BF16) for g in range(NTT)]
        VA = [k.sb(f"va{g}", [128, 4, 130], BF16) for g in range(NTT)]
        TBS = [k.sb(f"tbs{g}", [128, 512], BF16) for g in range(NTT)]
        HT = [k.sb("hTa", [128, 8, ST * 128], BF16), k.sb("hTb", [128, 8, ST * 128], BF16)]
        hcur = {"i": 0}
        pend = {"gen": None, "pos": -1, "dst": None}
        HB = k.sb("hb", [128, DM], BF16)
        W = [k.sb(f"w{i}", [128, 8, 256], BF16) for i in range(4)]
        ZB = [k.sb(f"zb{i}", [128, ST * 128], BF16) for i in range(2)]
        PMR = k.sb("pmr", [128, 128], BF16)
        PMD = k.sb("pmd", [128, 128], BF16)
        RT = k.sb("ropes", [128, 4, ST * 128], BF16)
        QK = k.sb("qk", [128, 8, ST * 128], BF16)
        KTOK = [k.sb(f"ktok{i}", [128, 512], BF16) for i in range(ST)]
        RV = [k.sb(f"rv{i}", [128, 512], BF16) for i in range(ST)]
        SG_ = [k.sb(f"sg{i}", [128, 512], BF16) for i in range(ST)]
        OT = [k.sb(f"oT{b}", [128, 4, ST * 128], BF16) for b in range(3)]
        SMs = [k.sb(f"sm{i}", [128, 512], BF16) for i in range(2)]
        QFs = [k.sb(f"qf{i}", [128, 512], BF16) for i in range(2)]
        QBs = [k.sb(f"qb{i}", [128, 512], BF16) for i in range(2)]
        OBQ = [[k.sb(f"ob{b_}{h_}", [128, 128], BF16) for h_ in range(4)] for b_ in range(2)]
        E_ = [k.sb(f"e{i}", [128, 512], BF16) for i in range(3)]
        SGM = [k.sb(f"sgm{i}", [128, 256], BF16) for i in range(2)]
        SF = k.sb("sf", [128, 512], F32)
        SFB = k.sb("sfb", [128, 512], BF16)
        TC = k.sb("tcur", [128, 512], F32)
        MCT = k.sb("mct", [128, 512], BF16)
        QDF = k.sb("qdf", [128, 512], BF16)
        QDB = k.sb("qdb", [128, 512], BF16)
        IDB = k.sb("idb", [128, 128], BF16)
        WST = k.sb("wst", [128, 4, 128], BF16)
        SV = k.sb("sv", [128, 128], F32)
        R0 = k.sb("r0", [128, 4], F32)
        R1 = k.sb("r1", [128, 4], F32)
        SVMa = k.sb("svma", [128, 8], F32)
        SVMb = k.sb("svmb", [128, 8], F32)
        SVM = k.sb("svm", [128, 32], F32)
        SVH = k.sb("svh", [128, 32], F32)
        SVN = k.sb("svn", [128, 48], F32)
        HB2 = k.sb("hb2", [128, DM], BF16)
        SCC = k.sb("scc", [128, 8, 2], BF16)
        CCF = k.sb("ccf", [128, 8, 2], F32)
        c_eps, c_nlam, c_ss, c_rs, c_lg, c_kdec, c_cdec, c_bs, c_tmp = 0, 1, 4, 12, 20, 28, 36, 44, 48
        c_ss4, c_e2, c_mean, c_sum2 = 56, 76, 80, 84

        PG = [k.ps(f"pg{i}", [128, 512]) for i in range(3)]
        OA = [k.ps(f"oa{i}", [128, 512]) for i in range(2)]
        KVP = k.ps("kvp", [128, 512])
        TP = [k.ps(f"tp{i}", [128, 1024], BF16) for i in range(2)]
        rr = {"pg": 0, "tp": 0, "w": 0, "e": 0, "sgm": 0, "mrb": 0, "zb": 0, "pgx": 0}
        PGX = PG + OA

        def nxt(key, lst):
            t = lst[rr[key] % len(lst)]
            rr[key] += 1
            return t

        dbg_names = []

        def dbg(name, ap, t, shape):
            if not debug:
                return
            d = nc.dram_tensor("dbg_" + name, list(shape), F32, kind="ExternalOutput").ap()
            k.dma("pool", d, ap, t, reads=[t], is_out=True)
            dbg_names.append(name)

        V = nc.vector
        A = nc.scalar
        P = nc.gpsimd
        PE = nc.tensor

        def mm(out, lhsT, rhs, start, stop, reads, writes, signal, sgc=False):
            k.op("pe", lambda: PE.matmul(out, lhsT, rhs, start=start, stop=stop, skip_group_check=sgc), reads, writes, signal)

        def tr(out, in_, reads, writes, signal):
            k.op("pe", lambda: PE.transpose(out, in_, IDB[:]), list(reads) + [IDB], writes, signal)

        def col(c, n=1):
            return SV[:, c:c + n]

        k.dma("sp", F2[:, 0:898], cf, F2, writes=[F2])
        k.dma("sp", CCF[:], cct, CCF, writes=[CCF])
        for g in range(NTC):
            k.dma("sp", Cx[g][:], ctx_in[g * 128:(g + 1) * 128, :], Cx[g], writes=[Cx[g]])
        k.dma("pool", PMR[:], pm_in[:, 0:128], PMR, writes=[PMR])
        k.dma("pool", PMD[:], pm_in[:, 128:256], PMD, writes=[PMD])
        k.op("dve", lambda: V.tensor_copy(IDB[:], F2[:, 0:128]), [F2], [IDB])
        k.op("dve", lambda: V.memset(SV[:], 0.0), [], [SV])
        k.op("dve", lambda: V.memset(col(c_eps), EPS), [], [SV])
        for g in range(NTT):
            k.op("pool", lambda: P.memset(VA[g][:], 1.0), [], [VA[g]])
        k.op("act", lambda: A.activation(CCF[:], CCF[:], AF.Silu), [CCF], [CCF])
        k.op("act", lambda: A.copy(SCC[:], CCF[:]), [CCF], [SCC])
        CF = F2
        D1, U_, D2, L_, P1, P2 = [F2[:, (i + 1) * 128:(i + 2) * 128] for i in range(6)]
        PC = F2[:, 7 * 128:7 * 128 + 2]

        GW = {"keys": [], "tiles": {}, "ni": 0, "ng": 0}
        WDEPTH = 2
        wcache = {}

        def resolve(key):
            kind = key[0]
            if kind == "mod":
                _, l_, b_ = key
                return w_mod[l_][:, b_ * 256:(b_ + 1) * 256], 8, None
            if kind == "in":
                _, l_, c0 = key
                return w_in[l_][:, c0:c0 + 256], 8, wbf_in[l_][:, c0:c0 + 256]
            if kind == "bo":
                _, l_, br_, q_ = key
                return w_bo[l_, br_][:, q_ * 256:(q_ + 1) * 256], 4, wbf_bo[l_, br_][:, q_ * 256:(q_ + 1) * 256]
            _, l_, q_ = key
            return w_out[l_][:, q_ * 256:(q_ + 1) * 256], 8, wbf_out[l_][:, q_ * 256:(q_ + 1) * 256]

        def gw_issue(key):
            src2d, nk, dst2d = resolve(key)
            wt = nxt("w", W)
            if dst2d is not None and key in wcache:
                k.dma("sp", wt[:, 0:nk, :], dst2d.rearrange("(kc p) c -> p kc c", p=128), wt, reads=[wcache[key]], writes=[wt])
            else:
                k.dma("pool", wt[:, 0:nk, :], src2d.rearrange("(kc p) c -> p kc c", p=128), wt, writes=[wt])
                if dst2d is not None and USE_WCACHE:
                    tobj = T(None, "wc")
                    k.dma("sp", dst2d.rearrange("(kc p) c -> p kc c", p=128), wt[:, 0:nk, :], wt, reads=[wt], writes=[tobj])
                    wcache[key] = tobj
            return wt

        def gw_issue_upto(n_):
            while GW["ni"] < min(n_, len(GW["keys"])):
                GW["tiles"][GW["ni"]] = gw_issue(GW["keys"][GW["ni"]])
                GW["ni"] += 1

        class WStream:
            def __init__(self, specs):
                self.specs = specs
                self.i = 0
                assert GW["keys"][GW["ng"]:GW["ng"] + len(specs)] == list(specs), (specs[:2], GW["keys"][GW["ng"]:GW["ng"] + 2])
                gw_issue_upto(GW["ng"] + WDEPTH)

            def get(self):
                key = self.specs[self.i]
                self.i += 1
                i_ = GW["ng"]
                assert GW["keys"][i_] == key
                gw_issue_upto(i_ + 1 + WDEPTH)
                GW["ng"] += 1
                return GW["tiles"].pop(i_)

        def win_spec(l, c0):
            return ("in", l, c0)

        def setup_specs(l):
            return [("mod", l, b) for b in range(12)]

        def p1_specs(l):
            return [win_spec(l, C_RK), win_spec(l, C_RK + 256), win_spec(l, C_RV), win_spec(l, C_RV + 256),
                    win_spec(l, C_DK), win_spec(l, C_DK + 256), win_spec(l, C_DV), win_spec(l, C_DV + 256)]

        def R_specs(l, states_only):
            specs = [win_spec(l, C_RK), win_spec(l, C_RK + 256), win_spec(l, C_RV), win_spec(l, C_RV + 256)]
            if not states_only:
                specs += [win_spec(l, C_RQ), win_spec(l, C_RQ + 256), win_spec(l, C_RG), win_spec(l, C_RG + 256)]
            return specs

        def D_specs(l):
            return [win_spec(l, C_DQ), win_spec(l, C_DQ + 256), win_spec(l, C_DG), win_spec(l, C_DG + 256)]

        def M_specs(l):
            return [win_spec(l, C_MV), win_spec(l, C_MV + 256), win_spec(l, C_MG), win_spec(l, C_MG + 256),
                    win_spec(l, C_MU), win_spec(l, C_MU + 256)]

        def G_specs(l):
            specs = []
            for br in range(3):
                for q in range(4):
                    specs.append(win_spec(l, C_ML + br * DM + q * 256))
                    specs.append(("bo", l, br, q))
            return specs

        def O_specs(l):
            return [("out", l, q) for q in range(4)]

        def p2_specs(l):
            return R_specs(l, False) + D_specs(l) + M_specs(l) + G_specs(l) + O_specs(l)

        def layer_setup(l):
            ws = WStream(setup_specs(l))
            for b in range(12):
                wt = ws.get()
                pg = nxt("pg", PG)
                for kc in range(8):
                    mm(pg[0:2, 0:256], SCC[:, kc, :], wt[:, kc, :], kc == 0, kc == 7, [SCC, wt], [pg], kc == 7)
                k.dma("sp", F1[0:2, 512:768], b_mod[l, b * 256:(b + 1) * 256].partition_broadcast(2), F1, writes=[F1])
                k.op("dve", lambda: V.tensor_tensor(F1[0:2, 768:1024], pg[0:2, 0:256], F1[0:2, 512:768], ALU.add), [pg, F1], [F1])
                k.dma("sp", mr_d[l][:, b * 256:(b + 1) * 256], F1[0:2, 768:1024], F1, reads=[F1], writes=[MRD[l][b]])
            k.dma("sp", F2[:, 0:898], cf, F2, writes=[F2])
            k.dma("sp", col(c_lg, 8), rdl[l].partition_broadcast(128), SV, writes=[SV])
            k.op("act", lambda: A.activation(col(c_lg, 8), col(c_lg, 8), AF.Sigmoid), [SV], [SV])
            k.op("act", lambda: A.activation(col(c_lg, 8), col(c_lg, 8), AF.Ln), [SV], [SV])
            k.op("act", lambda: A.activation(col(c_cdec, 8), col(c_lg, 8), AF.Exp, scale=128.0), [SV], [SV])
            for h in range(4):
                hs = slice(h * 128, (h + 1) * 128)
                k.op("act", lambda: A.activation(F1[:, 0:128], D1, AF.Exp, scale=col(c_lg + h)), [CF, SV], [F1])
                k.op("act", lambda: A.activation(F1[:, 128:256], D2, AF.Exp, scale=col(c_lg + 4 + h)), [CF, SV], [F1])
                k.op("dve", lambda: V.tensor_tensor(F1[:, 0:128], F1[:, 0:128], U_, ALU.mult), [F1, CF], [F1])
                k.op("dve", lambda: V.tensor_tensor(F1[:, 128:256], F1[:, 128:256], L_, ALU.mult), [F1, CF], [F1])
                k.op("dve", lambda: V.tensor_tensor(F1[:, 0:128], F1[:, 0:128], F1[:, 128:256], ALU.add), [F1], [F1])
                k.op("dve", lambda: V.tensor_scalar_mul(MCT[:, hs], F1[:, 0:128], RET_SCALE), [F1], [MCT])
                k.op("act", lambda: A.activation(QDF[:, hs], P1, AF.Exp, scale=col(c_lg + h)), [CF, SV], [QDF])
                k.op("act", lambda: A.activation(QDB[:, hs], P2, AF.Exp, scale=col(c_lg + 4 + h)), [CF, SV], [QDB])
                k.op("act", lambda: A.activation(col(c_kdec + h), PC[:, 0:1], AF.Exp, scale=col(c_lg + h)), [CF, SV], [SV])
                k.op("act", lambda: A.activation(col(c_kdec + 4 + h), PC[:, 1:2], AF.Exp, scale=col(c_lg + 4 + h)), [CF, SV], [SV])
            k.op("dve", lambda: V.tensor_scalar_mul(col(c_kdec, 8), col(c_kdec, 8), RET_SCALE), [SV], [SV])
            lam_init = 0.8 - 0.6 * math.exp(-0.3 * l)
            LQ = F1[:, 256:384]
            LK = F1[:, 384:512]
            k.dma("sp", LQ, dlq[l].partition_broadcast(128), F1, writes=[F1])
            k.dma("sp", LK, dlk[l].partition_broadcast(128), F1, writes=[F1])
            k.op("dve", lambda: V.tensor_tensor(LQ, LQ, LK, ALU.mult), [F1], [F1])
            k.op("dve", lambda: V.reduce_sum(col(c_e2, 2), LQ.rearrange("p (a b) -> p a b", a=2), AX.X), [F1], [SV])
            k.op("act", lambda: A.activation(col(c_e2, 2), col(c_e2, 2), AF.Exp), [SV], [SV])
            k.op("dve", lambda: V.tensor_tensor(col(c_nlam), col(c_e2 + 1), col(c_e2), ALU.subtract), [SV], [SV])
            k.op("dve", lambda: V.tensor_scalar_add(col(c_nlam), col(c_nlam), -lam_init), [SV], [SV])
            WSN_t = nxt("w", W)
            WSN = WSN_t[:, 0:2, :].rearrange("p k (g j) -> p (k g) j", j=128)
            k.dma("pool", WSN, mws[l].rearrange("g i j -> i g j"), WSN_t, writes=[WSN_t])
            k.dma("sp", col(c_bs, 4), mbs[l], SV, writes=[SV])
            tp = nxt("tp", TP)
            for g in range(4):
                tr(tp[:, g * 128:(g + 1) * 128], WSN[:, g, :], [WSN_t], [tp], g == 3)
            k.op("dve", lambda: V.tensor_copy(WST[:].rearrange("p g j -> p (g j)"), tp[:, 0:512]), [tp], [WST])
            k.op("dve", lambda: V.memset(SF[:], 0.0), [], [SF])
            k.op("dve", lambda: V.memset(TC[:], 0.0), [], [TC])
            k.op("pool", lambda: P.memset(SFB[:], 0.0), [], [SFB])

        def load_mod(l, s):
            st_ = [Y[0], Y[1], Y[2], Y[3], F1]
            k.dma("sp", st_[0][:], mr_d[l][s, 0:DM].partition_broadcast(128), st_[0], reads=MRD[l][0:4], writes=[st_[0]])
            k.dma("sp", st_[1][:], mr_d[l][s, DM:2 * DM].partition_broadcast(128), st_[1], reads=MRD[l][4:8], writes=[st_[1]])
            k.dma("sp", st_[2][:], g_pre[l].partition_broadcast(128), st_[2], writes=[st_[2]])
            k.dma("sp", st_[3][:], mr_d[l][s, 2 * DM:3 * DM].partition_broadcast(128), st_[3], reads=MRD[l][8:12], writes=[st_[3]])
            k.dma("sp", st_[4][:], g_post[l].partition_broadcast(128), st_[4], writes=[st_[4]])
            k.op("act", lambda: A.copy(SH[:], st_[0][:]), [st_[0]], [SH])
            k.op("dve", lambda: V.scalar_tensor_tensor(G1[:], st_[1][:], 1.0, st_[2][:], ALU.add, ALU.mult), [st_[1], st_[2]], [G1])
            k.op("dve", lambda: V.tensor_tensor(GT[:], st_[3][:], st_[4][:], ALU.mult), [st_[3], st_[4]], [GT])

        def rstd_from(ss_ap, n, inv_n, out_ap, t=None):
            t = SV if t is None else t
            k.op("act", lambda: A.activation(out_ap, ss_ap, AF.Sqrt, bias=col(c_eps), scale=inv_n), [t, SV], [t])
            k.op("dve", lambda: V.reciprocal(out_ap, out_ap), [t], [t])

        xseq = []
        xstate = {"pos": 0, "loaded": None}

        def x_load(l, sup):
            xsrc_ = x_in if l == 0 else xs_d
            for i, g in enumerate(sup):
                j = g - NTC
                rd = [XSD[l][j]] if l > 0 else []
                k.dma("sp", Xs[i][:], xsrc_[j * 128:(j + 1) * 128, :], Xs[i], reads=rd, writes=[Xs[i]])

        def make_hT_gen(l, sup, is_ctx, dst, tmps, hbs):
            n = len(sup)
            src = []
            if not is_ctx:
                assert xseq[xstate["pos"]] == (l, sup[0])
                if xstate["loaded"] != xstate["pos"]:
                    x_load(l, sup)
                xstate["pos"] += 1
            for i, g in enumerate(sup):
                src.append(Cx[g] if is_ctx else Xs[i])
            k.op("dve", lambda: V.memset(SVN[:, 0:n], 0.0), [], [SVN])
            for i in range(n):
                Hj = hbs[i % len(hbs)]
                k.op("act", lambda: A.activation(Hj[:], src[i][:], AF.Square, accum_out=SVN[:, i:i + 1]), [src[i]], [Hj, SVN])
            rstd_from(SVN[:, 0:n], n, 1.0 / DM, SVN[:, 8:8 + n], SVN)
            yield
            for i in range(n):
                Fx = tmps[i % len(tmps)]
                Hx = hbs[i % len(hbs)]
                k.op("dve", lambda: V.scalar_tensor_tensor(Fx[:], src[i][:], SVN[:, 8 + i:9 + i], G1[:], ALU.mult, ALU.mult),
                     [src[i], SVN, G1], [Fx])
                k.op("pool", lambda: P.tensor_tensor(Hx[:], Fx[:], SH[:], ALU.add), [Fx, SH], [Hx])
                yield
                tp = nxt("tp", TP)
                for kc in range(8):
                    tr(tp[:, kc * 128:(kc + 1) * 128], Hx[:, kc * 128:(kc + 1) * 128], [Hx], [tp], kc == 7)
                k.op("act", lambda: A.copy(dst[:, :, i * 128:(i + 1) * 128], tp[:].rearrange("p (k t) -> p k t", k=8)),
                     [tp], [dst])
                yield
            if not is_ctx and xstate["pos"] < len(xseq):
                nl, ns0 = xseq[xstate["pos"]]
                nsup = list(range(ns0, ns0 + ST))
                if nl == 0 or all(len(XSD[nl][g - NTC].w) > 0 for g in nsup):
                    x_load(nl, nsup)
                    xstate["loaded"] = xstate["pos"]

        def tick():
            if pend["gen"] is not None:
                try:
                    next(pend["gen"])
                except StopIteration:
                    pend["gen"] = None

        def drain():
            while pend["gen"] is not None:
                tick()

        def make_hT(l, sup, is_ctx, xsrc):
            if (not is_ctx) and pend["pos"] == xstate["pos"] - 1 and pend["dst"] is not None and pend["key"] == (l, sup[0]):
                drain()
            else:
                drain()
                dst = HT[1 - hcur["i"]]
                pend.update(gen=make_hT_gen(l, sup, is_ctx, dst, [F1, F2], [HB, HB2]), dst=dst, key=(l, sup[0]))
                pend["pos"] = xstate["pos"]
                drain()
            hcur["i"] = HT.index(pend["dst"])
            pend["dst"] = None

        def schedule_next_hT(l, tmps):
            if xstate["pos"] >= len(xseq):
                return
            nl, ns0 = xseq[xstate["pos"]]
            if nl != l:
                return
            nsup = list(range(ns0, ns0 + ST))
            dst = HT[1 - hcur["i"]]
            pend.update(gen=make_hT_gen(nl, nsup, False, dst, tmps, [HB, HB2]), dst=dst, key=(nl, ns0))
            pend["pos"] = xstate["pos"]
            tick()

        rope_state = {"loaded": None}

        def prefetch_rope_next(sup, step):
            idx = lsups.index(sup) + step
            if 0 <= idx < len(lsups):
                load_rope(lsups[idx])

        def load_rope(sup):
            if rope_state["loaded"] == sup[0]:
                return
            rope_state["loaded"] = sup[0]
            j0 = sup[0] - NTC
            n = len(sup)
            k.dma("pool", RT[:, :, 0:n * 128], ropet[:, :, j0 * 128:(j0 + n) * 128], RT, writes=[RT])

        def proj_fm_group(ws, dst, NT, is_ctx, kind, split=False, dkt_sup=None, hook=None):
            ci, si = (0, 1) if kind == "ret" else (2, 3)
            pm = PMR if kind == "ret" else PMD
            if split:
                k.op("pool", lambda: P.memset(QK[64:128, 0:4, 0:NT], 0.0), [], [QK])
                k.op("pool", lambda: P.memset(QK[0:64, 4:8, 0:NT], 0.0), [], [QK])
            wts = {}

            def proj(hc):
                wi, c = divmod(hc, 2)
                if c == 0:
                    wts[wi] = ws.get()
                wt = wts[wi]
                tick()
                hT = HT[hcur["i"]]
                pz = PG[hc % 2]
                for kc in range(8):
                    mm(pz[:, 0:NT], wt[:, kc, c * 128:(c + 1) * 128], hT[:, kc, 0:NT], kc == 0, kc == 7, [wt, hT], [pz], kc == 7)
                return pz

            def finish(hc, pz):
                outs = [(slice(0, 128), dst + hc)] if not split else [(slice(0, 64), hc), (slice(64, 128), 4 + hc)]
                if is_ctx:
                    if dkt_sup is None:
                        for rs_, oc in outs:
                            k.op("act", lambda: A.copy(QK[rs_, oc, 0:NT], pz[rs_, 0:NT]), [pz], [QK])
                    else:
                        for i, g in enumerate(dkt_sup):
                            k.op("act", lambda: A.copy(DKT[g][:, hc, :], pz[:, i * 128:(i + 1) * 128]), [pz], [DKT[g]])
                    return
                zb = nxt("zb", ZB)
                k.op("act", lambda: A.copy(zb[:, 0:NT], pz[:, 0:NT]), [pz], [zb])
                pr = PG[2]
                mm(pr[:, 0:NT], pm[:], zb[:, 0:NT], True, True, [pm, zb], [pr], True)
                Fa, Fb = (F1, F2) if hc % 2 == 0 else (Y[2], Y[3])
                k.op("dve", lambda: V.tensor_tensor(Fa[:, 0:NT], pz[:, 0:NT], RT[:, ci, 0:NT], ALU.mult), [pz, RT], [Fa])
                k.op("dve", lambda: V.tensor_tensor(Fb[:, 0:NT], pr[:, 0:NT], RT[:, si, 0:NT], ALU.mult), [pr, RT], [Fb])
                if dkt_sup is None:
                    for rs_, oc in outs:
                        k.op("pool", lambda: P.tensor_tensor(QK[rs_, oc, 0:NT], Fa[rs_, 0:NT], Fb[rs_, 0:NT], ALU.add), [Fa, Fb], [QK])
                else:
                    for i, g in enumerate(dkt_sup):
                        ts = slice(i * 128, (i + 1) * 128)
                        k.op("pool", lambda: P.tensor_tensor(DKT[g][:, hc, :], Fa[:, ts], Fb[:, ts], ALU.add), [Fa, Fb], [DKT[g]])

            pz_prev = proj(0)
            if hook is not None:
                hook(0)
            for hc in range(1, 4):
                pz_cur = proj(hc)
                if hook is not None:
                    hook(hc)
                finish(hc - 1, pz_prev)
                pz_prev = pz_cur
            finish(3, pz_prev)

        dst_t = [None]
        VN = KTOK
        DFO = Y
        yT = QK

        def proj_tm(wt, i, nk=8, src=None):
            pg = nxt("pgx", PGX)
            s = HT[hcur["i"]] if src is None else src
            for kc in range(nk):
                mm(pg[:, 0:256], s[:, kc, i * 128:(i + 1) * 128], wt[:, kc, :], kc == 0, kc == nk - 1, [wt, s], [pg], kc == nk - 1)
            return pg

        def k_side(l, ws, sup, is_ctx, dirn):
            n = len(sup)
            NT = n * 128
            proj_fm_group(ws, 4, NT, is_ctx, "ret")
            for i in range(n):
                tp = nxt("tp", TP)
                for h in range(4):
                    tr(tp[:, h * 128:(h + 1) * 128], QK[:, 4 + h, i * 128:(i + 1) * 128], [QK], [tp], h == 3)
                k.op("dve", lambda: V.tensor_tensor(KTOK[i][:].rearrange("p (h d) -> p h d", h=4), tp[:, 0:512].rearrange("p (h d) -> p h d", h=4),
                                                    col(c_kdec + 4 * dirn, 4).unsqueeze(2).to_broadcast([128, 4, 128]), ALU.mult), [tp, SV], [KTOK[i]])
            for wi in range(2):
                wt = ws.get()
                for i in range(n):
                    pg = proj_tm(wt, i)
                    k.op("act", lambda: A.copy(RV[i][:, wi * 256:(wi + 1) * 256], pg[:, 0:256]), [pg], [RV[i]])

        def kv_update(i, state, dirn):
            for h in range(4):
                hs = slice(h * 128, (h + 1) * 128)
                mm(KVP[:, hs], KTOK[i][:, hs], RV[i][:, hs], True, True, [KTOK[i], RV[i]], [KVP], h == 3)
            for h in range(4):
                hs = slice(h * 128, (h + 1) * 128)
                k.op("dve", lambda: V.scalar_tensor_tensor(state[:, hs], state[:, hs], col(c_cdec + 4 * dirn + h), KVP[:, hs], ALU.mult, ALU.add),
                     [state, SV, KVP], [state])

        def headnorm_batch(items, const, dstT, do_tr=True):
            nI = len(items)
            for j, (i, src_ap, src_t, gate_t) in enumerate(items):
                half = F2[:, (j % 2) * 512:(j % 2 + 1) * 512]
                k.op("act", lambda: A.activation(half, src_ap, AF.Square), [src_t], [F2])
                k.op("dve", lambda: V.reduce_sum(SVH[:, 4 * j:4 * j + 4], half.rearrange("p (h e) -> p h e", h=4), AX.X), [F2], [SVH])
            k.op("act", lambda: A.activation(SVH[:, 0:4 * nI], SVH[:, 0:4 * nI], AF.Sqrt, bias=col(c_eps), scale=1.0 / 128), [SVH, SV], [SVH])
            k.op("dve", lambda: V.reciprocal(SVH[:, 0:4 * nI], SVH[:, 0:4 * nI]), [SVH], [SVH])
            for j, (i, src_ap, src_t, gate_t) in enumerate(items):
                ob = OBQ[i % 2]
                for h in range(4):
                    hs = slice(h * 128, (h + 1) * 128)
                    if const == 1.0:
                        k.op("dve", lambda: V.scalar_tensor_tensor(ob[h][:], src_ap[:, hs], SVH[:, 4 * j + h:4 * j + h + 1], gate_t[:, hs],
                                                                   ALU.mult, ALU.mult), [src_t, SVH, gate_t], [ob[h]])
                    else:
                        k.op("dve", lambda: V.tensor_scalar(F1[:, 512 + h * 128:512 + (h + 1) * 128], src_ap[:, hs], SVH[:, 4 * j + h:4 * j + h + 1], const,
                                                            ALU.mult, ALU.mult), [src_t, SVH], [F1])
                if const != 1.0:
                    for h in range(4):
                        k.op("pool", lambda: P.tensor_tensor(ob[h][:], F1[:, 512 + h * 128:512 + (h + 1) * 128], gate_t[:, h * 128:(h + 1) * 128], ALU.mult),
                             [F1, gate_t], [ob[h]])
                if do_tr:
                    transpose_into(ob, 4, dstT, i)

        def transpose_into(src_ts, nchunks, dstT, i):
            tp = nxt("tp", TP)
            for c in range(nchunks):
                tr(tp[:, c * 128:(c + 1) * 128], src_ts[c][:], [src_ts[c]], [tp], c == nchunks - 1)
            k.op("act", lambda: A.copy(dstT[:, :, i * 128:(i + 1) * 128], tp[:, 0:nchunks * 128].rearrange("p (c t) -> p c t", c=nchunks)),
                 [tp], [dstT])

        def pass1(l, sup, is_ctx, xsrc):
            n = len(sup)
            NT = n * 128
            make_hT(l, sup, is_ctx, xsrc)
            if not is_ctx:
                load_rope(sup)
                schedule_next_hT(l, [Y[0], Y[1]])
            ws = WStream(p1_specs(l))
            k_side(l, ws, sup, is_ctx, 1)
            order = list(reversed(range(n)))

            def chain_step(j):
                if j < n:
                    i = order[j]
                    g = sup[i]
                    k.op("act", lambda: A.copy(TBS[g][:], TC[:]), [TC], [TBS[g]])
                    kv_update(i, TC, 1)

            proj_fm_group(ws, 0, NT, is_ctx, "diff", dkt_sup=sup, hook=chain_step)
            if not is_ctx:
                prefetch_rope_next(sup, -1)
            for wi in range(2):
                wt = ws.get()
                for i, g in enumerate(sup):
                    pg = proj_tm(wt, i)
                    k.op("act", lambda: A.copy(VA[g][:, 2 * wi:2 * wi + 2, 0:128], pg[:, 0:256].rearrange("p (h e) -> p h e", h=2)),
                         [pg], [VA[g]])

        def stage_R(l, sup, is_ctx, states_only):
            n = len(sup)
            NT = n * 128
            ws = WStream(R_specs(l, states_only))
            k_side(l, ws, sup, is_ctx, 0)
            if not states_only:
                proj_fm_group(ws, 0, NT, is_ctx, "ret")
                for wi in range(2):
                    wt = ws.get()
                    for i in range(n):
                        pg = proj_tm(wt, i)
                        k.op("act", lambda: A.activation(SG_[i][:, wi * 256:(wi + 1) * 256], pg[:, 0:256], AF.Silu), [pg], [SG_[i]])
            def prep(i):
                ts = slice(i * 128, (i + 1) * 128)
                ps = nxt("pg", PG)
                for h in range(4):
                    hs = slice(h * 128, (h + 1) * 128)
                    mm(ps[:, hs], QK[:, 4 + h, ts], QK[:, h, ts], True, True, [QK], [ps], h == 3)
                SM, QF, QB = SMs[i % 2], QFs[i % 2], QBs[i % 2]
                k.op("dve", lambda: V.tensor_tensor(SM[:], ps[:], MCT[:], ALU.mult), [ps, MCT], [SM])
                k.op("pool", lambda: P.tensor_tensor(QF[:].rearrange("p (h t) -> p h t", h=4), QK[:, 0:4, ts],
                                                     QDF[:].rearrange("p (h t) -> p h t", h=4), ALU.mult), [QK, QDF], [QF])
                k.op("pool", lambda: P.tensor_tensor(QB[:].rearrange("p (h t) -> p h t", h=4), QK[:, 0:4, ts],
                                                     QDB[:].rearrange("p (h t) -> p h t", h=4), ALU.mult), [QK, QDB], [QB])

            if not states_only:
                prep(0)
            for i, g in enumerate(sup):
                if not states_only and i + 1 < n:
                    prep(i + 1)
                kv_update(i, SF, 0)
                if not states_only:
                    SM, QF, QB = SMs[i % 2], QFs[i % 2], QBs[i % 2]
                    po = nxt("pg", PG)
                    for h in range(4):
                        hs = slice(h * 128, (h + 1) * 128)
                        mm(po[:, hs], SM[:, hs], RV[i][:, hs], True, False, [SM, RV[i]], [po], False)
                        mm(po[:, hs], QF[:, hs], SFB[:, hs], False, False, [QF, SFB], [po], False)
                        mm(po[:, hs], QB[:, hs], TBS[g][:, hs], False, True, [QB, TBS[g]], [po], h == 3)
                k.op("act", lambda: A.copy(SFB[:], SF[:]), [SF], [SFB])
                if not states_only:
                    headnorm_batch([(i, po[:], po, SG_[i])], 1.0, OT[0], do_tr=False)
                    if i > 0:
                        transpose_into(OBQ[(i - 1) % 2], 4, OT[0], i - 1)
            if not states_only:
                transpose_into(OBQ[(n - 1) % 2], 4, OT[0], n - 1)

        def stage_D(l, sup, is_ctx, before_tail=None):
            n = len(sup)
            NT = n * 128
            lam_init = 0.8 - 0.6 * math.exp(-0.3 * l)
            krange = list(range(NTC)) if is_ctx else list(range(NTT))
            ws = WStream(D_specs(l))
            proj_fm_group(ws, 0, NT, is_ctx, "diff", split=True)
            if not is_ctx:
                prefetch_rope_next(sup, +1)
            for wi in range(2):
                wt = ws.get()
                for i in range(n):
                    pg = proj_tm(wt, i)
                    k.op("act", lambda: A.activation(SG_[i][:, wi * 256:(wi + 1) * 256], pg[:, 0:256], AF.Silu), [pg], [SG_[i]])
            if not is_ctx:
                schedule_next_hT(l, [F2])
            steps = [(h, m, ki, kt) for h in range(4) for m in range(2) for ki, kt in enumerate(krange)]
            ps_of = {}
            LOOK = 2

            def emit_st(s_):
                h_, m_, ki_, kt_ = steps[s_]
                ps_ = nxt("pg", PG)
                mm(ps_[:, 0:NT], DKT[kt_][:, h_, :], QK[:, 4 * m_ + h_, 0:NT], True, True, [DKT[kt_], QK], [ps_], True)
                ps_of[s_] = ps_

            for s_ in range(min(LOOK, len(steps))):
                emit_st(s_)
            for s_ in range(len(steps)):
                h, m, ki, kt = steps[s_]
                ps = ps_of.pop(s_)
                e = nxt("e", E_)
                k.op("act", lambda: A.activation(e[:, 0:NT], ps[:, 0:NT], AF.Exp, scale=0.125), [ps], [e])
                if s_ + LOOK < len(steps):
                    emit_st(s_ + LOOK)
                if s_ % 8 == 7:
                    tick()
                for i in range(n):
                    oa = OA[i // 2]
                    o0 = (i % 2) * 129
                    mm(oa[:, o0:o0 + 129], e[:, i * 128:(i + 1) * 128], VA[kt][:, h, 0:129], ki == 0 and i % 2 == 0,
                       ki == len(krange) - 1, [e, VA[kt]], [oa], i == n - 1, sgc=True)
                if ki != len(krange) - 1:
                    continue
                nb = (n + 1) // 2
                for b_ in range(nb):
                    stg = Y[2 * m + b_]
                    k.op("dve", lambda: V.tensor_copy(stg[:, 512:512 + 258], OA[b_][:, 0:258]), [OA[b_]], [stg])
                if m == 0:
                    continue
                for b_ in range(nb):
                    nb2 = min(2, n - 2 * b_)
                    sv0 = Y[b_][:, 512:512 + 258].rearrange("p (t c) -> p t c", c=129)[:, 0:nb2, 128]
                    sv1 = Y[2 + b_][:, 512:512 + 258].rearrange("p (t c) -> p t c", c=129)[:, 0:nb2, 128]
                    k.op("dve", lambda: V.reciprocal(R0[:, 2 * b_:2 * b_ + nb2], sv0), [Y[b_]], [R0])
                    k.op("dve", lambda: V.reciprocal(R1[:, 2 * b_:2 * b_ + nb2], sv1), [Y[2 + b_]], [R1])
                k.op("dve", lambda: V.tensor_scalar_mul(R1[:, 0:n], R1[:, 0:n], col(c_nlam)), [R1, SV], [R1])
                for i in range(n):
                    s0 = Y[i // 2]
                    s1 = Y[2 + i // 2]
                    o0 = 512 + (i % 2) * 129
                    tt = SGM[i % 2]
                    ttv = tt[:].bitcast(F32)
                    k.op("dve", lambda: V.tensor_scalar_mul(ttv, s0[:, o0:o0 + 128], R0[:, i:i + 1]), [s0, R0], [tt])
                    k.op("dve", lambda: V.scalar_tensor_tensor(DFO[i][:, h * 128:(h + 1) * 128], s1[:, o0:o0 + 128], R1[:, i:i + 1], ttv,
                                                               ALU.mult, ALU.add), [s1, R1, tt], [DFO[i]])
            drain()
            if before_tail is not None:
                before_tail()
            headnorm_batch([(i, DFO[i][:, 0:512], DFO[i], SG_[i]) for i in range(n)], 1.0 - lam_init, OT[2])

        def stage_M_begin(l, sup):
            n = len(sup)
            ws = WStream(M_specs(l))
            k.op("dve", lambda: V.memset(SVMa[:], 0.0), [], [SVMa])
            k.op("dve", lambda: V.memset(SVMb[:], 0.0), [], [SVMb])
            for wi in range(2):
                wt = ws.get()
                for i in range(n):
                    pg = proj_tm(wt, i)
                    k.op("act", lambda: A.activation(Y[i][:, 512 + wi * 256:512 + (wi + 1) * 256], pg[:, 0:256], AF.Copy,
                                                     accum_out=SVMa[:, i * 2 + wi:i * 2 + wi + 1]), [pg], [Y[i], SVMa])
                    k.op("act", lambda: A.activation(VN[i][:, wi * 256:(wi + 1) * 256], pg[:, 0:256], AF.Square,
                                                     accum_out=SVMb[:, i * 2 + wi:i * 2 + wi + 1]), [pg], [VN[i], SVMb])
            return lambda: stage_M_rest(l, sup, ws)

        def stage_M_rest(l, sup, ws):
            n = len(sup)
            k.op("dve", lambda: V.reduce_sum(SVM[:, 16:16 + n], SVMa[:, 0:2 * n].rearrange("p (i w) -> p i w", w=2), AX.X), [SVMa], [SVM])
            k.op("dve", lambda: V.reduce_sum(SVM[:, 20:20 + n], SVMb[:, 0:2 * n].rearrange("p (i w) -> p i w", w=2), AX.X), [SVMb], [SVM])
            k.op("dve", lambda: V.tensor_scalar_mul(SVM[:, 16:16 + n], SVM[:, 16:16 + n], 1.0 / 512), [SVM], [SVM])
            k.op("dve", lambda: V.tensor_tensor(SVM[:, 24:24 + n], SVM[:, 16:16 + n], SVM[:, 16:16 + n], ALU.mult), [SVM], [SVM])
            k.op("dve", lambda: V.scalar_tensor_tensor(SVM[:, 20:20 + n], SVM[:, 20:20 + n], 1.0 / 512, SVM[:, 24:24 + n], ALU.mult, ALU.subtract), [SVM], [SVM])
            k.op("act", lambda: A.activation(SVM[:, 20:20 + n], SVM[:, 20:20 + n], AF.Sqrt, bias=col(c_eps), scale=1.0), [SVM, SV], [SVM])
            k.op("dve", lambda: V.reciprocal(SVM[:, 20:20 + n], SVM[:, 20:20 + n]), [SVM], [SVM])
            k.op("dve", lambda: V.scalar_tensor_tensor(SVM[:, 24:24 + n], SVM[:, 16:16 + n], -1.0, SVM[:, 20:20 + n], ALU.mult, ALU.mult), [SVM], [SVM])
            for i in range(n):
                k.op("pool", lambda: P.tensor_scalar(VN[i][:], Y[i][:, 512:1024], SVM[:, 20 + i:21 + i], SVM[:, 24 + i:25 + i], ALU.mult, ALU.add),
                     [Y[i], SVM], [VN[i]])
            for wi in range(2):
                wt = ws.get()
                for i in range(n):
                    pg = proj_tm(wt, i)
                    k.op("act", lambda: A.activation(SG_[i][:, wi * 256:(wi + 1) * 256], pg[:, 0:256], AF.Silu), [pg], [SG_[i]])
            for wi in range(2):
                wt = ws.get()
                for i in range(n):
                    pg = proj_tm(wt, i)
                    cs = slice(wi * 256, (wi + 1) * 256)
                    k.op("dve", lambda: V.tensor_tensor(SG_[i][:, cs], pg[:, 0:256], SG_[i][:, cs], ALU.mult), [pg, SG_[i]], [SG_[i]])
            for i in range(n):
                pg = nxt("pg", PG)
                for g in range(4):
                    gs = slice(g * 128, (g + 1) * 128)
                    mm(pg[:, gs], WST[:, g, :], VN[i][:, gs], True, True, [WST, VN[i]], [pg], g == 3)
                for g in range(4):
                    gs = slice(g * 128, (g + 1) * 128)
                    k.op("dve", lambda: V.scalar_tensor_tensor(OBQ[i % 2][g][:], pg[:, gs], col(c_bs + g), SG_[i][:, gs], ALU.add, ALU.mult),
                         [pg, SV, SG_[i]], [OBQ[i % 2][g]])
                transpose_into(OBQ[i % 2], 4, OT[1], i)

        def stage_G(l, sup):
            n = len(sup)
            ws = WStream(G_specs(l))
            for br in range(3):
                for q in range(4):
                    wl = ws.get()
                    wb = ws.get()
                    qs = slice(q * 256, (q + 1) * 256)
                    for i in range(n):
                        pa = proj_tm(wl, i)
                        sg = nxt("sgm", SGM)
                        k.op("act", lambda: A.activation(sg[:], pa[:, 0:256], AF.Sigmoid), [pa], [sg])
                        pb = proj_tm(wb, i, nk=4, src=OT[br])
                        if br == 0:
                            k.op("dve", lambda: V.tensor_tensor(Y[i][:, qs], pb[:, 0:256], sg[:], ALU.mult), [pb, sg], [Y[i]])
                        else:
                            ft = F1 if i % 2 == 0 else F2
                            k.op("dve", lambda: V.tensor_tensor(ft[:, 0:256], pb[:, 0:256], sg[:], ALU.mult), [pb, sg], [ft])
                            k.op("pool", lambda: P.tensor_tensor(Y[i][:, qs], Y[i][:, qs], ft[:, 0:256], ALU.add), [ft, Y[i]], [Y[i]])
            for i in range(n):
                k.op("act", lambda: A.copy(HB[:], Y[i][:]), [Y[i]], [HB])
                tp = nxt("tp", TP)
                for kc in range(8):
                    tr(tp[:, kc * 128:(kc + 1) * 128], HB[:, kc * 128:(kc + 1) * 128], [HB], [tp], kc == 7)
                k.op("act", lambda: A.copy(yT[:, :, i * 128:(i + 1) * 128], tp[:].rearrange("p (k t) -> p k t", k=8)), [tp], [yT])

        def stage_O(l, sup, is_ctx, xdst, last):
            n = len(sup)
            xsrc_ = x_in if l == 0 else xs_d
            FX = [F1, F2]

            def xres_load(i):
                j = sup[i] - NTC
                rd = [XSD[l][j]] if l > 0 else []
                k.dma("pool", FX[i % 2][:], xsrc_[j * 128:(j + 1) * 128, :], FX[i % 2], reads=rd, writes=[FX[i % 2]])

            if not is_ctx:
                for i in range(min(2, n)):
                    xres_load(i)
            ws = WStream(O_specs(l))
            k.op("dve", lambda: V.memset(SVN[:, 16:32], 0.0), [], [SVN])
            for q in range(4):
                wo = ws.get()
                qs = slice(q * 256, (q + 1) * 256)
                for i in range(n):
                    pg = proj_tm(wo, i, src=yT)
                    k.op("act", lambda: A.activation(Y[i][:, qs], pg[:, 0:256], AF.Copy), [pg], [Y[i]])
                    k.op("act", lambda: A.activation(SG_[i][:, 0:256], pg[:, 0:256], AF.Square, accum_out=SVN[:, 16 + i * 4 + q:17 + i * 4 + q]),
                         [pg], [SG_[i], SVN])
            k.op("dve", lambda: V.reduce_sum(SVN[:, 0:n], SVN[:, 16:16 + 4 * n].rearrange("p (i q) -> p i q", q=4), AX.X), [SVN], [SVN])
            rstd_from(SVN[:, 0:n], n, 1.0 / DM, SVN[:, 8:8 + n], SVN)
            for i, g in enumerate(sup):
                k.op("dve", lambda: V.scalar_tensor_tensor(Y[i][:], Y[i][:], SVN[:, 8 + i:9 + i], GT[:], ALU.mult, ALU.mult), [Y[i], SVN, GT], [Y[i]])
                if is_ctx:
                    k.op("pool", lambda: P.tensor_tensor(Cx[g][:], Cx[g][:], Y[i][:], ALU.add), [Cx[g], Y[i]], [Cx[g]])
                else:
                    fx = FX[i % 2]
                    k.op("pool", lambda: P.tensor_tensor(Y[i][:], Y[i][:], fx[:], ALU.add), [Y[i], fx], [Y[i]])
                    if i + 2 < n:
                        xres_load(i + 2)
                    j = g - NTC
                    if last:
                        k.dma("pool", xdst[j * 128:(j + 1) * 128, :], Y[i][:], Y[i], reads=[Y[i]], is_out=True)
                    else:
                        k.dma("pool", xdst[j * 128:(j + 1) * 128, :], Y[i][:], Y[i], reads=[Y[i]], writes=[XSD[l + 1][j]])

        def pass2(l, sup, is_ctx, xsrc, xdst, last):
            n = len(sup)
            dg = debug and l == 0 and (is_ctx or sup[0] == NTC)
            tag = "c_" if is_ctx else "l_"
            make_hT(l, sup, is_ctx, xsrc)
            if dg:
                dbg(tag + "hT", HT[hcur["i"]][:, :, 0:n * 128], HT[hcur["i"]], [128, 8, n * 128])
            if not is_ctx:
                load_rope(sup)
            stage_R(l, sup, is_ctx, False)
            if dg:
                dbg(tag + "qk_r", QK[:, :, 0:n * 128], QK, [128, 8, n * 128])
                dbg(tag + "oT_ret", OT[0][:, :, 0:n * 128], OT[0], [128, 4, n * 128])
                dbg(tag + "sf", SF[:], SF, [128, 512])
            mrest = {}
            stage_D(l, sup, is_ctx, before_tail=lambda: mrest.update(f=stage_M_begin(l, sup)))
            if dg:
                dbg(tag + "qk_d", QK[:, :, 0:n * 128], QK, [128, 8, n * 128])
                dbg(tag + "oT_diff", OT[2][:, :, 0:n * 128], OT[2], [128, 4, n * 128])
                dbg(tag + "dfo", Y[0][:, 0:512], Y[0], [128, 512])
            mrest["f"]()
            if dg:
                dbg(tag + "oT_mlp", OT[1][:, :, 0:n * 128], OT[1], [128, 4, n * 128])
            stage_G(l, sup)
            if dg:
                dbg(tag + "y", Y[0][:], Y[0], [128, DM])
            stage_O(l, sup, is_ctx, xdst, last)

        csup = list(range(NTC))
        lsups = [list(range(NTC + s * ST, NTC + (s + 1) * ST)) for s in range(NTL // ST)]
        for l in range(depth):
            xseq.extend([(l, sp_[0]) for sp_ in reversed(lsups)])
            xseq.extend([(l, sp_[0]) for sp_ in lsups])
        for l in range(depth):
            last_ = l == depth - 1
            GW["keys"] += setup_specs(l) + p1_specs(l)
            GW["keys"] += R_specs(l, True) if (last_ and depth == DEPTH) else p2_specs(l)
            for _ in lsups:
                GW["keys"] += p1_specs(l)
            for _ in lsups:
                GW["keys"] += p2_specs(l)
        for l in range(depth):
            last = l == depth - 1
            xsrc = x_in if l == 0 else xs_d
            xdst = out_d if last else xs_d
            if xstate["pos"] < len(xseq) and xstate["loaded"] != xstate["pos"]:
                nl_, ns0_ = xseq[xstate["pos"]]
                nsup_ = list(range(ns0_, ns0_ + ST))
                if nl_ == 0 or all(len(XSD[nl_][g_ - NTC].w) > 0 for g_ in nsup_):
                    x_load(nl_, nsup_)
                    xstate["loaded"] = xstate["pos"]
            layer_setup(l)
            load_mod(l, 1)
            pass1(l, csup, True, None)
            if last and depth == DEPTH:
                stage_R(l, csup, True, True)
            else:
                pass2(l, csup, True, None, None, False)
            if debug and l == 0:
                dbg("tc_ctx", TC[:], TC, [128, 512])
                dbg("dkt0", DKT[0][:], DKT[0], [128, 4, 128])
                dbg("va0", VA[0][:], VA[0], [128, 4, 130])
                dbg("mct", MCT[:], MCT, [128, 512])
                dbg("sv", SV[:], SV, [128, 128])
            load_mod(l, 0)
            for sup in reversed(lsups):
                pass1(l, sup, False, xsrc)
            for sup in lsups:
                pass2(l, sup, False, xsrc, xdst, last)
        assert GW["ng"] == len(GW["keys"]), (GW["ng"], len(GW["keys"]))
        k.finish()
        print("instr counts", k.nins, "sems", len(k.sem), "sbuf KiB/partition", k.sb_bytes / 1024)
    nc._dbg_names = dbg_names
    return nc


def _host_consts():
    j = np.arange(128, dtype=np.float64)[:, None]
    i = np.arange(128, dtype=np.float64)[None, :]
    ident = np.eye(128)
    d1 = np.maximum(i - j, 0.0)
    u = (i >= j).astype(np.float64)
    d2 = np.maximum(j - i, 0.0)
    lo = (j >= i).astype(np.float64)
    p1 = np.broadcast_to(i + 1.0, (128, 128))
    p2 = np.broadcast_to(128.0 - i, (128, 128))
    pc = np.concatenate([127.0 - j, j], axis=1)
    cf = np.concatenate([ident, d1, u, d2, lo, p1, p2, pc], axis=1).astype(np.float32)
    sel = np.zeros((2, 2, 128), np.float32)
    sel[0, 0] = 1.0
    sel[1, 1] = 1.0
    n = np.arange(SEQ)
    row = (n // 64).astype(np.float64)
    colp = (n % 64).astype(np.float64)

    def ang(head_dim):
        nf = head_dim // 4
        inv = 10000.0 ** (-np.arange(nf, dtype=np.float64) / nf)
        return np.concatenate([row[:, None] * inv, colp[:, None] * inv], axis=-1)

    a_r = ang(128)
    a_d = ang(64)
    rt = np.zeros((128, 4, SEQ), np.float32)
    d = np.arange(128)
    rt[:, 0, :] = np.cos(a_r)[:, d % 64].T
    rt[:, 1, :] = (np.sin(a_r)[:, d % 64] * np.where(d < 64, -1.0, 1.0)[None, :]).T
    dd = d % 64
    rt[:, 2, :] = np.cos(a_d)[:, dd % 32].T
    rt[:, 3, :] = (np.sin(a_d)[:, dd % 32] * np.where(dd < 32, -1.0, 1.0)[None, :]).T
    pm = np.zeros((128, 256), np.float32)
    d = np.arange(128)
    pm[d, (d + 64) % 128] = 1.0
    pm[d, 128 + (d // 64) * 64 + ((d % 64) + 32) % 64] = 1.0
    return cf, sel, rt, pm


_NC_CACHE = {}


_DBG = {}


def kernel(x, c, ctx, c_ctx, w_mod, b_mod, g_pre, g_post, w_in, ret_decay_logit, mlp_w_s, mlp_b_s,
           diff_lambda_q, diff_lambda_k, w_branch_out, w_out, _cores=8, _depth=DEPTH, _debug=False):
    f = lambda a: np.ascontiguousarray(np.asarray(a, dtype=np.float32))
    x, c, ctx, c_ctx = f(x), f(c), f(ctx), f(c_ctx)
    cf, sel, rt, pm = _host_consts()
    shared = {
        "w_mod": f(w_mod), "b_mod": f(b_mod), "g_pre": f(g_pre), "g_post": f(g_post), "w_in": f(w_in),
        "rdl": f(ret_decay_logit).reshape(DEPTH, 8), "mws": f(mlp_w_s),
        "mbs": np.ascontiguousarray(f(mlp_b_s).transpose(0, 2, 1)),
        "dlq": f(diff_lambda_q).reshape(DEPTH, 128), "dlk": f(diff_lambda_k).reshape(DEPTH, 128),
        "w_bo": f(w_branch_out), "w_out": f(w_out), "cf": cf, "ropet": rt, "pm": pm,
    }
    key = (_depth, _debug)
    if key not in _NC_CACHE:
        _NC_CACHE[key] = build_program(_depth, _debug)
    nc = _NC_CACHE[key]
    in_maps = []
    for b in range(_cores):
        cc = np.stack([c[b], c_ctx], axis=0)
        cct = np.ascontiguousarray(cc.reshape(2, 8, 128).transpose(2, 1, 0))
        m = dict(shared)
        m.update({"x": x[b], "ctx": ctx[b], "cct": cct})
        in_maps.append(m)
    res = run_bass_kernel_spmd(nc, in_maps, core_ids=list(range(_cores)))
    if _debug:
        for nm in nc._dbg_names:
            _DBG[nm] = np.asarray(res.results[0]["dbg_" + nm])
    return np.stack([np.asarray(r["out"], dtype=np.float32) for r in res.results], axis=0)
```
